# Optimizing a Trainium2 kernel written in Bass

```python
import math
import jax
import jax.numpy as jnp
from jax import lax
import numpy as np


D_MODEL = 1024
BATCH = 8
SEQ = 4096
DEPTH = 4
DEC_BATCH = 4
DEC_SEQ = 4096
PAST_LEN = 128

N_EVEN = (DEPTH + 1) // 2
N_ODD = DEPTH // 2

HY_WIDTH = D_MODEL
HY_EMB = 33
HY_FILTER_HIDDEN = 64
HY_SHORT_K = 3
HY_SHORT_DECAY_PCT = 0.3
HY_LONG_DECAY_PCT = 1.5
HY_DECAY_TARGET = 1e-2

SSD_WIDTH = D_MODEL
SSD_HEAD_DIM = 64
SSD_HEADS = SSD_WIDTH // SSD_HEAD_DIM
SSD_GROUPS = 4
SSD_STATE = 128
SSD_CONV_K = 4
SSD_CHUNK = 128
SSD_XBC = SSD_WIDTH + 2 * SSD_GROUPS * SSD_STATE

LRU_WIDTH = 2 * D_MODEL
LRU_HEADS = 16
LRU_BLOCK = LRU_WIDTH // LRU_HEADS
LRU_CONV_K = 4
LRU_C = 8.0

EVEN_IN = 4 * HY_WIDTH + SSD_WIDTH + SSD_XBC + 2 * SSD_HEADS
EVEN_MIX = HY_WIDTH + SSD_WIDTH
ODD_IN = 2 * LRU_WIDTH

kernel_name = 'hybrid_hyena_ssd_rglru_adaln_encoder'


def rms_norm(x, g, eps=1e-6):
    xf = x.astype(jnp.float32)
    y = xf * lax.rsqrt(jnp.mean(xf * xf, axis=-1, keepdims=True) + eps)
    return (y * g.astype(jnp.float32)).astype(x.dtype)


def dw_conv_centred(x, w, b):
    K = w.shape[0]
    L = x.shape[1]
    left = K // 2
    xp = jnp.pad(x, ((0, 0), (left, K - 1 - left), (0, 0)))
    return b + sum(xp[:, k:k + L] * w[k] for k in range(K))


def hyena_filter(L, w1, b1, w2, b2, w3, freq):
    f32 = jnp.float32
    pos = jnp.arange(L, dtype=f32)[:, None]
    t = pos / max(L - 1, 1)
    bands = (HY_EMB - 1) // 2
    f = jnp.linspace(1e-4, bands - 1, bands, dtype=f32)[None, :]
    ang = f * pos * (2.0 * math.pi / L)
    z = jnp.concatenate([t, jnp.cos(ang), -jnp.sin(ang)], axis=-1)
    fr = freq.astype(f32)
    h = jnp.sin(fr * (z @ w1.astype(f32) + b1.astype(f32)))
    h = jnp.sin(fr * (h @ w2.astype(f32) + b2.astype(f32)))
    h = h @ w3.astype(f32)
    deltas = jnp.abs(jnp.linspace(math.log(HY_DECAY_TARGET) / HY_LONG_DECAY_PCT,
                                  math.log(HY_DECAY_TARGET) / HY_SHORT_DECAY_PCT,
                                  HY_WIDTH, dtype=f32))
    window = jnp.exp(-t * deltas)
    h_fwd = h[:, :HY_WIDTH] * window
    h_bwd = h[:, HY_WIDTH:] * window
    filt = jnp.concatenate([h_fwd, jnp.zeros((1, HY_WIDTH), f32), h_bwd[:0:-1]], axis=0)
    return filt / jnp.sum(jnp.abs(filt), axis=0, keepdims=True)


def hyena_branch(u, conv_w, conv_b, fw1, fb1, fw2, fb2, fw3, freq, bias):
    L = u.shape[1]
    uc = dw_conv_centred(u, conv_w, conv_b)
    x0, x1, v = jnp.split(uc, 3, axis=-1)
    filt = hyena_filter(L, fw1, fb1, fw2, fb2, fw3, freq)
    vf = (v * x1).astype(jnp.float32)
    n = 2 * L
    y = jnp.fft.irfft(jnp.fft.rfft(vf, n=n, axis=1) * jnp.fft.rfft(filt, n=n, axis=0)[None],
                      n=n, axis=1)[:, :L]
    y = y + vf * bias.astype(jnp.float32)
    return x0 * y.astype(u.dtype)


def ssd_scan(x, dt, a, bm, cm):
    b, L = x.shape[0], x.shape[1]
    c, q = L // SSD_CHUNK, SSD_CHUNK
    G, R, P, N = SSD_GROUPS, SSD_HEADS // SSD_GROUPS, SSD_HEAD_DIM, SSD_STATE
    xs = (x * dt[..., None]).reshape(b, c, q, G, R, P)
    la = (dt * a).reshape(b, c, q, G, R)
    bc = bm.reshape(b, c, q, G, N)
    cc = cm.reshape(b, c, q, G, N)
    a_cum = jnp.cumsum(la, axis=2)
    diff = a_cum[:, :, :, None] - a_cum[:, :, None, :]
    mask = jnp.tril(jnp.ones((q, q), dtype=bool))[:, :, None, None]
    seg = jnp.exp(jnp.where(mask, diff, -jnp.inf))
    cb = jnp.einsum('bclgn,bcsgn->bclsg', cc, bc)
    y_diag = jnp.einsum('bclsgr,bcsgrp->bclgrp', cb[..., None] * seg, xs)
    decay_states = jnp.exp(a_cum[:, :, -1:] - a_cum)
    states = jnp.einsum('bclgn,bclgrp->bcgrpn', bc, decay_states[..., None] * xs)
    chunk_decay = jnp.exp(a_cum[:, :, -1])

    def step(h, inp):
        s, d = inp
        return h * d[..., None, None] + s, h

    h0 = jnp.zeros((b, G, R, P, N), x.dtype)
    _, prev = lax.scan(step, h0, (jnp.moveaxis(states, 1, 0), jnp.moveaxis(chunk_decay, 1, 0)))
    prev = jnp.moveaxis(prev, 0, 1)
    y_off = jnp.einsum('bclgn,bcgrpn->bclgrp', cc, prev) * jnp.exp(a_cum)[..., None]
    return (y_diag + y_off).reshape(b, L, SSD_HEADS, P)


def ssd_branch(z, xbc, dt_raw, conv_w, conv_b, dt_bias, a_log, d_skip, norm_g):
    f32 = jnp.float32
    b, L, _ = z.shape
    xbc = jax.nn.silu(dw_conv_centred(xbc, conv_w, conv_b)).astype(f32)
    xs, bm, cm = jnp.split(xbc, [SSD_WIDTH, SSD_WIDTH + SSD_GROUPS * SSD_STATE], axis=-1)
    xs = xs.reshape(b, L, SSD_HEADS, SSD_HEAD_DIM)
    bm = bm.reshape(b, L, SSD_GROUPS, SSD_STATE)
    cm = cm.reshape(b, L, SSD_GROUPS, SSD_STATE)
    dt = jax.nn.softplus(dt_raw.astype(f32).reshape(b, L, 2, SSD_HEADS) + dt_bias.astype(f32))
    a = -jnp.exp(a_log.astype(f32))
    fl = lambda t: jnp.flip(t, axis=1)
    y_fwd = ssd_scan(xs, dt[:, :, 0], a[0], bm, cm)
    y_bwd = fl(ssd_scan(fl(xs), fl(dt[:, :, 1]), a[1], fl(bm), fl(cm)))
    y = y_fwd + y_bwd + xs * d_skip.astype(f32)[:, None]
    y = y.reshape(b, L, SSD_WIDTH) * jax.nn.silu(z.astype(f32))
    yg = y.reshape(b, L, SSD_GROUPS, SSD_WIDTH // SSD_GROUPS)
    yg = yg * lax.rsqrt(jnp.mean(yg * yg, axis=-1, keepdims=True) + 1e-5)
    return (yg.reshape(b, L, SSD_WIDTH) * norm_g.astype(f32)).astype(z.dtype)


def rg_lru(x, w_a, b_a, w_x, b_x, lam):
    f32 = jnp.float32
    b, L, W = x.shape
    xh = x.reshape(b, L, LRU_HEADS, LRU_BLOCK)
    r = jax.nn.sigmoid(jnp.einsum('blhi,hij->blhj', xh, w_a.astype(f32)).reshape(b, L, W) + b_a.astype(f32))
    i = jax.nn.sigmoid(jnp.einsum('blhi,hij->blhj', xh, w_x.astype(f32)).reshape(b, L, W) + b_x.astype(f32))
    log_a = LRU_C * r * jax.nn.log_sigmoid(lam.astype(f32))
    a = jnp.exp(log_a)
    u = x * i * jnp.sqrt(-jnp.expm1(2.0 * log_a))

    def step(h, inp):
        a_t, u_t = inp
        h = a_t * h + u_t
        return h, h

    _, hs = lax.scan(step, jnp.zeros((b, W), f32), (jnp.swapaxes(a, 0, 1), jnp.swapaxes(u, 0, 1)))
    return jnp.swapaxes(hs, 0, 1)


def even_mixer(h, p, j):
    u = h @ p['ev_w_in'][j]
    o1 = 3 * HY_WIDTH
    o2 = 4 * HY_WIDTH
    o3 = o2 + SSD_WIDTH
    o4 = o3 + SSD_XBC
    hy_u, hy_gate, z, xbc, dt_raw = jnp.split(u, [o1, o2, o3, o4], axis=-1)
    y_a = hyena_branch(hy_u, p['hy_conv_w'][j], p['hy_conv_b'][j], p['hy_fw1'][j], p['hy_fb1'][j],
                       p['hy_fw2'][j], p['hy_fb2'][j], p['hy_fw3'][j], p['hy_freq'][j],
                       p['hy_bias'][j]) * jax.nn.silu(hy_gate)
    y_b = ssd_branch(z, xbc, dt_raw, p['ssd_conv_w'][j], p['ssd_conv_b'][j], p['ssd_dt_bias'][j],
                     p['ssd_A_log'][j], p['ssd_D'][j], p['ssd_norm_g'][j])
    return jnp.concatenate([y_a, y_b], axis=-1) @ p['ev_w_out'][j]


def odd_mixer(h, p, j):
    u = h @ p['od_w_in'][j]
    xb, gate = jnp.split(u, 2, axis=-1)
    xb = dw_conv_centred(xb, p['lru_conv_w'][j], p['lru_conv_b'][j]).astype(jnp.float32)
    y_fwd = rg_lru(xb, p['lru_w_a'][j, 0], p['lru_b_a'][j, 0], p['lru_w_x'][j, 0],
                   p['lru_b_x'][j, 0], p['lru_lam'][j, 0])
    y_bwd = jnp.flip(rg_lru(jnp.flip(xb, axis=1), p['lru_w_a'][j, 1], p['lru_b_a'][j, 1],
                            p['lru_w_x'][j, 1], p['lru_b_x'][j, 1], p['lru_lam'][j, 1]), axis=1)
    y = (y_fwd + y_bwd).astype(h.dtype) * jax.nn.silu(gate)
    return y @ p['od_w_out'][j]


def trunk(x, c, p):
    cs = jax.nn.silu(c)
    for i in range(DEPTH):
        shift, scale, gate = jnp.split(cs @ p['mod_w'][i] + p['mod_b'][i], 3, axis=-1)
        hn = rms_norm(x, p['norm_g'][i]) * (1.0 + scale[:, None]) + shift[:, None]
        out = even_mixer(hn, p, i // 2) if i % 2 == 0 else odd_mixer(hn, p, i // 2)
        x = x + gate[:, None] * out
    return rms_norm(x, p['final_g'])


def setup_inputs(seed: int = 0) -> dict:
    key = jax.random.key(seed)
    ks = iter(list(jax.random.split(key, 48)))
    f32 = jnp.float32

    def nrm(shape, scale):
        return scale * jax.random.normal(next(ks), shape, f32)

    def unif(shape, lo, hi):
        return jax.random.uniform(next(ks), shape, f32, minval=lo, maxval=hi)

    D = D_MODEL
    x_prompt = nrm((BATCH, SEQ, D), 1.0)
    x_sample = nrm((DEC_BATCH, DEC_SEQ, D), 1.0)
    c_prompt = nrm((BATCH, D), 1.0)
    c_sample = nrm((DEC_BATCH, D), 1.0)
    mod_w = nrm((DEPTH, D, 3 * D), 0.5 * D ** -0.5)
    mod_b = nrm((DEPTH, 3 * D), 0.02)
    norm_g = 1.0 + nrm((DEPTH, D), 0.02)
    final_g = 1.0 + nrm((D,), 0.02)
    ev_w_in = nrm((N_EVEN, D, EVEN_IN), D ** -0.5)
    ev_w_out = nrm((N_EVEN, EVEN_MIX, D), EVEN_MIX ** -0.5)
    hy_conv_w = nrm((N_EVEN, HY_SHORT_K, 3 * HY_WIDTH), HY_SHORT_K ** -0.5)
    hy_conv_b = nrm((N_EVEN, 3 * HY_WIDTH), 0.02)
    hy_fw1 = nrm((N_EVEN, HY_EMB, HY_FILTER_HIDDEN), HY_EMB ** -0.5)
    hy_fb1 = nrm((N_EVEN, HY_FILTER_HIDDEN), 0.02)
    hy_fw2 = nrm((N_EVEN, HY_FILTER_HIDDEN, HY_FILTER_HIDDEN), HY_FILTER_HIDDEN ** -0.5)
    hy_fb2 = nrm((N_EVEN, HY_FILTER_HIDDEN), 0.02)
    hy_fw3 = nrm((N_EVEN, HY_FILTER_HIDDEN, 2 * HY_WIDTH), HY_FILTER_HIDDEN ** -0.5)
    hy_freq = 1.0 + nrm((N_EVEN, HY_FILTER_HIDDEN), 0.1)
    hy_bias = nrm((N_EVEN, HY_WIDTH), 0.1)
    ssd_conv_w = nrm((N_EVEN, SSD_CONV_K, SSD_XBC), SSD_CONV_K ** -0.5)
    ssd_conv_b = nrm((N_EVEN, SSD_XBC), 0.02)
    dt0 = jnp.exp(unif((N_EVEN, 2, SSD_HEADS), math.log(1e-3), math.log(1e-1)))
    ssd_dt_bias = dt0 + jnp.log(-jnp.expm1(-dt0))
    ssd_A_log = jnp.log(unif((N_EVEN, 2, SSD_HEADS), 1.0, 16.0))
    ssd_D = 1.0 + nrm((N_EVEN, SSD_HEADS), 0.1)
    ssd_norm_g = 1.0 + nrm((N_EVEN, SSD_WIDTH), 0.02)
    od_w_in = nrm((N_ODD, D, ODD_IN), D ** -0.5)
    od_w_out = nrm((N_ODD, LRU_WIDTH, D), LRU_WIDTH ** -0.5)
    lru_conv_w = nrm((N_ODD, LRU_CONV_K, LRU_WIDTH), LRU_CONV_K ** -0.5)
    lru_conv_b = nrm((N_ODD, LRU_WIDTH), 0.02)
    lru_w_a = nrm((N_ODD, 2, LRU_HEADS, LRU_BLOCK, LRU_BLOCK), LRU_BLOCK ** -0.5)
    lru_b_a = nrm((N_ODD, 2, LRU_WIDTH), 0.02)
    lru_w_x = nrm((N_ODD, 2, LRU_HEADS, LRU_BLOCK, LRU_BLOCK), LRU_BLOCK ** -0.5)
    lru_b_x = nrm((N_ODD, 2, LRU_WIDTH), 0.02)
    s = unif((N_ODD, 2, LRU_WIDTH), 0.9, 0.999) ** (1.0 / LRU_C)
    lru_lam = jnp.log(s) - jnp.log1p(-s)
    return {'x_prompt': x_prompt, 'x_sample': x_sample, 'c_prompt': c_prompt, 'c_sample': c_sample,
            'mod_w': mod_w, 'mod_b': mod_b, 'norm_g': norm_g, 'final_g': final_g,
            'ev_w_in': ev_w_in, 'ev_w_out': ev_w_out,
            'hy_conv_w': hy_conv_w, 'hy_conv_b': hy_conv_b, 'hy_fw1': hy_fw1, 'hy_fb1': hy_fb1,
            'hy_fw2': hy_fw2, 'hy_fb2': hy_fb2, 'hy_fw3': hy_fw3, 'hy_freq': hy_freq, 'hy_bias': hy_bias,
            'ssd_conv_w': ssd_conv_w, 'ssd_conv_b': ssd_conv_b, 'ssd_dt_bias': ssd_dt_bias,
            'ssd_A_log': ssd_A_log, 'ssd_D': ssd_D, 'ssd_norm_g': ssd_norm_g,
            'od_w_in': od_w_in, 'od_w_out': od_w_out, 'lru_conv_w': lru_conv_w, 'lru_conv_b': lru_conv_b,
            'lru_w_a': lru_w_a, 'lru_b_a': lru_b_a, 'lru_w_x': lru_w_x, 'lru_b_x': lru_b_x,
            'lru_lam': lru_lam}


def reference(x_prompt, x_sample, c_prompt, c_sample, mod_w, mod_b, norm_g, final_g,
              ev_w_in, ev_w_out, hy_conv_w, hy_conv_b, hy_fw1, hy_fb1, hy_fw2, hy_fb2, hy_fw3,
              hy_freq, hy_bias, ssd_conv_w, ssd_conv_b, ssd_dt_bias, ssd_A_log, ssd_D, ssd_norm_g,
              od_w_in, od_w_out, lru_conv_w, lru_conv_b, lru_w_a, lru_b_a, lru_w_x, lru_b_x, lru_lam):
    p = {'mod_w': mod_w, 'mod_b': mod_b, 'norm_g': norm_g, 'final_g': final_g,
         'ev_w_in': ev_w_in, 'ev_w_out': ev_w_out,
         'hy_conv_w': hy_conv_w, 'hy_conv_b': hy_conv_b, 'hy_fw1': hy_fw1, 'hy_fb1': hy_fb1,
         'hy_fw2': hy_fw2, 'hy_fb2': hy_fb2, 'hy_fw3': hy_fw3, 'hy_freq': hy_freq, 'hy_bias': hy_bias,
         'ssd_conv_w': ssd_conv_w, 'ssd_conv_b': ssd_conv_b, 'ssd_dt_bias': ssd_dt_bias,
         'ssd_A_log': ssd_A_log, 'ssd_D': ssd_D, 'ssd_norm_g': ssd_norm_g,
         'od_w_in': od_w_in, 'od_w_out': od_w_out, 'lru_conv_w': lru_conv_w, 'lru_conv_b': lru_conv_b,
         'lru_w_a': lru_w_a, 'lru_b_a': lru_b_a, 'lru_w_x': lru_w_x, 'lru_b_x': lru_b_x,
         'lru_lam': lru_lam}
    y_prompt = trunk(x_prompt, c_prompt, p)
    y_sample = trunk(x_sample, c_sample, p)
    return (y_prompt, y_sample)
```

```python
import math
from contextlib import ExitStack
import numpy as np
import ml_dtypes
import concourse.bass as bass
import concourse.mybir as mybir
from concourse.bass_utils import run_bass_kernel_spmd

F32 = mybir.dt.float32
BF16 = mybir.dt.bfloat16
AF = mybir.ActivationFunctionType
ALU = mybir.AluOpType

D = 1024
L = 4096
S = 2
NCORES = 8
NT = L // 512
EVEN_IN = 7200


class Buf:
    __slots__ = ("w", "r")

    def __init__(self):
        self.w = None
        self.r = []


class Eng:
    def __init__(self, name, h, sem, inc):
        self.name = name
        self.h = h
        self.sem = sem
        self.inc = inc
        self.cnt = 0
        self.waited = {}

    def wait(self, dep):
        e, v = dep
        if e is self and self.name == "pe":
            return
        if self.waited.get(e.name, 0) >= v:
            return
        self.h.wait_ge(e.sem, v)
        self.waited[e.name] = v


class FW:
    NSLOT = 24

    def __init__(self, nc, es):
        self.nc = nc

        def mk(name, h, inc):
            return Eng(name, h, es.enter_context(nc.semaphore("s_" + name)), inc)

        self.pe = mk("pe", nc.tensor, 1)
        self.act = mk("act", nc.scalar, 1)
        self.dve = mk("dve", nc.vector, 1)
        self.pool = mk("pool", nc.gpsimd, 1)
        self.sp = Eng("sp", nc.sync, None, 0)
        self.slots = [mk("dq%d" % i, None, 16) for i in range(self.NSLOT)]
        self.slots_sw = [mk("dw%d" % i, None, 16) for i in range(8)]
        self.slot_i = 0
        self.slot_sw_i = 0
        self.engs = [self.pe, self.act, self.dve, self.pool]
        self.psum = []
        self.ps_i = 0
        self.pending = []
        self.pend_r = set()

    def _deps(self, eng, reads, writes):
        for b in reads:
            if b.w is not None:
                eng.wait(b.w)
        for b in writes:
            if b.w is not None:
                eng.wait(b.w)
            for d in b.r:
                eng.wait(d)

    def _mark(self, me, reads, writes):
        for b in reads:
            b.r = [x for x in b.r if x[0] is not me[0]] + [me]
        for b in writes:
            b.w = me
            b.r = []

    def _autoflush(self, writes):
        if self.pending and any(id(b) in self.pend_r for b in writes):
            self.flush()

    def flush(self):
        p, self.pending = self.pending, []
        self.pend_r = set()
        with self.nc.allow_non_contiguous_dma(reason="deferred store"):
            for (issuer, out, in_, reads, writes) in p:
                self.dma(issuer, out, in_, reads, writes)

    def op(self, eng, fn, reads=(), writes=()):
        self._autoflush(writes)
        self._deps(eng, reads, writes)
        ins = fn(eng.h)
        eng.cnt += 1
        ins.then_inc(eng.sem, 1)
        self._mark((eng, eng.cnt), reads, writes)

    def dma(self, issuer, out, in_, reads=(), writes=(), defer=False):
        if defer:
            self.pending.append((issuer, out, in_, list(reads), list(writes)))
            for b in reads:
                self.pend_r.add(id(b))
            return
        self._autoflush(writes)
        if issuer is self.pool:
            slot = self.slots_sw[self.slot_sw_i % len(self.slots_sw)]
            self.slot_sw_i += 1
        else:
            slot = self.slots[self.slot_i % self.NSLOT]
            self.slot_i += 1
        if slot.cnt:
            issuer.wait((slot, slot.cnt))
        self._deps(issuer, reads, writes)
        ins = issuer.h.dma_start(out=out, in_=in_)
        slot.cnt += 16
        ins.then_inc(slot.sem, 16)
        self._mark((slot, slot.cnt), reads, writes)

    def barrier(self):
        self.flush()
        toks = [(e, e.cnt) for e in self.engs if e.cnt] + [(s, s.cnt) for s in self.slots + self.slots_sw if s.cnt]
        for e in self.engs + [self.sp]:
            for t in toks:
                e.wait(t)

    def bank(self):
        b = self.psum[self.ps_i % len(self.psum)]
        self.ps_i += 1
        return b


class T:
    def __init__(self, t, nb=1):
        self.t = t
        self.b = [Buf() for _ in range(nb)]

    def __getitem__(self, k):
        return self.t[k]


DEBUG = {}


def build_program(layers=(0, 1, 2, 3)):
    nc = bass.Bass("TRN2", target_bir_lowering=False)

    def din(name, shape, dt=F32):
        return nc.dram_tensor(name, list(shape), dt, kind="ExternalInput").ap()

    def dscr(name, shape, dt=F32):
        return nc.dram_tensor(name, list(shape), dt, kind="Internal").ap()

    xin = din("xin", [S, L, D])
    cin = din("cin", [S, D])
    mod_w = din("mod_w", [4, D, 3 * D])
    mod_b = din("mod_b", [4, 3 * D])
    norm_g = din("norm_g", [4, D])
    final_g = din("final_g", [D])
    od_w_in = din("od_w_in", [2, D, 4096])
    od_w_out = din("od_w_out", [2, 2048, D])
    lru_conv_w = din("lru_conv_w", [2, 4, 2048])
    lru_conv_b = din("lru_conv_b", [2, 2048])
    lru_w_a = din("lru_w_a", [2, 2, 16, 128, 128])
    lru_b_a = din("lru_b_a", [2, 2, 2048])
    lru_w_x = din("lru_w_x", [2, 2, 16, 128, 128])
    lru_b_x = din("lru_b_x", [2, 2, 2048])
    lru_lam = din("lru_lam", [2, 2, 2048])
    ev_w_out = din("ev_w_out", [2, 2048, D])
    identF_d = din("identF_d", [128, 128])
    ev_w_in = din("ev_w_in", [2, D, EVEN_IN])
    hy_conv_w = din("hy_conv_w", [2, 3, 3072])
    hy_conv_b = din("hy_conv_b", [2, 3072])
    hy_fw1 = din("hy_fw1", [2, 33, 64])
    hy_fb1 = din("hy_fb1", [2, 64])
    hy_fw2 = din("hy_fw2", [2, 64, 64])
    hy_fb2 = din("hy_fb2", [2, 64])
    hy_fw3 = din("hy_fw3", [2, 64, 2048])
    hy_freq = din("hy_freq", [2, 64])
    hy_bias = din("hy_bias", [2, 1024])
    ssd_conv_w = din("ssd_conv_w", [2, 4, 2048])
    ssd_conv_b = din("ssd_conv_b", [2, 2048])
    ssd_dt_bias = din("ssd_dt_bias", [2, 32])
    ssd_A_log = din("ssd_A_log", [2, 32])
    ssd_D = din("ssd_D", [2, 16])
    ssd_norm_g = din("ssd_norm_g", [2, 1024])
    tabF = din("tabF", [32, 128, 2 * 32 * 128], BF16)
    tabI = din("tabI", [8, 32, 128, 2 * 512], BF16)
    zfeat = din("zfeat", [33, L])
    trow = din("trow", [L])
    deltas = din("deltas", [1024])
    mrow_d = din("mrow", [2, L])
    mask_d = din("maskd", [2, 128, 128])
    xs_tm = dscr("xs_tm", [L, 1024], BF16)
    B_tm = dscr("B_tm", [L, 512], BF16)
    B_fm = dscr("B_fm", [512, L], BF16)
    C_fm = dscr("C_fm", [512, L], BF16)
    z_tm = dscr("z_tm", [L, 1024], BF16)
    Wd = dscr("Wd", [64, L])
    ecd_d = dscr("ecd_d", [64 * 32])
    Yf = dscr("Yf", [L, 1024])
    vf_tm = dscr("vf_tm", [L, 1024], BF16)
    xg_fm = dscr("xg_fm", [1024, L], BF16)
    filt_tm = dscr("filt_tm", [2, L, 1024], BF16)
    Gsp = dscr("Gsp", [2, 32, 128, 2, 1024])
    Ysp = dscr("Ysp", [32, 128, 2, 1024], BF16)
    yout = nc.dram_tensor("yout", [S, L, D], F32, kind="ExternalOutput").ap()

    xres = dscr("xres", [S, D, L])
    if DEBUG.get("dump"):
        mixfm = nc.dram_tensor("mixfm", [2048, L], BF16, kind="ExternalOutput").ap()
        hn_dump = nc.dram_tensor("hn_dump", [128, 8, L], BF16, kind="ExternalOutput").ap()
    else:
        mixfm = dscr("mixfm", [2048, L], BF16)

    es = ExitStack()
    with es:
        fw = FW(nc, es)
        pe, act, dve, pool, sp = fw.pe, fw.act, fw.dve, fw.pool, fw.sp

        uid = [0]

        def sb(st, name, shape, dt=F32, nb=1):
            uid[0] += 1
            return T(st.enter_context(nc.sbuf_tensor("%s_%d" % (name, uid[0]), list(shape), dt)), nb)

        def ldvec(dst, src1d, bufs):
            with nc.allow_non_contiguous_dma(reason="tiny param loads"):
                fw.dma(sp, dst, src1d.rearrange("(h p) -> p h", p=128), writes=bufs)

        for i in range(8):
            fw.psum.append(T(es.enter_context(nc.psum_tensor("ps%d" % i, [128, 512], F32))))

        hn = sb(es, "hn", [128, 8, L], BF16, nb=NT)
        identF = sb(es, "identF", [128, 128])
        identB = sb(es, "identB", [128, 128], BF16)
        onesB = sb(es, "onesB", [128, 128], BF16)
        MOD = sb(es, "MOD", [128, 5, S, 3, 8])
        fw.dma(sp, identF[:], identF_d[:, :], writes=identF.b)
        fw.op(dve, lambda h: h.tensor_copy(out=identB[:], in_=identF[:]), reads=identF.b, writes=identB.b)
        fw.op(dve, lambda h: h.memset(onesB[:], 1.0), writes=onesB.b)

        with ExitStack() as st:
            cT = sb(st, "cT", [128, 8, S])
            csT = sb(st, "csT", [128, 8, S], BF16)
            mb = sb(st, "mb", [128, 4, 24])
            ng = sb(st, "ng", [128, 5, 8])
            mw = [sb(st, "mw%d" % i, [128, 8, 1536], BF16) for i in range(2)]
            mraw = sb(st, "mraw", [128, 24, S])
            for s_ in range(S):
                ldvec(cT[:, :, s_], cin[s_], cT.b)
            for i in range(4):
                ldvec(mb[:, i, :], mod_b[i], mb.b)
                ldvec(ng[:, i, :], norm_g[i], ng.b)
            ldvec(ng[:, 4, :], final_g, ng.b)
            fw.op(act, lambda h: h.activation(out=csT[:], in_=cT[:], func=AF.Silu), reads=cT.b, writes=csT.b)
            for i in range(4):
                ps = fw.bank()
                for half in range(2):
                    w = mw[half]
                    fw.dma(pool, w[:], mod_w[i].rearrange("(k p) c -> p k c", p=128)[:, :, half * 1536:(half + 1) * 1536],
                           writes=w.b)
                    for jj in range(12):
                        j = half * 12 + jj
                        for k in range(8):
                            fw.op(pe, lambda h, j=j, jj=jj, k=k, w=w, ps=ps: h.matmul(
                                ps[:, j * S:(j + 1) * S], lhsT=w[:, k, jj * 128:(jj + 1) * 128], rhs=csT[:, k, :],
                                start=(k == 0), stop=(k == 7)), reads=w.b + csT.b, writes=ps.b)
                fw.op(dve, lambda h, ps=ps, i=i: h.tensor_tensor(
                    out=mraw[:], in0=ps[:, 0:24 * S].rearrange("p (j s) -> p j s", s=S),
                    in1=mb[:, i, :].unsqueeze(2).to_broadcast([128, 24, S]), op=ALU.add),
                    reads=ps.b + mb.b, writes=mraw.b)
                for s in range(S):
                    fw.op(dve, lambda h, i=i, s=s: h.scalar_tensor_tensor(
                        out=MOD[:, i, s, 0, :], in0=mraw[:, 8:16, s], scalar=1.0, in1=ng[:, i, :],
                        op0=ALU.add, op1=ALU.mult), reads=mraw.b + ng.b, writes=MOD.b)
                    fw.op(dve, lambda h, i=i, s=s: h.tensor_copy(out=MOD[:, i, s, 1, :], in_=mraw[:, 0:8, s]),
                          reads=mraw.b, writes=MOD.b)
                    fw.op(dve, lambda h, i=i, s=s: h.tensor_copy(out=MOD[:, i, s, 2, :], in_=mraw[:, 16:24, s]),
                          reads=mraw.b, writes=MOD.b)
            for s in range(S):
                fw.op(dve, lambda h, s=s: h.tensor_copy(out=MOD[:, 4, s, 0, :], in_=ng[:, 4, :]), reads=ng.b, writes=MOD.b)
                fw.op(dve, lambda h, s=s: h.memset(MOD[:, 4, s, 1, :], 0.0), writes=MOD.b)
            fw.barrier()

        def norm_tile(st_bufs, xt, li, s, tt, out_fn):
            sq, rstd = st_bufs
            fw.op(act, lambda h: h.activation(out=sq[:], in_=xt[:], func=AF.Square), reads=xt.b, writes=sq.b)
            ps = fw.bank()
            for k in range(8):
                fw.op(pe, lambda h, k=k: h.matmul(ps[:], lhsT=onesB[:], rhs=sq[:, k, :], start=(k == 0), stop=(k == 7)),
                      reads=onesB.b + sq.b, writes=ps.b)
            fw.op(act, lambda h: h.activation(out=rstd[:], in_=ps[:], func=AF.Sqrt, scale=1.0 / D, bias=epsT[:]),
                  reads=ps.b + epsT.b, writes=rstd.b)
            fw.op(dve, lambda h: h.reciprocal(out=rstd[:], in_=rstd[:]), reads=rstd.b, writes=rstd.b)
            for k in range(8):
                out_fn(k, rstd)

        epsT = sb(es, "epsT", [128, 1])
        fw.op(dve, lambda h: h.memset(epsT[:], 1e-6), writes=epsT.b)

        xres_v = [xres[s].rearrange("(cb p) l -> p cb l", p=128) for s in range(S)]
        mix_v = mixfm.rearrange("(kc p) l -> p kc l", p=128)

        with ExitStack() as st:
            xa = [sb(st, "xa%d" % i, [128, 4, D]) for i in range(2)]
            xf = [sb(st, "xf%d" % i, [128, 8, 512]) for i in range(2)]
            n = 0
            for s in range(S):
                for tt in range(NT):
                    a = xa[n % 2]
                    f = xf[n % 2]
                    n += 1
                    fw.dma(sp, a[:], xin[s, tt * 512:(tt + 1) * 512, :].rearrange("(a p) c -> p a c", p=128), writes=a.b)
                    fw.flush()
                    for cb in range(8):
                        ps = fw.bank()
                        for sub in range(4):
                            fw.op(pe, lambda h, sub=sub, cb=cb, a=a, ps=ps: h.transpose(
                                out=ps[:, sub * 128:(sub + 1) * 128], in_=a[:, sub, cb * 128:(cb + 1) * 128],
                                identity=identF[:]), reads=a.b + identF.b, writes=ps.b)
                        e = act if cb % 2 == 0 else dve
                        if e is act:
                            fw.op(act, lambda h, cb=cb, f=f, ps=ps: h.copy(out=f[:, cb, :], in_=ps[:]), reads=ps.b, writes=f.b)
                        else:
                            fw.op(dve, lambda h, cb=cb, f=f, ps=ps: h.tensor_copy(out=f[:, cb, :], in_=ps[:]), reads=ps.b, writes=f.b)
                    fw.dma(sp, xres_v[s][:, :, tt * 512:(tt + 1) * 512], f[:], reads=f.b, defer=True)
            fw.barrier()

        def phase_norm0(s, li):
            with ExitStack() as st:
                xt = [sb(st, "n0x%d" % i, [128, 8, 512]) for i in range(2)]
                sq = sb(st, "n0sq", [128, 8, 512], BF16)
                rstd = sb(st, "n0rs", [128, 512])
                tmp = sb(st, "n0tmp", [128, 512])
                for tt in range(NT):
                    x = xt[tt % 2]
                    fw.dma(sp, x[:], xres_v[s][:, :, tt * 512:(tt + 1) * 512], writes=x.b)

                    def out_fn(k, rstd, x=x, tt=tt):
                        fw.op(dve, lambda h: h.scalar_tensor_tensor(
                            out=tmp[:], in0=x[:, k, :], scalar=MOD[:, li, s, 0, k:k + 1], in1=rstd[:],
                            op0=ALU.mult, op1=ALU.mult), reads=x.b + rstd.b + MOD.b, writes=tmp.b)
                        fw.op(act, lambda h: h.activation(
                            out=hn[:, k, tt * 512:(tt + 1) * 512], in_=tmp[:], func=AF.Identity,
                            bias=MOD[:, li, s, 1, k:k + 1], scale=1.0), reads=tmp.b + MOD.b, writes=[hn.b[tt]])
                    norm_tile((sq, rstd), x, li, s, tt, out_fn)
                if DEBUG.get("dump"):
                    fw.dma(sp, hn_dump[:, :, :], hn[:], reads=hn.b, defer=True)
                fw.barrier()

        def phase_out(s, li, w_out_d, nli):
            final = (nli == 4)
            with ExitStack() as st:
                wo = sb(st, "wo", [128, 16, D], BF16)
                mt = [sb(st, "mt%d" % i, [128, 16, 512], BF16) for i in range(2)]
                xo = [sb(st, "xo%d" % i, [128, 8, 512]) for i in range(2)]
                sq = sb(st, "osq", [128, 8, 512], BF16)
                rstd = sb(st, "ors", [128, 512])
                tmp = sb(st, "otmp", [128, 512])
                ytm = [sb(st, "oytm%d" % i, [128, 4, D]) for i in range(1)] if final else None
                fw.dma(pool, wo[:], w_out_d.rearrange("(kc p) c -> p kc c", p=128), writes=wo.b)
                def stage1(tt):
                        m = mt[tt % 2]
                        xo_t = xo[tt % 2]
                        xn_t = xo_t
                        yfm = xo_t
                        sl = slice(tt * 512, (tt + 1) * 512)
                        fw.dma(sp, m[:], mix_v[:, :, sl], writes=m.b)
                        fw.dma(sp, xo_t[:], xres_v[s][:, :, sl], writes=xo_t.b)
                        fw.flush()
                        for ob in range(8):
                            ps = fw.bank()
                            for kc in range(16):
                                fw.op(pe, lambda h, ob=ob, kc=kc, ps=ps, m=m: h.matmul(
                                    ps[:], lhsT=wo[:, kc, ob * 128:(ob + 1) * 128], rhs=m[:, kc, :],
                                    start=(kc == 0), stop=(kc == 15)), reads=wo.b + m.b, writes=ps.b)
                            if DEBUG.get("skip_mix"):
                                continue
                            fw.op(dve, lambda h, ob=ob, ps=ps, xo_t=xo_t, xn_t=xn_t: h.scalar_tensor_tensor(
                                out=xn_t[:, ob, :], in0=ps[:], scalar=MOD[:, li, s, 2, ob:ob + 1], in1=xo_t[:, ob, :],
                                op0=ALU.mult, op1=ALU.add), reads=ps.b + xo_t.b + MOD.b, writes=xn_t.b)
                        if not final:
                            fw.dma(sp, xres_v[s][:, :, sl], xn_t[:], reads=xn_t.b, defer=True)


                def stage2(tt):
                        xo_t = xo[tt % 2]
                        xn_t = xo_t
                        yfm = xo_t
                        sl = slice(tt * 512, (tt + 1) * 512)
                        def out_fn(k, rstd, xn_t=xn_t, tt=tt):
                            fw.op(dve, lambda h: h.scalar_tensor_tensor(
                                out=tmp[:], in0=xn_t[:, k, :], scalar=MOD[:, nli, s, 0, k:k + 1], in1=rstd[:],
                                op0=ALU.mult, op1=ALU.mult), reads=xn_t.b + rstd.b + MOD.b, writes=tmp.b)
                            if final:
                                fw.op(act, lambda h: h.copy(out=yfm[:, k, :], in_=tmp[:]), reads=tmp.b, writes=yfm.b)
                            else:
                                fw.op(act, lambda h: h.activation(
                                    out=hn[:, k, tt * 512:(tt + 1) * 512], in_=tmp[:], func=AF.Identity,
                                    bias=MOD[:, nli, s, 1, k:k + 1], scale=1.0), reads=tmp.b + MOD.b, writes=[hn.b[tt]])
                        norm_tile((sq, rstd), xn_t, nli, s, tt, out_fn)
                        if final:
                            yt = ytm[0]
                            for sub in range(4):
                                for half in range(2):
                                    ps = fw.bank()
                                    for c4 in range(4):
                                        cb = half * 4 + c4
                                        fw.op(pe, lambda h, sub=sub, cb=cb, c4=c4, ps=ps: h.transpose(
                                            out=ps[:, c4 * 128:(c4 + 1) * 128], in_=yfm[:, cb, sub * 128:(sub + 1) * 128],
                                            identity=identF[:]), reads=yfm.b + identF.b, writes=ps.b)
                                    if half == 0:
                                        fw.op(act, lambda h, sub=sub, ps=ps, yt=yt: h.copy(out=yt[:, sub, 0:512], in_=ps[:]),
                                              reads=ps.b, writes=yt.b)
                                    else:
                                        fw.op(dve, lambda h, sub=sub, ps=ps, yt=yt: h.tensor_copy(out=yt[:, sub, 512:1024], in_=ps[:]),
                                              reads=ps.b, writes=yt.b)
                            fw.dma(sp, yout[s, sl, :].rearrange("(a p) c -> p a c", p=128), yt[:], reads=yt.b, defer=True)

                stage1(0)
                for tt in range(NT):
                    if tt + 1 < NT:
                        stage1(tt + 1)
                    stage2(tt)
                fw.barrier()

        def phase_odd(s, j):
            with ExitStack() as st:
                prm = sb(st, "oprm", [128, 16, 16])
                wbuf = [sb(st, "owb%d" % i, [128, 8, 2, 128], BF16) for i in range(2)]
                gw = [sb(st, "ogw%d" % i, [128, 2, 2, 128], BF16) for i in range(2)]
                xp = sb(st, "oxp", [128, L + 4])
                xc = [sb(st, "oxc%d" % i, [128, L], BF16) for i in range(2)]
                sg = [sb(st, "osg%d" % i, [128, L], BF16) for i in range(2)]
                af = sb(st, "oaf", [128, L])
                uf = sb(st, "ouf", [128, L])
                hf = sb(st, "ohf", [128, L])
                ym = [sb(st, "oym%d" % i, [128, L], BF16) for i in range(1)]
                t1 = sb(st, "ot1", [128, L])
                for k in range(4):
                    ldvec(prm[:, :, k], lru_conv_w[j, k], prm.b)
                ldvec(prm[:, :, 4], lru_conv_b[j], prm.b)
                for d in range(2):
                    ldvec(prm[:, :, 5 + d], lru_b_a[j, d], prm.b)
                    ldvec(prm[:, :, 7 + d], lru_b_x[j, d], prm.b)
                    ldvec(prm[:, :, 9 + d], lru_lam[j, d], prm.b)
                fw.op(act, lambda h: h.activation(out=prm[:, :, 11:13], in_=prm[:, :, 9:11], func=AF.Exp, scale=-1.0),
                      reads=prm.b, writes=prm.b)
                fw.op(act, lambda h: h.activation(out=prm[:, :, 11:13], in_=prm[:, :, 11:13], func=AF.Ln, bias=1.0, scale=1.0),
                      reads=prm.b, writes=prm.b)
                fw.op(dve, lambda h: h.tensor_scalar(out=prm[:, :, 9:11], in0=prm[:, :, 11:13], scalar1=-8.0, scalar2=None, op0=ALU.mult),
                      reads=prm.b, writes=prm.b)
                fw.op(dve, lambda h: h.tensor_scalar(out=prm[:, :, 11:13], in0=prm[:, :, 9:11], scalar1=2.0, scalar2=None, op0=ALU.mult),
                      reads=prm.b, writes=prm.b)
                fw.op(dve, lambda h: h.memset(xp[:, 0:2], 0.0), writes=xp.b)
                fw.op(dve, lambda h: h.memset(xp[:, L + 2:L + 4], 0.0), writes=xp.b)
                w_in_v = od_w_in[j].rearrange("(k p) c -> p k c", p=128)
                def load_w(hb_):
                    wb_ = wbuf[hb_ % 2]
                    fw.dma(pool, wb_[:, :, 0, :], w_in_v[:, :, hb_ * 128:(hb_ + 1) * 128], writes=wb_.b)
                    fw.dma(pool, wb_[:, :, 1, :], w_in_v[:, :, 2048 + hb_ * 128:2048 + (hb_ + 1) * 128], writes=wb_.b)

                def load_gw(hb_):
                    g_ = gw[hb_ % 2]
                    fw.dma(pool, g_[:, :, 0, :], lru_w_a[j, :, hb_].rearrange("d i o -> i d o"), writes=g_.b)
                    fw.dma(pool, g_[:, :, 1, :], lru_w_x[j, :, hb_].rearrange("d i o -> i d o"), writes=g_.b)

                def stage_a(hb):
                    wb = wbuf[hb % 2]
                    xc_ = xc[hb % 2]
                    sg_ = sg[hb % 2]
                    for tt in range(NT):
                        sl = slice(tt * 512, (tt + 1) * 512)
                        ps = fw.bank()
                        for k in range(8):
                            fw.op(pe, lambda h, k=k, ps=ps, sl=sl: h.matmul(ps[:], lhsT=wb[:, k, 0, :], rhs=hn[:, k, sl],
                                                                         start=(k == 0), stop=(k == 7)),
                                  reads=wb.b + [hn.b[tt]], writes=ps.b)
                        fw.op(act, lambda h, ps=ps, tt=tt: h.copy(out=xp[:, 2 + tt * 512:2 + (tt + 1) * 512], in_=ps[:]),
                              reads=ps.b, writes=xp.b)
                        ps2 = fw.bank()
                        for k in range(8):
                            fw.op(pe, lambda h, k=k, ps2=ps2, sl=sl: h.matmul(ps2[:], lhsT=wb[:, k, 1, :], rhs=hn[:, k, sl],
                                                                           start=(k == 0), stop=(k == 7)),
                                  reads=wb.b + [hn.b[tt]], writes=ps2.b)
                        fw.op(act, lambda h, ps2=ps2, sl=sl: h.activation(out=sg_[:, sl], in_=ps2[:], func=AF.Silu),
                              reads=ps2.b, writes=sg_.b)
                    if hb + 1 < 16:
                        load_w(hb + 1)
                    fw.op(act, lambda h: h.activation(out=hf[:], in_=xp[:, 0:L], func=AF.Identity,
                                                      scale=prm[:, hb, 0:1], bias=prm[:, hb, 4:5]),
                          reads=xp.b + prm.b, writes=hf.b)
                    for k in (1, 2):
                        fw.op(dve, lambda h, k=k: h.scalar_tensor_tensor(out=hf[:], in0=xp[:, k:k + L], scalar=prm[:, hb, k:k + 1],
                                                                        in1=hf[:], op0=ALU.mult, op1=ALU.add),
                              reads=xp.b + prm.b + hf.b, writes=hf.b)
                    fw.op(dve, lambda h: h.scalar_tensor_tensor(out=xc_[:], in0=xp[:, 3:3 + L], scalar=prm[:, hb, 3:4],
                                                                in1=hf[:], op0=ALU.mult, op1=ALU.add),
                          reads=xp.b + prm.b + hf.b, writes=xc_.b)

                def stage_b(hb):
                    g = gw[hb % 2]
                    xc_ = xc[hb % 2]
                    sg_ = sg[hb % 2]
                    y = ym[0]
                    for d in range(2):
                        for tt in range(NT):
                            sl = slice(tt * 512, (tt + 1) * 512)
                            psa = fw.bank()
                            fw.op(pe, lambda h, psa=psa, sl=sl: h.matmul(psa[:], lhsT=g[:, d, 0, :], rhs=xc_[:, sl], start=True, stop=True),
                                  reads=g.b + xc_.b, writes=psa.b)
                            psx = fw.bank()
                            fw.op(pe, lambda h, psx=psx, sl=sl: h.matmul(psx[:], lhsT=g[:, d, 1, :], rhs=xc_[:, sl], start=True, stop=True),
                                  reads=g.b + xc_.b, writes=psx.b)
                            fw.op(act, lambda h, psa=psa, sl=sl: h.activation(out=t1[:, sl], in_=psa[:], func=AF.Sigmoid,
                                                                             bias=prm[:, hb, 5 + d:6 + d], scale=1.0),
                                  reads=psa.b + prm.b, writes=t1.b)
                            fw.op(act, lambda h, psx=psx, sl=sl: h.activation(out=uf[:, sl], in_=psx[:], func=AF.Sigmoid,
                                                                             bias=prm[:, hb, 7 + d:8 + d], scale=1.0),
                                  reads=psx.b + prm.b, writes=uf.b)
                        if d == 1 and hb + 1 < 16:
                            load_gw(hb + 1)
                        fw.op(act, lambda h: h.activation(out=af[:], in_=t1[:], func=AF.Exp, scale=prm[:, hb, 9 + d:10 + d]),
                              reads=t1.b + prm.b, writes=af.b)
                        fw.op(act, lambda h: h.activation(out=t1[:], in_=t1[:], func=AF.Exp, scale=prm[:, hb, 11 + d:12 + d]),
                              reads=t1.b + prm.b, writes=t1.b)
                        fw.op(act, lambda h: h.activation(out=t1[:], in_=t1[:], func=AF.Sqrt, scale=-1.0, bias=oneT[:]),
                              reads=t1.b + oneT.b, writes=t1.b)
                        fw.op(dve, lambda h: h.tensor_tensor(out=uf[:], in0=uf[:], in1=xc_[:], op=ALU.mult),
                              reads=uf.b + xc_.b, writes=uf.b)
                        fw.op(dve, lambda h: h.tensor_tensor(out=uf[:], in0=uf[:], in1=t1[:], op=ALU.mult),
                              reads=uf.b + t1.b, writes=uf.b)
                        if d == 0:
                            fw.op(dve, lambda h: h.tensor_tensor_scan(out=hf[:], data0=af[:], data1=uf[:], initial=0.0,
                                                                      op0=ALU.mult, op1=ALU.add),
                                  reads=af.b + uf.b, writes=hf.b)
                        else:
                            fw.op(dve, lambda h: h.tensor_tensor_scan(out=t1[:, ::-1], data0=af[:, ::-1], data1=uf[:, ::-1],
                                                                      initial=0.0, op0=ALU.mult, op1=ALU.add),
                                  reads=af.b + uf.b, writes=t1.b)
                    fw.op(dve, lambda h: h.tensor_tensor(out=hf[:], in0=hf[:], in1=t1[:], op=ALU.add),
                          reads=hf.b + t1.b, writes=hf.b)
                    fw.op(dve, lambda h: h.tensor_tensor(out=y[:], in0=hf[:], in1=sg_[:], op=ALU.mult),
                          reads=hf.b + sg_.b, writes=y.b)
                    fw.dma(sp, mixfm[hb * 128:(hb + 1) * 128, :], y[:], reads=y.b, defer=True)

                load_w(0)
                load_gw(0)
                stage_a(0)
                for hb in range(16):
                    if hb + 1 < 16:
                        stage_a(hb + 1)
                    stage_b(hb)
                    fw.flush()
                fw.barrier()


        def conv_fm(xp, K, prm_ap, out_t, tmp):
            fw.op(act, lambda h: h.activation(out=tmp[:], in_=xp[:, 0:L], func=AF.Identity, scale=prm_ap(0), bias=prm_ap(K)),
                  reads=xp.b + PRM.b, writes=tmp.b)
            for k in range(1, K):
                o = out_t if k == K - 1 else tmp
                fw.op(dve, lambda h, k=k, o=o: h.scalar_tensor_tensor(out=o[:], in0=xp[:, k:k + L], scalar=prm_ap(k), in1=tmp[:],
                                                                     op0=ALU.mult, op1=ALU.add),
                      reads=xp.b + PRM.b + tmp.b, writes=o.b)

        WLOADED = {}
        PEND_T = []

        def proj_fm(w_v, col0, wb, evac, nxt=None):
            if WLOADED.get(id(wb)) != (id(w_v), col0):
                fw.dma(pool, wb[:], w_v[:, :, col0:col0 + 128], writes=wb.b)
            WLOADED.pop(id(wb), None)
            _proj_body(wb, evac)
            while PEND_T:
                transpose_to_tm(*PEND_T.pop(0))
            if nxt is not None:
                ncol, nwb = nxt
                fw.dma(pool, nwb[:], w_v[:, :, ncol:ncol + 128], writes=nwb.b)
                WLOADED[id(nwb)] = (id(w_v), ncol)

        def _proj_body(wb, evac):
            for tt in range(NT):
                ps = fw.bank()
                for k in range(8):
                    fw.op(pe, lambda h, k=k, ps=ps, tt=tt: h.matmul(ps[:], lhsT=wb[:, k, :], rhs=hn[:, k, tt * 512:(tt + 1) * 512],
                                                                  start=(k == 0), stop=(k == 7)),
                          reads=wb.b + [hn.b[tt]], writes=ps.b)
                evac(tt, ps)

        def transpose_to_tm(src, dst_v):
            for q in range(4):
                stg = TSTG[q % 2]
                for t8 in range(2):
                    ps = fw.bank()
                    psb = ps[:].bitcast(BF16)
                    for i in range(4):
                        t = q * 8 + t8 * 4 + i
                        fw.op(pe, lambda h, i=i, t=t, psb=psb: h.transpose(out=psb[:, i * 128:(i + 1) * 128],
                                                                         in_=src[:, t * 128:(t + 1) * 128], identity=identB[:]),
                              reads=src.b + identB.b, writes=ps.b)
                    fw.op(act, lambda h, t8=t8, psb=psb, stg=stg: h.copy(
                        out=stg[:, t8 * 4:(t8 + 1) * 4, :], in_=psb[:, 0:512].rearrange("p (i c) -> p i c", c=128)),
                        reads=ps.b, writes=stg.b)
                with nc.allow_non_contiguous_dma(reason="256B runs"):
                    fw.dma(sp, dst_v[:, q * 8:(q + 1) * 8, :], stg[:], reads=stg.b, defer=True)

        def dft_forward(src_c_v, src_s_v, c0, ncol, consume):
            vc = DFT_SRC[0]
            fw.dma(sp, vc[:, :, 0:ncol], src_c_v[:, :, c0:c0 + ncol], writes=vc.b)
            if src_s_v is not None:
                vs = DFT_SRC[1]
                fw.dma(sp, vs[:, :, 0:ncol], src_s_v[:, :, c0:c0 + ncol], writes=vs.b)
            else:
                vs = vc
            for kb in range(32):
                tb = DFT_TAB[kb % 2]
                fw.dma(sp, tb[:], tabF[kb], writes=tb.b)
                fw.flush()
                tv = tb[:].rearrange("p (cs t k) -> p cs t k", cs=2, t=32)
                for g0 in range(0, ncol, 512):
                    gw_ = min(512, ncol - g0)
                    pc = fw.bank()
                    pss = fw.bank()
                    for t in range(32):
                        fw.op(pe, lambda h, t=t, pc=pc, tv=tv, g0=g0, gw_=gw_: h.matmul(pc[:, 0:gw_], lhsT=tv[:, 0, t, :], rhs=vc[:, t, g0:g0 + gw_],
                                                                                      start=(t == 0), stop=(t == 31)),
                              reads=tb.b + vc.b, writes=pc.b)
                    for t in range(32):
                        fw.op(pe, lambda h, t=t, pss=pss, tv=tv, g0=g0, gw_=gw_: h.matmul(pss[:, 0:gw_], lhsT=tv[:, 1, t, :], rhs=vs[:, t, g0:g0 + gw_],
                                                                                        start=(t == 0), stop=(t == 31)),
                              reads=tb.b + vs.b, writes=pss.b)
                    consume(kb, g0, gw_, pc, pss)

        def sin_big(out_t, ps, n, f4, f8, b4, b8, s4, s8):
            fw.op(act, lambda h: h.activation(out=s4[0:64, 0:n], in_=ps[0:64, 0:n], func=AF.Sin, scale=f4, bias=b4),
                  reads=ps.b + PRM.b, writes=s4.b)
            fw.op(act, lambda h: h.activation(out=s8[0:64, 0:n], in_=ps[0:64, 0:n], func=AF.Sin, scale=f8, bias=b8),
                  reads=ps.b + PRM.b, writes=s8.b)
            fw.op(dve, lambda h: h.tensor_tensor(out=s8[0:64, 0:n], in0=s8[0:64, 0:n], in1=s8[0:64, 0:n], op=ALU.mult), reads=s8.b, writes=s8.b)
            fw.op(dve, lambda h: h.tensor_scalar(out=s8[0:64, 0:n], in0=s8[0:64, 0:n], scalar1=-8.0, scalar2=4.0, op0=ALU.mult, op1=ALU.add),
                  reads=s8.b, writes=s8.b)
            fw.op(dve, lambda h: h.tensor_tensor(out=s8[0:64, 0:n], in0=s8[0:64, 0:n], in1=s4[0:64, 0:n], op=ALU.mult), reads=s8.b + s4.b, writes=s8.b)
            fw.op(dve, lambda h: h.tensor_tensor(out=s4[0:64, 0:n], in0=s4[0:64, 0:n], in1=s4[0:64, 0:n], op=ALU.mult), reads=s4.b, writes=s4.b)
            fw.op(dve, lambda h: h.tensor_scalar(out=s4[0:64, 0:n], in0=s4[0:64, 0:n], scalar1=-2.0, scalar2=1.0, op0=ALU.mult, op1=ALU.add),
                  reads=s4.b, writes=s4.b)
            fw.op(dve, lambda h: h.tensor_tensor(out=out_t, in0=s8[0:64, 0:n], in1=s4[0:64, 0:n], op=ALU.mult), reads=s8.b + s4.b, writes=out_bufs[0])

        out_bufs = [None]
        PRM = sb(es, "PRM", [128, 64, 8])
        TSTG = [None, None]
        DFT_SRC = [None, None]
        DFT_TAB = [None, None]

        def phase_filter(j):
            with ExitStack() as st:
                zf = sb(st, "fz", [33, L])
                w1 = sb(st, "fw1", [33, 64])
                w2 = sb(st, "fw2", [64, 64])
                w3 = sb(st, "fw3", [64, 2048])
                h1 = sb(st, "fh1", [64, L])
                h2 = sb(st, "fh2", [64, L])
                s4 = sb(st, "fs4", [64, 512])
                s8 = sb(st, "fs8", [64, 512])
                win = sb(st, "fwin", [128, L])
                hfw = sb(st, "fhf", [128, L])
                hbw = sb(st, "fhb", [128, L])
                hsum = sb(st, "fhs", [128, L], BF16)
                hdif = sb(st, "fhd", [128, L], BF16)
                nrm = sb(st, "fnrm", [128, 4])
                TSTG[0] = sb(st, "fst0", [128, 8, 128], BF16)
                TSTG[1] = sb(st, "fst1", [128, 8, 128], BF16)
                fw.dma(sp, zf[:], zfeat[:, :], writes=zf.b)
                fw.dma(sp, w1[:], hy_fw1[j], writes=w1.b)
                fw.dma(sp, w2[:], hy_fw2[j], writes=w2.b)
                fw.dma(sp, w3[:], hy_fw3[j], writes=w3.b)
                with nc.allow_non_contiguous_dma(reason="tiny"):
                    fw.dma(sp, PRM[0:64, 0, 0:1], hy_freq[j].rearrange("(p o) -> p o", o=1), writes=PRM.b)
                    fw.dma(sp, PRM[0:64, 0, 1:2], hy_fb1[j].rearrange("(p o) -> p o", o=1), writes=PRM.b)
                    fw.dma(sp, PRM[0:64, 0, 2:3], hy_fb2[j].rearrange("(p o) -> p o", o=1), writes=PRM.b)
                    fw.dma(sp, PRM[:, 2:10, 0], deltas.rearrange("(b p) -> p b", p=128), writes=PRM.b)
                P = lambda a, b: PRM[0:64, a, b:b + 1]
                fw.op(dve, lambda h: h.tensor_scalar(out=P(1, 0), in0=P(0, 0), scalar1=0.25, scalar2=None, op0=ALU.mult), reads=PRM.b, writes=PRM.b)
                fw.op(dve, lambda h: h.tensor_scalar(out=P(1, 1), in0=P(0, 0), scalar1=0.125, scalar2=None, op0=ALU.mult), reads=PRM.b, writes=PRM.b)
                for (bi, o) in ((1, 2), (2, 4)):
                    fw.op(dve, lambda h, bi=bi, o=o: h.tensor_tensor(out=P(1, o), in0=P(0, bi), in1=P(1, 0), op=ALU.mult), reads=PRM.b, writes=PRM.b)
                    fw.op(dve, lambda h, bi=bi, o=o: h.tensor_tensor(out=P(1, o + 1), in0=P(0, bi), in1=P(1, 1), op=ALU.mult), reads=PRM.b, writes=PRM.b)
                fw.op(dve, lambda h: h.tensor_scalar(out=PRM[:, 2:10, 1], in0=PRM[:, 2:10, 0], scalar1=-1.0, scalar2=None, op0=ALU.mult),
                      reads=PRM.b, writes=PRM.b)
                for tt in range(NT):
                    sl = slice(tt * 512, (tt + 1) * 512)
                    ps = fw.bank()
                    fw.op(pe, lambda h, ps=ps, sl=sl: h.matmul(ps[0:64, :], lhsT=w1[:], rhs=zf[:, sl], start=True, stop=True),
                          reads=w1.b + zf.b, writes=ps.b)
                    out_bufs[0] = h1.b
                    sin_big(h1[:, sl], ps, 512, P(1, 0), P(1, 1), P(1, 2), P(1, 3), s4, s8)
                for tt in range(NT):
                    sl = slice(tt * 512, (tt + 1) * 512)
                    ps = fw.bank()
                    fw.op(pe, lambda h, ps=ps, sl=sl: h.matmul(ps[0:64, :], lhsT=w2[:], rhs=h1[:, sl], start=True, stop=True),
                          reads=w2.b + h1.b, writes=ps.b)
                    out_bufs[0] = h2.b
                    sin_big(h2[:, sl], ps, 512, P(1, 0), P(1, 1), P(1, 4), P(1, 5), s4, s8)
                filt_v = [filt_tm[i].rearrange("(t p) c -> p t c", p=128) for i in range(2)]
                for b in range(8):
                    fw.dma(sp, win[:], trow.partition_broadcast(128), writes=win.b)
                    fw.op(act, lambda h, b=b: h.activation(out=win[:], in_=win[:], func=AF.Exp, scale=PRM[:, 2 + b, 1:2]),
                          reads=win.b + PRM.b, writes=win.b)
                    for (half, dst) in ((0, hfw), (1, hbw)):
                        for tt in range(NT):
                            sl = slice(tt * 512, (tt + 1) * 512)
                            ps = fw.bank()
                            c0 = half * 1024 + b * 128
                            fw.op(pe, lambda h, ps=ps, sl=sl, c0=c0: h.matmul(ps[:], lhsT=w3[:, c0:c0 + 128], rhs=h2[:, sl], start=True, stop=True),
                                  reads=w3.b + h2.b, writes=ps.b)
                            fw.op(dve, lambda h, ps=ps, sl=sl, dst=dst: h.tensor_tensor(out=dst[:, sl], in0=ps[:], in1=win[:, sl], op=ALU.mult),
                                  reads=ps.b + win.b, writes=dst.b)
                    fw.op(dve, lambda h: h.memset(hbw[:, 0:1], 0.0), writes=hbw.b)
                    fw.op(dve, lambda h: h.tensor_reduce(out=nrm[:, 0:1], in_=hfw[:], axis=mybir.AxisListType.X, op=ALU.add, apply_absolute_value=True),
                          reads=hfw.b, writes=nrm.b)
                    fw.op(dve, lambda h: h.tensor_reduce(out=nrm[:, 1:2], in_=hbw[:], axis=mybir.AxisListType.X, op=ALU.add, apply_absolute_value=True),
                          reads=hbw.b, writes=nrm.b)
                    fw.op(dve, lambda h: h.tensor_tensor(out=nrm[:, 2:3], in0=nrm[:, 0:1], in1=nrm[:, 1:2], op=ALU.add), reads=nrm.b, writes=nrm.b)
                    fw.op(dve, lambda h: h.reciprocal(out=nrm[:, 3:4], in_=nrm[:, 2:3]), reads=nrm.b, writes=nrm.b)
                    fw.op(dve, lambda h: h.tensor_tensor(out=win[:], in0=hfw[:], in1=hbw[:], op=ALU.add), reads=hfw.b + hbw.b, writes=win.b)
                    fw.op(act, lambda h: h.activation(out=hsum[:], in_=win[:], func=AF.Identity, scale=nrm[:, 3:4]), reads=win.b + nrm.b, writes=hsum.b)
                    fw.op(dve, lambda h: h.tensor_tensor(out=hfw[:], in0=hfw[:], in1=hbw[:], op=ALU.subtract), reads=hfw.b + hbw.b, writes=hfw.b)
                    fw.op(act, lambda h: h.activation(out=hdif[:], in_=hfw[:], func=AF.Identity, scale=nrm[:, 3:4]), reads=hfw.b + nrm.b, writes=hdif.b)
                    transpose_to_tm(hsum, filt_v[0][:, :, b * 128:(b + 1) * 128])
                    transpose_to_tm(hdif, filt_v[1][:, :, b * 128:(b + 1) * 128])
                fw.barrier()
            with ExitStack() as st:
                DFT_SRC[0] = sb(st, "gsrc0", [128, 32, 512], BF16)
                DFT_SRC[1] = sb(st, "gsrc1", [128, 32, 512], BF16)
                DFT_TAB[0] = sb(st, "gtab0", [128, 2 * 32 * 128], BF16)
                DFT_TAB[1] = sb(st, "gtab1", [128, 2 * 32 * 128], BF16)
                brow = sb(st, "gbrow", [128, 1024])
                go = [sb(st, "gout%d" % i, [128, 2, 512]) for i in range(2)]
                fw.dma(sp, brow[:], hy_bias[j].partition_broadcast(128), writes=brow.b)
                filt_v = [filt_tm[i].rearrange("(t p) c -> p t c", p=128) for i in range(2)]
                for half in range(2):
                    c0 = half * 512

                    def consume(kb, g0, gw_, pc, pss, c0=c0):
                        g = go[kb % 2]
                        fw.op(dve, lambda h: h.tensor_tensor(out=g[:, 0, :], in0=pc[:], in1=brow[:, c0:c0 + 512], op=ALU.add),
                              reads=pc.b + brow.b, writes=g.b)
                        fw.op(act, lambda h: h.copy(out=g[:, 1, :], in_=pss[:]), reads=pss.b, writes=g.b)
                        fw.dma(sp, Gsp[j, kb, :, :, c0:c0 + 512], g[:], reads=g.b, defer=True)
                    dft_forward(filt_v[0], filt_v[1], c0, 512, consume)
                fw.barrier()

        def phase_hyena(s, j):
            w_v = ev_w_in[j].rearrange("(k p) c -> p k c", p=128)
            vf_v = vf_tm.rearrange("(t p) c -> p t c", p=128)
            WLOADED.clear()
            with ExitStack() as st:
                wb = [sb(st, "hwb%d" % i, [128, 8, 128], BF16) for i in range(2)]
                xp = [sb(st, "hxp%d" % i, [128, L + 2]) for i in range(2)]
                tmp = sb(st, "htmp", [128, L])
                c0t = sb(st, "hc0", [128, L])
                c1t = sb(st, "hc1", [128, L])
                sg = sb(st, "hsg", [128, L], BF16)
                ob = [sb(st, "hob%d" % i, [128, L], BF16) for i in range(3)]
                TSTG[0] = sb(st, "hst0", [128, 8, 128], BF16)
                TSTG[1] = sb(st, "hst1", [128, 8, 128], BF16)
                for k in range(3):
                    ldvec(PRM[:, 0:24, k], hy_conv_w[j, k], PRM.b)
                ldvec(PRM[:, 0:24, 3], hy_conv_b[j], PRM.b)
                for x in xp:
                    fw.op(dve, lambda h, x=x: h.memset(x[:, 0:1], 0.0), writes=x.b)
                    fw.op(dve, lambda h, x=x: h.memset(x[:, L + 1:L + 2], 0.0), writes=x.b)
                seq = []
                for b in range(8):
                    seq += [b * 128, 3072 + b * 128, (8 + b) * 128, (16 + b) * 128]
                n = 0

                def nxt_():
                    return (seq[n + 1], wb[(n + 1) % 2]) if n + 1 < len(seq) else None

                def conv_block(blk, out_t):
                    nonlocal n
                    x = xp[n % 2]
                    w = wb[n % 2]
                    nx = nxt_()
                    n += 1
                    proj_fm(w_v, blk * 128, w, lambda tt, ps, x=x: fw.op(
                        act, lambda h: h.copy(out=x[:, 1 + tt * 512:1 + (tt + 1) * 512], in_=ps[:]), reads=ps.b, writes=x.b), nx)
                    conv_fm(x, 3, lambda k, blk=blk: PRM[:, blk, k:k + 1], out_t, tmp)

                for b in range(8):
                    conv_block(b, c0t)
                    w = wb[n % 2]
                    nx = nxt_()
                    n += 1
                    proj_fm(w_v, 3072 + b * 128, w, lambda tt, ps: fw.op(
                        act, lambda h: h.activation(out=sg[:, tt * 512:(tt + 1) * 512], in_=ps[:], func=AF.Silu), reads=ps.b, writes=sg.b), nx)
                    o = ob[0]
                    fw.op(dve, lambda h, o=o: h.tensor_tensor(out=o[:], in0=c0t[:], in1=sg[:], op=ALU.mult), reads=c0t.b + sg.b, writes=o.b)
                    fw.dma(sp, xg_fm[b * 128:(b + 1) * 128, :], o[:], reads=o.b, defer=True)
                    conv_block(8 + b, c0t)
                    conv_block(16 + b, c1t)
                    o = ob[1 + b % 2]
                    fw.op(dve, lambda h, o=o: h.tensor_tensor(out=o[:], in0=c0t[:], in1=c1t[:], op=ALU.mult), reads=c0t.b + c1t.b, writes=o.b)
                    PEND_T.append((o, vf_v[:, :, b * 128:(b + 1) * 128]))
                while PEND_T:
                    transpose_to_tm(*PEND_T.pop(0))
                fw.barrier()

        def phase_hyena_dft(s, j):
            vf_v = vf_tm.rearrange("(t p) c -> p t c", p=128)
            with ExitStack() as st:
                DFT_SRC[0] = sb(st, "dsrc0", [128, 32, 1024], BF16)
                DFT_TAB[0] = sb(st, "dtab0", [128, 2 * 32 * 128], BF16)
                DFT_TAB[1] = sb(st, "dtab1", [128, 2 * 32 * 128], BF16)
                Yt = [sb(st, "dY%d" % i, [128, 2, 1024], BF16) for i in range(2)]
                gt = [sb(st, "dg%d" % i, [128, 2, 1024]) for i in range(2)]
                t1 = sb(st, "dt1", [128, 512])
                t2 = sb(st, "dt2", [128, 512])

                def consume(kb, g0, gw_, pc, pss):
                    g = gt[kb % 2]
                    Y = Yt[kb % 2]
                    cs_ = slice(g0, g0 + gw_)
                    if g0 == 0:
                        fw.dma(sp, g[:], Gsp[j, kb], writes=g.b)
                    fw.op(dve, lambda h: h.tensor_tensor(out=t1[:, 0:gw_], in0=pc[:, 0:gw_], in1=g[:, 0, cs_], op=ALU.mult), reads=pc.b + g.b, writes=t1.b)
                    fw.op(dve, lambda h: h.tensor_tensor(out=t2[:, 0:gw_], in0=pss[:, 0:gw_], in1=g[:, 1, cs_], op=ALU.mult), reads=pss.b + g.b, writes=t2.b)
                    fw.op(pool, lambda h: h.tensor_tensor(out=Y[:, 0, cs_], in0=t1[:, 0:gw_], in1=t2[:, 0:gw_], op=ALU.subtract), reads=t1.b + t2.b, writes=Y.b)
                    fw.op(dve, lambda h: h.tensor_tensor(out=t1[:, 0:gw_], in0=pc[:, 0:gw_], in1=g[:, 1, cs_], op=ALU.mult), reads=pc.b + g.b, writes=t1.b)
                    fw.op(dve, lambda h: h.tensor_tensor(out=t2[:, 0:gw_], in0=pss[:, 0:gw_], in1=g[:, 0, cs_], op=ALU.mult), reads=pss.b + g.b, writes=t2.b)
                    fw.op(pool, lambda h: h.tensor_tensor(out=Y[:, 1, cs_], in0=t1[:, 0:gw_], in1=t2[:, 0:gw_], op=ALU.add), reads=t1.b + t2.b, writes=Y.b)
                    if g0 + gw_ == 1024:
                        fw.dma(sp, Ysp[kb], Y[:], reads=Y.b, defer=True)
                dft_forward(vf_v, None, 0, 1024, consume)
                fw.barrier()
            with ExitStack() as st:
                Yh = sb(st, "dYh", [128, 32, 2, 512], BF16)
                ti = [sb(st, "dti%d" % i, [128, 2, 512], BF16) for i in range(3)]
                xg = [sb(st, "dxg%d" % i, [128, 4, 512], BF16) for i in range(2)]
                mo = [sb(st, "dmo%d" % i, [128, 4, 512], BF16) for i in range(2)]
                for half in range(2):
                    c0 = half * 512
                    for cs in range(2):
                        fw.dma(sp, Yh[:, :, cs, :], Ysp[:, :, cs, c0:c0 + 512].rearrange("k p c -> p k c"), writes=Yh.b)
                    for nq in range(8):
                        banks = [fw.bank() for _ in range(4)]
                        xgt = xg[nq % 2]
                        fw.dma(sp, xgt[:], xg_fm[c0:c0 + 512, nq * 512:(nq + 1) * 512].rearrange("(b p) n -> p b n", p=128), writes=xgt.b)
                        for kb in range(32):
                            t = ti[kb % 3]
                            fw.dma(sp, t[:], tabI[nq, kb].rearrange("p (cs n) -> p cs n", cs=2), writes=t.b)
                            if kb == 2:
                                fw.flush()
                            for cb in range(4):
                                for cs in range(2):
                                    fw.op(pe, lambda h, cb=cb, cs=cs, kb=kb, t=t, banks=banks: h.matmul(
                                        banks[cb][:], lhsT=Yh[:, kb, cs, cb * 128:(cb + 1) * 128], rhs=t[:, cs, :],
                                        start=(kb == 0 and cs == 0), stop=(kb == 31 and cs == 1)),
                                        reads=Yh.b + t.b, writes=banks[cb].b)
                        m = mo[nq % 2]
                        for cb in range(4):
                            fw.op(dve, lambda h, cb=cb, m=m, xgt=xgt, banks=banks: h.tensor_tensor(out=m[:, cb, :], in0=banks[cb][:], in1=xgt[:, cb, :], op=ALU.mult),
                                  reads=banks[cb].b + xgt.b, writes=m.b)
                        fw.dma(sp, mixfm[c0:c0 + 512, nq * 512:(nq + 1) * 512].rearrange("(b p) n -> p b n", p=128), m[:], reads=m.b, defer=True)
                fw.barrier()

        def phase_ssd(s, j):
            w_v = ev_w_in[j].rearrange("(k p) c -> p k c", p=128)
            xs_v = xs_tm.rearrange("(t p) c -> p t c", p=128)
            Bt_v = B_tm.rearrange("(t p) c -> p t c", p=128)
            WLOADED.clear()
            with ExitStack() as st0:
                TM = sb(st0, "sTM", [128, 32, 4, 64])
                ecd = sb(st0, "secd", [128, 64, 32])
                with ExitStack() as st:
                    wb = [sb(st, "swb%d" % i, [128, 8, 128], BF16) for i in range(2)]
                    xp = [sb(st, "sxp0", [128, L + 3])] * 2
                    tmp = sb(st, "stmp", [128, L])
                    c0t = sb(st, "sc0", [128, L])
                    ob = [sb(st, "sob%d" % i, [128, L], BF16) for i in range(2)]
                    wz = sb(st, "swz", [128, 8, 1024], BF16)
                    zt = [sb(st, "szt%d" % i, [128, 1024], BF16) for i in range(2)]
                    TSTG[0] = sb(st, "sst0", [128, 8, 128], BF16)
                    TSTG[1] = sb(st, "sst1", [128, 8, 128], BF16)
                    fw.dma(pool, wz[:], w_v[:, :, 4096:5120], writes=wz.b)
                    for k in range(4):
                        ldvec(PRM[:, 0:16, k], ssd_conv_w[j, k], PRM.b)
                    ldvec(PRM[:, 0:16, 4], ssd_conv_b[j], PRM.b)
                    for x in xp[:1]:
                        fw.op(dve, lambda h, x=x: h.memset(x[:, 0:2], 0.0), writes=x.b)
                        fw.op(dve, lambda h, x=x: h.memset(x[:, L + 2:L + 3], 0.0), writes=x.b)
                    for blk in range(16):
                        x = xp[blk % 2]
                        o = ob[blk % 2]
                        proj_fm(w_v, 5120 + blk * 128, wb[blk % 2], lambda tt, ps, x=x: fw.op(
                            act, lambda h: h.copy(out=x[:, 2 + tt * 512:2 + (tt + 1) * 512], in_=ps[:]), reads=ps.b, writes=x.b),
                            (5120 + (blk + 1) * 128, wb[(blk + 1) % 2]) if blk < 15 else None)
                        conv_fm(x, 4, lambda k, blk=blk: PRM[:, blk, k:k + 1], c0t, tmp)
                        fw.op(act, lambda h, o=o: h.activation(out=o[:], in_=c0t[:], func=AF.Silu), reads=c0t.b, writes=o.b)
                        if blk < 8:
                            PEND_T.append((o, xs_v[:, :, blk * 128:(blk + 1) * 128]))
                        elif blk < 12:
                            fw.dma(sp, B_fm[(blk - 8) * 128:(blk - 7) * 128, :], o[:], reads=o.b, defer=True)
                            PEND_T.append((o, Bt_v[:, :, (blk - 8) * 128:(blk - 7) * 128]))
                        else:
                            fw.dma(sp, C_fm[(blk - 12) * 128:(blk - 11) * 128, :], o[:], reads=o.b, defer=True)
                    while PEND_T:
                        transpose_to_tm(*PEND_T.pop(0))
                    for c in range(32):
                        z = zt[c % 2]
                        for half in range(2):
                            ps = fw.bank()
                            for k in range(8):
                                fw.op(pe, lambda h, k=k, ps=ps, c=c, half=half: h.matmul(
                                    ps[:], lhsT=hn[:, k, c * 128:(c + 1) * 128], rhs=wz[:, k, half * 512:(half + 1) * 512],
                                    start=(k == 0), stop=(k == 7)), reads=wz.b + [hn.b[c // 4]], writes=ps.b)
                            fw.op(act, lambda h, ps=ps, z=z, half=half: h.activation(out=z[:, half * 512:(half + 1) * 512], in_=ps[:], func=AF.Silu),
                                  reads=ps.b, writes=z.b)
                        fw.dma(sp, z_tm[c * 128:(c + 1) * 128, :], z[:], reads=z.b, defer=True)
                    fw.barrier()
                with ExitStack() as st:
                    wdt = sb(st, "swdt", [128, 8, 64], BF16)
                    dtf = sb(st, "sdtf", [64, L])
                    laf = sb(st, "slaf", [64, L])
                    Wt = sb(st, "sW", [64, L])
                    mk = sb(st, "smk", [64, L])
                    Tt = sb(st, "sT", [64, 32])
                    pp = sb(st, "spp", [64, 4])
                    fw.op(dve, lambda h: h.memset(wdt[:], 0.0), writes=wdt.b)
                    fw.op(dve, lambda h: h.memset(pp[:], 0.0), writes=pp.b)
                    for d in range(2):
                        fw.dma(pool, wdt[:, :, d * 32:d * 32 + 16], w_v[:, :, 7168 + d * 16:7168 + (d + 1) * 16], writes=wdt.b)
                        with nc.allow_non_contiguous_dma(reason="tiny"):
                            fw.dma(sp, pp[d * 32:d * 32 + 16, 0:1], ssd_dt_bias[j, d * 16:(d + 1) * 16].rearrange("(p o) -> p o", o=1), writes=pp.b)
                            fw.dma(sp, pp[d * 32:d * 32 + 16, 1:2], ssd_A_log[j, d * 16:(d + 1) * 16].rearrange("(p o) -> p o", o=1), writes=pp.b)
                    fw.op(act, lambda h: h.activation(out=pp[:, 2:3], in_=pp[:, 1:2], func=AF.Exp), reads=pp.b, writes=pp.b)
                    fw.op(dve, lambda h: h.tensor_scalar(out=pp[:, 2:3], in0=pp[:, 2:3], scalar1=-1.0, scalar2=None, op0=ALU.mult), reads=pp.b, writes=pp.b)
                    for tt in range(NT):
                        sl = slice(tt * 512, (tt + 1) * 512)
                        ps = fw.bank()
                        for k in range(8):
                            fw.op(pe, lambda h, k=k, ps=ps, sl=sl: h.matmul(ps[0:64, :], lhsT=wdt[:, k, :], rhs=hn[:, k, sl], start=(k == 0), stop=(k == 7)),
                                  reads=wdt.b + [hn.b[tt]], writes=ps.b)
                        fw.op(act, lambda h, ps=ps, sl=sl: h.activation(out=dtf[:, sl], in_=ps[0:64, :], func=AF.Exp, bias=pp[:, 0:1], scale=1.0),
                              reads=ps.b + pp.b, writes=dtf.b)
                    fw.op(act, lambda h: h.activation(out=dtf[:], in_=dtf[:], func=AF.Ln, bias=1.0, scale=1.0), reads=dtf.b, writes=dtf.b)
                    fw.op(dve, lambda h: h.tensor_scalar(out=laf[:], in0=dtf[:], scalar1=pp[:, 2:3], scalar2=None, op0=ALU.mult),
                          reads=dtf.b + pp.b, writes=laf.b)
                    fw.op(dve, lambda h: h.memset(Wt[:], 0.0), writes=Wt.b)
                    fw.dma(sp, mk[:], mrow_d[0].partition_broadcast(64), writes=mk.b)
                    fw.op(dve, lambda h: h.tensor_tensor_scan(out=Wt[0:16, :], data0=mk[0:16, :], data1=laf[0:16, :], initial=0.0,
                                                              op0=ALU.mult, op1=ALU.add), reads=mk.b + laf.b, writes=Wt.b)
                    fw.dma(sp, mk[:], mrow_d[1].partition_broadcast(64), writes=mk.b)
                    fw.op(dve, lambda h: h.tensor_tensor_scan(out=Wt[32:48, ::-1], data0=mk[32:48, ::-1], data1=laf[32:48, ::-1], initial=0.0,
                                                              op0=ALU.mult, op1=ALU.add), reads=mk.b + laf.b, writes=Wt.b)
                    Wv = Wt[:].rearrange("p (c l) -> p c l", l=128)
                    fw.op(dve, lambda h: h.memset(Tt[:], 0.0), writes=Tt.b)
                    fw.op(dve, lambda h: h.tensor_copy(out=Tt[0:16, :], in_=Wv[0:16, :, 127]), reads=Wt.b, writes=Tt.b)
                    fw.op(dve, lambda h: h.tensor_copy(out=Tt[32:48, :], in_=Wv[32:48, :, 0]), reads=Wt.b, writes=Tt.b)
                    fw.op(dve, lambda h: h.tensor_tensor(out=mk[:].rearrange("p (c l) -> p c l", l=128), in0=Tt[:].unsqueeze(2).to_broadcast([64, 32, 128]),
                                                         in1=Wv, op=ALU.subtract), reads=Tt.b + Wt.b, writes=mk.b)
                    fw.op(act, lambda h: h.activation(out=mk[:], in_=mk[:], func=AF.Exp), reads=mk.b, writes=mk.b)
                    fw.op(act, lambda h: h.activation(out=laf[:], in_=Wt[:], func=AF.Exp), reads=Wt.b, writes=laf.b)
                    fw.op(act, lambda h: h.activation(out=Tt[:], in_=Tt[:], func=AF.Exp), reads=Tt.b, writes=Tt.b)
                    fw.dma(sp, Wd[:, :], Wt[:], reads=Wt.b, defer=True)
                    fw.dma(sp, ecd_d.rearrange("(p c) -> p c", c=32), Tt[:], reads=Tt.b, defer=True)
                    for c in range(32):
                        ps = fw.bank()
                        for qi, q in enumerate((dtf, Wt, laf, mk)):
                            fw.op(pe, lambda h, qi=qi, q=q, ps=ps, c=c: h.transpose(out=ps[:, qi * 64:(qi + 1) * 64], in_=q[:, c * 128:(c + 1) * 128],
                                                                                 identity=identF[0:64, 0:64]), reads=q.b + identF.b, writes=ps.b)
                        fw.op(act, lambda h, ps=ps, c=c: h.copy(out=TM[:, c, :, :], in_=ps[:, 0:256].rearrange("p (q d) -> p q d", d=64)),
                              reads=ps.b, writes=TM.b)
                    fw.barrier()
                    fw.dma(sp, ecd[:].rearrange("p d c -> p (d c)"), ecd_d.partition_broadcast(128), writes=ecd.b)
                    fw.barrier()
                with ExitStack() as st:
                    prev = sb(st, "sprev", [128, 1024])
                    prevb = sb(st, "sprevb", [128, 1024], BF16)
                    xs_t = [sb(st, "sxs%d" % i, [128, 1024], BF16) for i in range(2)]
                    Btt = [sb(st, "sBt%d" % i, [128, 512], BF16) for i in range(2)]
                    Bft = [sb(st, "sBf%d" % i, [128, 4, 128], BF16) for i in range(2)]
                    Cft = [sb(st, "sCf%d" % i, [128, 4, 128], BF16) for i in range(2)]
                    wbc = [sb(st, "swbc%d" % i, [128, 16, 128]) for i in range(2)]
                    xsd2 = [sb(st, "sxsd%d" % i, [128, 1024], BF16) for i in range(2)]
                    xsdo2 = [sb(st, "sxsdo%d" % i, [128, 1024], BF16) for i in range(2)]
                    cbm = sb(st, "scbm", [128, 4, 128], BF16)
                    ddA = sb(st, "sddA", [128, 16, 128])
                    EA = sb(st, "sEA", [128, 16, 128], BF16)
                    MA = sb(st, "sMA", [128, 16, 128], BF16)
                    ysum = sb(st, "sys", [128, 1024])
                    yft2 = [sb(st, "syf%d" % i, [128, 1024]) for i in range(2)]
                    tmpd = sb(st, "std", [128, 1024])
                    ztt2 = [sb(st, "sztt%d" % i, [128, 1024], BF16) for i in range(2)]
                    ngrow = sb(st, "sng", [128, 1024])
                    Drow = sb(st, "sD", [128, 16])
                    ynb = sb(st, "synb", [128, 1024], BF16)
                    mo = sb(st, "smo", [128, 8, 128], BF16)
                    ss = sb(st, "sss", [128, 8])
                    junk = sb(st, "sjunk", [128, 256])
                    eps5 = sb(st, "seps5", [128, 1])
                    mks = [sb(st, "smask%d" % i, [128, 128]) for i in range(2)]
                    fw.dma(sp, ngrow[:], ssd_norm_g[j].partition_broadcast(128), writes=ngrow.b)
                    fw.dma(sp, Drow[:], ssd_D[j].partition_broadcast(128), writes=Drow.b)
                    fw.dma(sp, mks[0][:], mask_d[0], writes=mks[0].b)
                    fw.dma(sp, mks[1][:], mask_d[1], writes=mks[1].b)
                    fw.op(dve, lambda h: h.memset(eps5[:], 1e-5), writes=eps5.b)
                    ecdv = ecd
                    Bf_v = B_fm.rearrange("(g n) l -> n g l", n=128)
                    Cf_v = C_fm.rearrange("(g n) l -> n g l", n=128)
                    mix_o = mixfm[1024:2048, :].rearrange("(b p) l -> p b l", p=128)
                    it = [0]

                    def front(d, c):
                        ro = d * 32
                        i_ = it[0]
                        it[0] += 1
                        X = dict(c=c, d=d, ro=ro, xs=xs_t[i_ % 2], Bt=Btt[i_ % 2], Bf=Bft[i_ % 2], Cf=Cft[i_ % 2], wb=wbc[i_ % 2],
                                 xsd=xsd2[i_ % 2], xsdo=xsdo2[i_ % 2], yft=yft2[i_ % 2], ztt=ztt2[i_ % 2])
                        yft, ztt = X["yft"], X["ztt"]
                        xs, Bt, Bf, Cf, wb_, xsd, xsdo = X["xs"], X["Bt"], X["Bf"], X["Cf"], X["wb"], X["xsd"], X["xsdo"]
                        rows = slice(c * 128, (c + 1) * 128)
                        fw.dma(sp, xs[:], xs_tm[rows, :], writes=xs.b)
                        fw.dma(sp, Bt[:], B_tm[rows, :], writes=Bt.b)
                        with nc.allow_non_contiguous_dma(reason="256B runs"):
                            fw.dma(sp, Bf[:], Bf_v[:, :, rows], writes=Bf.b)
                            fw.dma(sp, Cf[:], Cf_v[:, :, rows], writes=Cf.b)
                        fw.dma(sp, wb_[:], Wd[ro:ro + 16, rows].unsqueeze(0).to_broadcast([128, 16, 128]), writes=wb_.b)
                        if d == 1:
                            fw.dma(sp, yft[:], Yf[rows, :], writes=yft.b)
                            fw.dma(sp, ztt[:], z_tm[rows, :], writes=ztt.b)
                        fw.flush()
                        xs3 = xs[:].rearrange("p (h e) -> p h e", e=64)
                        fw.op(dve, lambda h: h.tensor_tensor(out=xsd[:].rearrange("p (h e) -> p h e", e=64), in0=xs3,
                                                             in1=TM[:, c, 0, ro:ro + 16].unsqueeze(2).to_broadcast([128, 16, 64]), op=ALU.mult),
                              reads=xs.b + TM.b, writes=xsd.b)
                        fw.op(pool, lambda h: h.tensor_tensor(out=xsdo[:].rearrange("p (h e) -> p h e", e=64),
                                                              in0=xsd[:].rearrange("p (h e) -> p h e", e=64),
                                                              in1=TM[:, c, 3, ro:ro + 16].unsqueeze(2).to_broadcast([128, 16, 64]), op=ALU.mult),
                              reads=xsd.b + TM.b, writes=xsdo.b)
                        pcb = fw.psum[4]
                        for g in range(4):
                            fw.op(pe, lambda h, g=g: h.matmul(pcb[:, g * 128:(g + 1) * 128], lhsT=Bf[:, g, :], rhs=Cf[:, g, :], start=True, stop=True),
                                  reads=Bf.b + Cf.b, writes=pcb.b)
                        fw.op(dve, lambda h: h.tensor_tensor(out=cbm[:], in0=pcb[:].rearrange("p (g l) -> p g l", l=128),
                                                             in1=mks[d][:].unsqueeze(1).to_broadcast([128, 4, 128]), op=ALU.mult),
                              reads=pcb.b + mks[d].b, writes=cbm.b)
                        Yd = [fw.psum[2 * (i_ % 2)], fw.psum[2 * (i_ % 2) + 1]]
                        X["Yd"] = Yd
                        wsb = TM[:, c, 1, ro:ro + 16].unsqueeze(2).to_broadcast([128, 16, 128])
                        fw.op(dve, lambda h: h.tensor_tensor(out=ddA[:], in0=wb_[:], in1=wsb, op=ALU.min),
                              reads=wb_.b + TM.b, writes=ddA.b)
                        fw.op(pool, lambda h: h.tensor_tensor(out=ddA[:], in0=ddA[:], in1=wsb, op=ALU.subtract),
                              reads=ddA.b + TM.b, writes=ddA.b)
                        fw.op(act, lambda h: h.activation(out=EA[:], in_=ddA[:], func=AF.Exp), reads=ddA.b, writes=EA.b)
                        fw.op(dve, lambda h: h.tensor_tensor(out=MA[:].rearrange("p (g r) l -> p g r l", r=4),
                                                             in0=EA[:].rearrange("p (g r) l -> p g r l", r=4),
                                                             in1=cbm[:].unsqueeze(2).to_broadcast([128, 4, 4, 128]), op=ALU.mult),
                              reads=EA.b + cbm.b, writes=MA.b)
                        for hh in range(16):
                            fw.op(pe, lambda h, hh=hh: h.matmul(Yd[hh // 8][:, (hh % 8) * 64:(hh % 8 + 1) * 64], lhsT=MA[:, hh, :],
                                                                rhs=xsd[:, hh * 64:(hh + 1) * 64], start=True, stop=True),
                                  reads=MA.b + xsd.b, writes=Yd[hh // 8].b)
                        return X

                    def back(X):
                        c, d, ro = X["c"], X["d"], X["ro"]
                        xs, Bt, Cf, xsdo, Yd = X["xs"], X["Bt"], X["Cf"], X["xsdo"], X["Yd"]
                        yft, ztt = X["yft"], X["ztt"]
                        rows = slice(c * 128, (c + 1) * 128)
                        xs3 = xs[:].rearrange("p (h e) -> p h e", e=64)
                        Yo = [fw.psum[5], fw.psum[6]]
                        for g in range(4):
                            fw.op(pe, lambda h, g=g: h.matmul(Yo[g // 2][:, (g % 2) * 256:(g % 2 + 1) * 256], lhsT=Cf[:, g, :],
                                                              rhs=prevb[:, g * 256:(g + 1) * 256], start=True, stop=True),
                                  reads=Cf.b + prevb.b, writes=Yo[g // 2].b)
                        for half in range(2):
                            fw.op(dve, lambda h, half=half: h.tensor_tensor(
                                out=ysum[:, half * 512:(half + 1) * 512].rearrange("p (h e) -> p h e", e=64),
                                in0=Yo[half][:].rearrange("p (h e) -> p h e", e=64),
                                in1=TM[:, c, 2, ro + half * 8:ro + half * 8 + 8].unsqueeze(2).to_broadcast([128, 8, 64]), op=ALU.mult),
                                reads=Yo[half].b + TM.b, writes=ysum.b)
                        St = [fw.psum[5], fw.psum[6]]
                        for g in range(4):
                            fw.op(pe, lambda h, g=g: h.matmul(St[g // 2][:, (g % 2) * 256:(g % 2 + 1) * 256], lhsT=Bt[:, g * 128:(g + 1) * 128],
                                                              rhs=xsdo[:, g * 256:(g + 1) * 256], start=True, stop=True),
                                  reads=Bt.b + xsdo.b, writes=St[g // 2].b)
                        for half in range(2):
                            fw.op(dve, lambda h, half=half: h.tensor_tensor(out=ysum[:, half * 512:(half + 1) * 512], in0=Yd[half][:],
                                                                           in1=ysum[:, half * 512:(half + 1) * 512], op=ALU.add),
                                  reads=Yd[half].b + ysum.b, writes=ysum.b)
                        fw.op(pool, lambda h: h.tensor_tensor(out=prev[:].rearrange("p (h e) -> p h e", e=64), in0=prev[:].rearrange("p (h e) -> p h e", e=64),
                                                              in1=ecd[:, ro:ro + 16, c].unsqueeze(2).to_broadcast([128, 16, 64]), op=ALU.mult),
                              reads=prev.b + ecd.b, writes=prev.b)
                        for half in range(2):
                            fw.op(dve, lambda h, half=half: h.tensor_tensor(out=prev[:, half * 512:(half + 1) * 512], in0=St[half][:],
                                                                           in1=prev[:, half * 512:(half + 1) * 512], op=ALU.add),
                                  reads=St[half].b + prev.b, writes=prev.b)
                        fw.op(act, lambda h: h.copy(out=prevb[:], in_=prev[:]), reads=prev.b, writes=prevb.b)
                        if d == 0:
                            fw.dma(sp, Yf[rows, :], ysum[:], reads=ysum.b, defer=True)
                        else:
                            fw.op(pool, lambda h: h.tensor_tensor(out=ysum[:], in0=ysum[:], in1=yft[:], op=ALU.add), reads=ysum.b + yft.b, writes=ysum.b)
                            fw.op(dve, lambda h: h.tensor_tensor(out=tmpd[:].rearrange("p (h e) -> p h e", e=64), in0=xs3,
                                                                 in1=Drow[:].unsqueeze(2).to_broadcast([128, 16, 64]), op=ALU.mult),
                                  reads=xs.b + Drow.b, writes=tmpd.b)
                            fw.op(pool, lambda h: h.tensor_tensor(out=ysum[:], in0=ysum[:], in1=tmpd[:], op=ALU.add), reads=ysum.b + tmpd.b, writes=ysum.b)
                            fw.op(pool, lambda h: h.tensor_tensor(out=ysum[:], in0=ysum[:], in1=ztt[:], op=ALU.mult), reads=ysum.b + ztt.b, writes=ysum.b)
                            for g in range(4):
                                fw.op(act, lambda h, g=g: h.activation(out=junk[:], in_=ysum[:, g * 256:(g + 1) * 256], func=AF.Square,
                                                                       accum_out=ss[:, g:g + 1]), reads=ysum.b, writes=junk.b + ss.b)
                            fw.op(act, lambda h: h.activation(out=ss[:, 4:8], in_=ss[:, 0:4], func=AF.Sqrt, scale=1.0 / 256.0, bias=eps5[:]),
                                  reads=ss.b + eps5.b, writes=ss.b)
                            fw.op(dve, lambda h: h.reciprocal(out=ss[:, 4:8], in_=ss[:, 4:8]), reads=ss.b, writes=ss.b)
                            fw.op(dve, lambda h: h.tensor_tensor(out=ysum[:].rearrange("p (g e) -> p g e", e=256), in0=ysum[:].rearrange("p (g e) -> p g e", e=256),
                                                                 in1=ss[:, 4:8].unsqueeze(2).to_broadcast([128, 4, 256]), op=ALU.mult),
                                  reads=ysum.b + ss.b, writes=ysum.b)
                            fw.op(pool, lambda h: h.tensor_tensor(out=ynb[:], in0=ysum[:], in1=ngrow[:], op=ALU.mult), reads=ysum.b + ngrow.b, writes=ynb.b)
                            for t8 in range(2):
                                ps = fw.psum[7] if t8 == 0 else fw.psum[4]
                                psb = ps[:].bitcast(BF16)
                                for i in range(4):
                                    cb = t8 * 4 + i
                                    fw.op(pe, lambda h, i=i, cb=cb, psb=psb, ps=ps: h.transpose(out=psb[:, i * 128:(i + 1) * 128],
                                                                                              in_=ynb[:, cb * 128:(cb + 1) * 128], identity=identB[:]),
                                          reads=ynb.b + identB.b, writes=ps.b)
                                fw.op(act, lambda h, t8=t8, psb=psb, ps=ps: h.copy(out=mo[:, t8 * 4:(t8 + 1) * 4, :],
                                                                                  in_=psb[:, 0:512].rearrange("p (i c) -> p i c", c=128)),
                                      reads=ps.b, writes=mo.b)
                            fw.dma(sp, mix_o[:, :, rows], mo[:], reads=mo.b, defer=True)

                    for d in range(2):
                        if d == 1:
                            fw.barrier()
                        fw.op(dve, lambda h: h.memset(prev[:], 0.0), writes=prev.b)
                        fw.op(dve, lambda h: h.memset(prevb[:], 0.0), writes=prevb.b)
                        order = list(range(32)) if d == 0 else list(range(31, -1, -1))
                        X = front(d, order[0])
                        for i_c in range(32):
                            Xn = front(d, order[i_c + 1]) if i_c + 1 < 32 else None
                            back(X)
                            X = Xn
                    fw.barrier()

        oneT = sb(es, "oneT", [128, 1])
        fw.op(dve, lambda h: h.memset(oneT[:], 1.0), writes=oneT.b)
        for j in range(2):
            if 2 * j in layers:
                phase_filter(j)

        for s in range(S):
            first = True
            for li in range(4):
                if li not in layers:
                    continue
                if first:
                    phase_norm0(s, li)
                    first = False
                rest = [l for l in layers if l > li]
                nli = rest[0] if rest else 4
                if li % 2 == 1:
                    phase_odd(s, li // 2)
                    phase_out(s, li, od_w_out[li // 2], nli)
                else:
                    phase_hyena(s, li // 2)
                    phase_hyena_dft(s, li // 2)
                    if not DEBUG.get("no_ssd"):
                        phase_ssd(s, li // 2)
                    phase_out(s, li, ev_w_out[li // 2], nli)
    return nc


_CONST_CACHE = {}


def _consts():
    if _CONST_CACHE:
        return _CONST_CACHE
    N = 2 * L
    c = {"identF_d": np.eye(128, dtype=np.float32)}
    n = np.arange(L, dtype=np.float64)[:, None]
    k = np.arange(L, dtype=np.float64)[None, :]
    th = (2.0 * np.pi / N) * ((n * (2 * k + 1)) % (2 * N)) / 2.0
    Cm = np.cos(th).astype(np.float32)
    Sm = np.sin(th).astype(np.float32)
    del th
    bf = ml_dtypes.bfloat16
    tF = np.stack([Cm, Sm], 0).reshape(2, 32, 128, 32, 128)
    c["tabF"] = np.ascontiguousarray(tF.transpose(3, 2, 0, 1, 4)).astype(bf).reshape(32, 128, 2 * 32 * 128)
    tI = (np.stack([Cm, Sm], 0) * np.float32(2.0 / N)).reshape(2, 8, 512, 32, 128)
    c["tabI"] = np.ascontiguousarray(tI.transpose(1, 3, 4, 0, 2)).astype(bf).reshape(8, 32, 128, 2 * 512)
    del Cm, Sm, tF, tI
    pos = np.arange(L, dtype=np.float32)[:, None]
    t = pos / np.float32(L - 1)
    f = np.linspace(1e-4, 15, 16, dtype=np.float32)[None, :]
    ang = f * pos * np.float32(2.0 * math.pi / L)
    z = np.concatenate([t, np.cos(ang), -np.sin(ang)], axis=-1).astype(np.float32)
    c["zfeat"] = np.ascontiguousarray(z.T)
    c["trow"] = np.ascontiguousarray(t[:, 0])
    c["deltas"] = np.abs(np.linspace(math.log(1e-2) / 1.5, math.log(1e-2) / 0.3, 1024, dtype=np.float32)).astype(np.float32)
    l = np.arange(L)
    c["mrow"] = np.stack([(l % 128 != 0), (l % 128 != 127)]).astype(np.float32)
    si = np.arange(128)[:, None]
    li_ = np.arange(128)[None, :]
    c["maskd"] = np.stack([(li_ >= si), (li_ <= si)]).astype(np.float32)
    _CONST_CACHE.update(c)
    return _CONST_CACHE


def kernel(**inputs):
    return _run(inputs, (0, 1, 2, 3))


def _run(inputs, layers):
    inp = {k: np.ascontiguousarray(np.asarray(v)) for k, v in inputs.items()}
    seqs_x = [inp["x_prompt"][i] for i in range(8)] + [inp["x_sample"][i] for i in range(4)]
    seqs_c = [inp["c_prompt"][i] for i in range(8)] + [inp["c_sample"][i] for i in range(4)]
    assign = [(2 * k, 2 * k + 1) for k in range(4)] + [(8 + k, 8 + k) for k in range(4)]
    nc = build_program(layers)
    wnames = ["mod_w", "mod_b", "norm_g", "final_g", "od_w_in", "od_w_out", "lru_conv_w", "lru_conv_b",
              "lru_w_a", "lru_b_a", "lru_w_x", "lru_b_x", "lru_lam", "ev_w_out",
              "ev_w_in", "hy_conv_w", "hy_conv_b", "hy_fw1", "hy_fb1", "hy_fw2", "hy_fb2", "hy_fw3", "hy_freq",
              "hy_bias", "ssd_conv_w", "ssd_conv_b", "ssd_norm_g"]
    consts = _consts()
    in_maps = []
    for (a, b) in assign:
        m = {"xin": np.stack([seqs_x[a], seqs_x[b]]), "cin": np.stack([seqs_c[a], seqs_c[b]])}
        for w in wnames:
            m[w] = inp[w]
        m["ssd_dt_bias"] = inp["ssd_dt_bias"].reshape(2, 32)
        m["ssd_A_log"] = inp["ssd_A_log"].reshape(2, 32)
        m["ssd_D"] = inp["ssd_D"]
        m.update(consts)
        in_maps.append(m)
    res = run_bass_kernel_spmd(nc, in_maps, core_ids=list(range(NCORES)))
    if DEBUG.get("dump"):
        DEBUG["res"] = res.results
    outs = [r["yout"] for r in res.results]
    yp = np.stack([outs[k][i] for k in range(4) for i in range(2)]).astype(np.float32)
    ys = np.stack([outs[4 + k][0] for k in range(4)]).astype(np.float32)
    return (yp, ys)
```

```python
import math
from contextlib import ExitStack
import numpy as np
import ml_dtypes
import concourse.bass as bass
import concourse.mybir as mybir
from concourse.bass_utils import run_bass_kernel_spmd

F32 = mybir.dt.float32
BF16 = mybir.dt.bfloat16
AF = mybir.ActivationFunctionType
ALU = mybir.AluOpType

D = 1024
L = 4096
S = 2
NCORES = 8
NT = L // 512
EVEN_IN = 7200


class Buf:
    __slots__ = ("w", "r")

    def __init__(self):
        self.w = None
        self.r = []


class Eng:
    def __init__(self, name, h, sem, inc):
        self.name = name
        self.h = h
        self.sem = sem
        self.inc = inc
        self.cnt = 0
        self.waited = {}

    def wait(self, dep):
        e, v = dep
        if e is self and self.name == "pe":
            return
        if self.waited.get(e.name, 0) >= v:
            return
        self.h.wait_ge(e.sem, v)
        self.waited[e.name] = v


class FW:
    NSLOT = 24

    def __init__(self, nc, es):
        self.nc = nc

        def mk(name, h, inc):
            return Eng(name, h, es.enter_context(nc.semaphore("s_" + name)), inc)

        self.pe = mk("pe", nc.tensor, 1)
        self.act = mk("act", nc.scalar, 1)
        self.dve = mk("dve", nc.vector, 1)
        self.pool = mk("pool", nc.gpsimd, 1)
        self.sp = Eng("sp", nc.sync, None, 0)
        self.slots = [mk("dq%d" % i, None, 16) for i in range(self.NSLOT)]
        self.slots_sw = [mk("dw%d" % i, None, 16) for i in range(8)]
        self.slot_i = 0
        self.slot_sw_i = 0
        self.engs = [self.pe, self.act, self.dve, self.pool]
        self.psum = []
        self.ps_i = 0
        self.pending = []
        self.pend_r = set()

    def _deps(self, eng, reads, writes):
        for b in reads:
            if b.w is not None:
                eng.wait(b.w)
        for b in writes:
            if b.w is not None:
                eng.wait(b.w)
            for d in b.r:
                eng.wait(d)

    def _mark(self, me, reads, writes):
        for b in reads:
            b.r = [x for x in b.r if x[0] is not me[0]] + [me]
        for b in writes:
            b.w = me
            b.r = []

    def _autoflush(self, writes):
        if self.pending and any(id(b) in self.pend_r for b in writes):
            self.flush()

    def flush(self):
        p, self.pending = self.pending, []
        self.pend_r = set()
        with self.nc.allow_non_contiguous_dma(reason="deferred store"):
            for (issuer, out, in_, reads, writes) in p:
                self.dma(issuer, out, in_, reads, writes)

    def op(self, eng, fn, reads=(), writes=()):
        self._autoflush(writes)
        self._deps(eng, reads, writes)
        ins = fn(eng.h)
        eng.cnt += 1
        ins.then_inc(eng.sem, 1)
        self._mark((eng, eng.cnt), reads, writes)

    def dma(self, issuer, out, in_, reads=(), writes=(), defer=False):
        if defer:
            self.pending.append((issuer, out, in_, list(reads), list(writes)))
            for b in reads:
                self.pend_r.add(id(b))
            return
        self._autoflush(writes)
        if issuer is self.pool:
            slot = self.slots_sw[self.slot_sw_i % len(self.slots_sw)]
            self.slot_sw_i += 1
        else:
            slot = self.slots[self.slot_i % self.NSLOT]
            self.slot_i += 1
        if slot.cnt:
            issuer.wait((slot, slot.cnt))
        self._deps(issuer, reads, writes)
        ins = issuer.h.dma_start(out=out, in_=in_)
        slot.cnt += 16
        ins.then_inc(slot.sem, 16)
        self._mark((slot, slot.cnt), reads, writes)

    def barrier(self):
        self.flush()
        toks = [(e, e.cnt) for e in self.engs if e.cnt] + [(s, s.cnt) for s in self.slots + self.slots_sw if s.cnt]
        for e in self.engs + [self.sp]:
            for t in toks:
                e.wait(t)

    def bank(self):
        b = self.psum[self.ps_i % len(self.psum)]
        self.ps_i += 1
        return b


class T:
    def __init__(self, t, nb=1):
        self.t = t
        self.b = [Buf() for _ in range(nb)]

    def __getitem__(self, k):
        return self.t[k]


DEBUG = {}


def build_program(layers=(0, 1, 2, 3)):
    nc = bass.Bass("TRN2", target_bir_lowering=False)

    def din(name, shape, dt=F32):
        return nc.dram_tensor(name, list(shape), dt, kind="ExternalInput").ap()

    def dscr(name, shape, dt=F32):
        return nc.dram_tensor(name, list(shape), dt, kind="Internal").ap()

    xin = din("xin", [S, L, D])
    cin = din("cin", [S, D])
    mod_w = din("mod_w", [4, D, 3 * D])
    mod_b = din("mod_b", [4, 3 * D])
    norm_g = din("norm_g", [4, D])
    final_g = din("final_g", [D])
    od_w_in = din("od_w_in", [2, D, 4096])
    od_w_out = din("od_w_out", [2, 2048, D])
    lru_conv_w = din("lru_conv_w", [2, 4, 2048])
    lru_conv_b = din("lru_conv_b", [2, 2048])
    lru_w_a = din("lru_w_a", [2, 2, 16, 128, 128])
    lru_b_a = din("lru_b_a", [2, 2, 2048])
    lru_w_x = din("lru_w_x", [2, 2, 16, 128, 128])
    lru_b_x = din("lru_b_x", [2, 2, 2048])
    lru_lam = din("lru_lam", [2, 2, 2048])
    ev_w_out = din("ev_w_out", [2, 2048, D])
    identF_d = din("identF_d", [128, 128])
    ev_w_in = din("ev_w_in", [2, D, EVEN_IN])
    hy_conv_w = din("hy_conv_w", [2, 3, 3072])
    hy_conv_b = din("hy_conv_b", [2, 3072])
    hy_fw1 = din("hy_fw1", [2, 33, 64])
    hy_fb1 = din("hy_fb1", [2, 64])
    hy_fw2 = din("hy_fw2", [2, 64, 64])
    hy_fb2 = din("hy_fb2", [2, 64])
    hy_fw3 = din("hy_fw3", [2, 64, 2048])
    hy_freq = din("hy_freq", [2, 64])
    hy_bias = din("hy_bias", [2, 1024])
    ssd_conv_w = din("ssd_conv_w", [2, 4, 2048])
    ssd_conv_b = din("ssd_conv_b", [2, 2048])
    ssd_dt_bias = din("ssd_dt_bias", [2, 32])
    ssd_A_log = din("ssd_A_log", [2, 32])
    ssd_D = din("ssd_D", [2, 16])
    ssd_norm_g = din("ssd_norm_g", [2, 1024])
    tabF = din("tabF", [32, 128, 2 * 32 * 128], BF16)
    tabI = din("tabI", [8, 32, 128, 2 * 512], BF16)
    zfeat = din("zfeat", [33, L])
    trow = din("trow", [L])
    deltas = din("deltas", [1024])
    mrow_d = din("mrow", [2, L])
    mask_d = din("maskd", [2, 128, 128])
    xs_tm = dscr("xs_tm", [L, 1024], BF16)
    B_tm = dscr("B_tm", [L, 512], BF16)
    B_fm = dscr("B_fm", [512, L], BF16)
    C_fm = dscr("C_fm", [512, L], BF16)
    z_tm = dscr("z_tm", [L, 1024], BF16)
    Wd = dscr("Wd", [64, L])
    ecd_d = dscr("ecd_d", [64 * 32])
    Yf = dscr("Yf", [L, 1024])
    vf_tm = dscr("vf_tm", [L, 1024], BF16)
    xg_fm = dscr("xg_fm", [1024, L], BF16)
    filt_tm = dscr("filt_tm", [2, L, 1024], BF16)
    Gsp = dscr("Gsp", [2, 32, 128, 2, 1024])
    Ysp = dscr("Ysp", [32, 128, 2, 1024], BF16)
    yout = nc.dram_tensor("yout", [S, L, D], F32, kind="ExternalOutput").ap()

    xres = dscr("xres", [S, D, L])
    if DEBUG.get("dump"):
        mixfm = nc.dram_tensor("mixfm", [2048, L], BF16, kind="ExternalOutput").ap()
        hn_dump = nc.dram_tensor("hn_dump", [128, 8, L], BF16, kind="ExternalOutput").ap()
    else:
        mixfm = dscr("mixfm", [2048, L], BF16)

    es = ExitStack()
    with es:
        fw = FW(nc, es)
        pe, act, dve, pool, sp = fw.pe, fw.act, fw.dve, fw.pool, fw.sp

        uid = [0]

        def sb(st, name, shape, dt=F32, nb=1):
            uid[0] += 1
            return T(st.enter_context(nc.sbuf_tensor("%s_%d" % (name, uid[0]), list(shape), dt)), nb)

        def ldvec(dst, src1d, bufs):
            with nc.allow_non_contiguous_dma(reason="tiny param loads"):
                fw.dma(sp, dst, src1d.rearrange("(h p) -> p h", p=128), writes=bufs)

        for i in range(8):
            fw.psum.append(T(es.enter_context(nc.psum_tensor("ps%d" % i, [128, 512], F32))))

        hn = sb(es, "hn", [128, 8, L], BF16, nb=NT)
        identF = sb(es, "identF", [128, 128])
        identB = sb(es, "identB", [128, 128], BF16)
        onesB = sb(es, "onesB", [128, 128], BF16)
        MOD = sb(es, "MOD", [128, 5, S, 3, 8])
        fw.dma(sp, identF[:], identF_d[:, :], writes=identF.b)
        fw.op(dve, lambda h: h.tensor_copy(out=identB[:], in_=identF[:]), reads=identF.b, writes=identB.b)
        fw.op(dve, lambda h: h.memset(onesB[:], 1.0), writes=onesB.b)

        with ExitStack() as st:
            cT = sb(st, "cT", [128, 8, S])
            csT = sb(st, "csT", [128, 8, S], BF16)
            mb = sb(st, "mb", [128, 4, 24])
            ng = sb(st, "ng", [128, 5, 8])
            mw = [sb(st, "mw%d" % i, [128, 8, 1536], BF16) for i in range(2)]
            mraw = sb(st, "mraw", [128, 24, S])
            for s_ in range(S):
                ldvec(cT[:, :, s_], cin[s_], cT.b)
            for i in range(4):
                ldvec(mb[:, i, :], mod_b[i], mb.b)
                ldvec(ng[:, i, :], norm_g[i], ng.b)
            ldvec(ng[:, 4, :], final_g, ng.b)
            fw.op(act, lambda h: h.activation(out=csT[:], in_=cT[:], func=AF.Silu), reads=cT.b, writes=csT.b)
            for i in range(4):
                ps = fw.bank()
                for half in range(2):
                    w = mw[half]
                    fw.dma(pool, w[:], mod_w[i].rearrange("(k p) c -> p k c", p=128)[:, :, half * 1536:(half + 1) * 1536],
                           writes=w.b)
                    for jj in range(12):
                        j = half * 12 + jj
                        for k in range(8):
                            fw.op(pe, lambda h, j=j, jj=jj, k=k, w=w, ps=ps: h.matmul(
                                ps[:, j * S:(j + 1) * S], lhsT=w[:, k, jj * 128:(jj + 1) * 128], rhs=csT[:, k, :],
                                start=(k == 0), stop=(k == 7)), reads=w.b + csT.b, writes=ps.b)
                fw.op(dve, lambda h, ps=ps, i=i: h.tensor_tensor(
                    out=mraw[:], in0=ps[:, 0:24 * S].rearrange("p (j s) -> p j s", s=S),
                    in1=mb[:, i, :].unsqueeze(2).to_broadcast([128, 24, S]), op=ALU.add),
                    reads=ps.b + mb.b, writes=mraw.b)
                for s in range(S):
                    fw.op(dve, lambda h, i=i, s=s: h.scalar_tensor_tensor(
                        out=MOD[:, i, s, 0, :], in0=mraw[:, 8:16, s], scalar=1.0, in1=ng[:, i, :],
                        op0=ALU.add, op1=ALU.mult), reads=mraw.b + ng.b, writes=MOD.b)
                    fw.op(dve, lambda h, i=i, s=s: h.tensor_copy(out=MOD[:, i, s, 1, :], in_=mraw[:, 0:8, s]),
                          reads=mraw.b, writes=MOD.b)
                    fw.op(dve, lambda h, i=i, s=s: h.tensor_copy(out=MOD[:, i, s, 2, :], in_=mraw[:, 16:24, s]),
                          reads=mraw.b, writes=MOD.b)
            for s in range(S):
                fw.op(dve, lambda h, s=s: h.tensor_copy(out=MOD[:, 4, s, 0, :], in_=ng[:, 4, :]), reads=ng.b, writes=MOD.b)
                fw.op(dve, lambda h, s=s: h.memset(MOD[:, 4, s, 1, :], 0.0), writes=MOD.b)
            fw.barrier()

        def norm_tile(st_bufs, xt, li, s, tt, out_fn):
            sq, rstd = st_bufs
            fw.op(act, lambda h: h.activation(out=sq[:], in_=xt[:], func=AF.Square), reads=xt.b, writes=sq.b)
            ps = fw.bank()
            for k in range(8):
                fw.op(pe, lambda h, k=k: h.matmul(ps[:], lhsT=onesB[:], rhs=sq[:, k, :], start=(k == 0), stop=(k == 7)),
                      reads=onesB.b + sq.b, writes=ps.b)
            fw.op(act, lambda h: h.activation(out=rstd[:], in_=ps[:], func=AF.Sqrt, scale=1.0 / D, bias=epsT[:]),
                  reads=ps.b + epsT.b, writes=rstd.b)
            fw.op(dve, lambda h: h.reciprocal(out=rstd[:], in_=rstd[:]), reads=rstd.b, writes=rstd.b)
            for k in range(8):
                out_fn(k, rstd)

        epsT = sb(es, "epsT", [128, 1])
        fw.op(dve, lambda h: h.memset(epsT[:], 1e-6), writes=epsT.b)

        xres_v = [xres[s].rearrange("(cb p) l -> p cb l", p=128) for s in range(S)]
        mix_v = mixfm.rearrange("(kc p) l -> p kc l", p=128)

        with ExitStack() as st:
            xa = [sb(st, "xa%d" % i, [128, 4, D]) for i in range(2)]
            xf = [sb(st, "xf%d" % i, [128, 8, 512]) for i in range(2)]
            n = 0
            for s in range(S):
                for tt in range(NT):
                    a = xa[n % 2]
                    f = xf[n % 2]
                    n += 1
                    fw.dma(sp, a[:], xin[s, tt * 512:(tt + 1) * 512, :].rearrange("(a p) c -> p a c", p=128), writes=a.b)
                    fw.flush()
                    for cb in range(8):
                        ps = fw.bank()
                        for sub in range(4):
                            fw.op(pe, lambda h, sub=sub, cb=cb, a=a, ps=ps: h.transpose(
                                out=ps[:, sub * 128:(sub + 1) * 128], in_=a[:, sub, cb * 128:(cb + 1) * 128],
                                identity=identF[:]), reads=a.b + identF.b, writes=ps.b)
                        e = act if cb % 2 == 0 else dve
                        if e is act:
                            fw.op(act, lambda h, cb=cb, f=f, ps=ps: h.copy(out=f[:, cb, :], in_=ps[:]), reads=ps.b, writes=f.b)
                        else:
                            fw.op(dve, lambda h, cb=cb, f=f, ps=ps: h.tensor_copy(out=f[:, cb, :], in_=ps[:]), reads=ps.b, writes=f.b)
                    fw.dma(sp, xres_v[s][:, :, tt * 512:(tt + 1) * 512], f[:], reads=f.b, defer=True)
            fw.barrier()

        def phase_norm0(s, li):
            with ExitStack() as st:
                xt = [sb(st, "n0x%d" % i, [128, 8, 512]) for i in range(2)]
                sq = sb(st, "n0sq", [128, 8, 512], BF16)
                rstd = sb(st, "n0rs", [128, 512])
                tmp = sb(st, "n0tmp", [128, 512])
                for tt in range(NT):
                    x = xt[tt % 2]
                    fw.dma(sp, x[:], xres_v[s][:, :, tt * 512:(tt + 1) * 512], writes=x.b)

                    def out_fn(k, rstd, x=x, tt=tt):
                        fw.op(dve, lambda h: h.scalar_tensor_tensor(
                            out=tmp[:], in0=x[:, k, :], scalar=MOD[:, li, s, 0, k:k + 1], in1=rstd[:],
                            op0=ALU.mult, op1=ALU.mult), reads=x.b + rstd.b + MOD.b, writes=tmp.b)
                        fw.op(act, lambda h: h.activation(
                            out=hn[:, k, tt * 512:(tt + 1) * 512], in_=tmp[:], func=AF.Identity,
                            bias=MOD[:, li, s, 1, k:k + 1], scale=1.0), reads=tmp.b + MOD.b, writes=[hn.b[tt]])
                    norm_tile((sq, rstd), x, li, s, tt, out_fn)
                if DEBUG.get("dump"):
                    fw.dma(sp, hn_dump[:, :, :], hn[:], reads=hn.b, defer=True)
                fw.barrier()

        def phase_out(s, li, w_out_d, nli):
            final = (nli == 4)
            with ExitStack() as st:
                wo = sb(st, "wo", [128, 16, D], BF16)
                mt = [sb(st, "mt%d" % i, [128, 16, 512], BF16) for i in range(2)]
                xo = [sb(st, "xo%d" % i, [128, 8, 512]) for i in range(2)]
                sq = sb(st, "osq", [128, 8, 512], BF16)
                rstd = sb(st, "ors", [128, 512])
                tmp = sb(st, "otmp", [128, 512])
                ytm = [sb(st, "oytm%d" % i, [128, 4, D]) for i in range(1)] if final else None
                fw.dma(pool, wo[:], w_out_d.rearrange("(kc p) c -> p kc c", p=128), writes=wo.b)
                def stage1(tt):
                        m = mt[tt % 2]
                        xo_t = xo[tt % 2]
                        xn_t = xo_t
                        yfm = xo_t
                        sl = slice(tt * 512, (tt + 1) * 512)
                        fw.dma(sp, m[:], mix_v[:, :, sl], writes=m.b)
                        fw.dma(sp, xo_t[:], xres_v[s][:, :, sl], writes=xo_t.b)
                        fw.flush()
                        for ob in range(8):
                            ps = fw.bank()
                            for kc in range(16):
                                fw.op(pe, lambda h, ob=ob, kc=kc, ps=ps, m=m: h.matmul(
                                    ps[:], lhsT=wo[:, kc, ob * 128:(ob + 1) * 128], rhs=m[:, kc, :],
                                    start=(kc == 0), stop=(kc == 15)), reads=wo.b + m.b, writes=ps.b)
                            if DEBUG.get("skip_mix"):
                                continue
                            fw.op(dve, lambda h, ob=ob, ps=ps, xo_t=xo_t, xn_t=xn_t: h.scalar_tensor_tensor(
                                out=xn_t[:, ob, :], in0=ps[:], scalar=MOD[:, li, s, 2, ob:ob + 1], in1=xo_t[:, ob, :],
                                op0=ALU.mult, op1=ALU.add), reads=ps.b + xo_t.b + MOD.b, writes=xn_t.b)
                        if not final:
                            fw.dma(sp, xres_v[s][:, :, sl], xn_t[:], reads=xn_t.b, defer=True)


                def stage2(tt):
                        xo_t = xo[tt % 2]
                        xn_t = xo_t
                        yfm = xo_t
                        sl = slice(tt * 512, (tt + 1) * 512)
                        def out_fn(k, rstd, xn_t=xn_t, tt=tt):
                            fw.op(dve, lambda h: h.scalar_tensor_tensor(
                                out=tmp[:], in0=xn_t[:, k, :], scalar=MOD[:, nli, s, 0, k:k + 1], in1=rstd[:],
                                op0=ALU.mult, op1=ALU.mult), reads=xn_t.b + rstd.b + MOD.b, writes=tmp.b)
                            if final:
                                fw.op(act, lambda h: h.copy(out=yfm[:, k, :], in_=tmp[:]), reads=tmp.b, writes=yfm.b)
                            else:
                                fw.op(act, lambda h: h.activation(
                                    out=hn[:, k, tt * 512:(tt + 1) * 512], in_=tmp[:], func=AF.Identity,
                                    bias=MOD[:, nli, s, 1, k:k + 1], scale=1.0), reads=tmp.b + MOD.b, writes=[hn.b[tt]])
                        norm_tile((sq, rstd), xn_t, nli, s, tt, out_fn)
                        if final:
                            yt = ytm[0]
                            for sub in range(4):
                                for half in range(2):
                                    ps = fw.bank()
                                    for c4 in range(4):
                                        cb = half * 4 + c4
                                        fw.op(pe, lambda h, sub=sub, cb=cb, c4=c4, ps=ps: h.transpose(
                                            out=ps[:, c4 * 128:(c4 + 1) * 128], in_=yfm[:, cb, sub * 128:(sub + 1) * 128],
                                            identity=identF[:]), reads=yfm.b + identF.b, writes=ps.b)
                                    if half == 0:
                                        fw.op(act, lambda h, sub=sub, ps=ps, yt=yt: h.copy(out=yt[:, sub, 0:512], in_=ps[:]),
                                              reads=ps.b, writes=yt.b)
                                    else:
                                        fw.op(dve, lambda h, sub=sub, ps=ps, yt=yt: h.tensor_copy(out=yt[:, sub, 512:1024], in_=ps[:]),
                                              reads=ps.b, writes=yt.b)
                            fw.dma(sp, yout[s, sl, :].rearrange("(a p) c -> p a c", p=128), yt[:], reads=yt.b, defer=True)

                stage1(0)
                for tt in range(NT):
                    if tt + 1 < NT:
                        stage1(tt + 1)
                    stage2(tt)
                fw.barrier()

        def phase_odd(s, j):
            with ExitStack() as st:
                prm = sb(st, "oprm", [128, 16, 16])
                wbuf = [sb(st, "owb%d" % i, [128, 8, 2, 128], BF16) for i in range(2)]
                gw = [sb(st, "ogw%d" % i, [128, 2, 2, 128], BF16) for i in range(2)]
                xp = sb(st, "oxp", [128, L + 4])
                xc = [sb(st, "oxc%d" % i, [128, L], BF16) for i in range(2)]
                sg = [sb(st, "osg%d" % i, [128, L], BF16) for i in range(2)]
                af = sb(st, "oaf", [128, L])
                uf = sb(st, "ouf", [128, L])
                hf = sb(st, "ohf", [128, L])
                ym = [sb(st, "oym%d" % i, [128, L], BF16) for i in range(1)]
                t1 = sb(st, "ot1", [128, L])
                for k in range(4):
                    ldvec(prm[:, :, k], lru_conv_w[j, k], prm.b)
                ldvec(prm[:, :, 4], lru_conv_b[j], prm.b)
                for d in range(2):
                    ldvec(prm[:, :, 5 + d], lru_b_a[j, d], prm.b)
                    ldvec(prm[:, :, 7 + d], lru_b_x[j, d], prm.b)
                    ldvec(prm[:, :, 9 + d], lru_lam[j, d], prm.b)
                fw.op(act, lambda h: h.activation(out=prm[:, :, 11:13], in_=prm[:, :, 9:11], func=AF.Exp, scale=-1.0),
                      reads=prm.b, writes=prm.b)
                fw.op(act, lambda h: h.activation(out=prm[:, :, 11:13], in_=prm[:, :, 11:13], func=AF.Ln, bias=1.0, scale=1.0),
                      reads=prm.b, writes=prm.b)
                fw.op(dve, lambda h: h.tensor_scalar(out=prm[:, :, 9:11], in0=prm[:, :, 11:13], scalar1=-8.0, scalar2=None, op0=ALU.mult),
                      reads=prm.b, writes=prm.b)
                fw.op(dve, lambda h: h.tensor_scalar(out=prm[:, :, 11:13], in0=prm[:, :, 9:11], scalar1=2.0, scalar2=None, op0=ALU.mult),
                      reads=prm.b, writes=prm.b)
                fw.op(dve, lambda h: h.memset(xp[:, 0:2], 0.0), writes=xp.b)
                fw.op(dve, lambda h: h.memset(xp[:, L + 2:L + 4], 0.0), writes=xp.b)
                w_in_v = od_w_in[j].rearrange("(k p) c -> p k c", p=128)
                def load_w(hb_):
                    wb_ = wbuf[hb_ % 2]
                    fw.dma(pool, wb_[:, :, 0, :], w_in_v[:, :, hb_ * 128:(hb_ + 1) * 128], writes=wb_.b)
                    fw.dma(pool, wb_[:, :, 1, :], w_in_v[:, :, 2048 + hb_ * 128:2048 + (hb_ + 1) * 128], writes=wb_.b)

                def load_gw(hb_):
                    g_ = gw[hb_ % 2]
                    fw.dma(pool, g_[:, :, 0, :], lru_w_a[j, :, hb_].rearrange("d i o -> i d o"), writes=g_.b)
                    fw.dma(pool, g_[:, :, 1, :], lru_w_x[j, :, hb_].rearrange("d i o -> i d o"), writes=g_.b)

                def stage_a(hb):
                    wb = wbuf[hb % 2]
                    xc_ = xc[hb % 2]
                    sg_ = sg[hb % 2]
                    for tt in range(NT):
                        sl = slice(tt * 512, (tt + 1) * 512)
                        ps = fw.bank()
                        for k in range(8):
                            fw.op(pe, lambda h, k=k, ps=ps, sl=sl: h.matmul(ps[:], lhsT=wb[:, k, 0, :], rhs=hn[:, k, sl],
                                                                         start=(k == 0), stop=(k == 7)),
                                  reads=wb.b + [hn.b[tt]], writes=ps.b)
                        fw.op(act, lambda h, ps=ps, tt=tt: h.copy(out=xp[:, 2 + tt * 512:2 + (tt + 1) * 512], in_=ps[:]),
                              reads=ps.b, writes=xp.b)
                        ps2 = fw.bank()
                        for k in range(8):
                            fw.op(pe, lambda h, k=k, ps2=ps2, sl=sl: h.matmul(ps2[:], lhsT=wb[:, k, 1, :], rhs=hn[:, k, sl],
                                                                           start=(k == 0), stop=(k == 7)),
                                  reads=wb.b + [hn.b[tt]], writes=ps2.b)
                        fw.op(act, lambda h, ps2=ps2, sl=sl: h.activation(out=sg_[:, sl], in_=ps2[:], func=AF.Silu),
                              reads=ps2.b, writes=sg_.b)
                    if hb + 1 < 16:
                        load_w(hb + 1)
                    fw.op(act, lambda h: h.activation(out=hf[:], in_=xp[:, 0:L], func=AF.Identity,
                                                      scale=prm[:, hb, 0:1], bias=prm[:, hb, 4:5]),
                          reads=xp.b + prm.b, writes=hf.b)
                    for k in (1, 2):
                        fw.op(dve, lambda h, k=k: h.scalar_tensor_tensor(out=hf[:], in0=xp[:, k:k + L], scalar=prm[:, hb, k:k + 1],
                                                                        in1=hf[:], op0=ALU.mult, op1=ALU.add),
                              reads=xp.b + prm.b + hf.b, writes=hf.b)
                    fw.op(dve, lambda h: h.scalar_tensor_tensor(out=xc_[:], in0=xp[:, 3:3 + L], scalar=prm[:, hb, 3:4],
                                                                in1=hf[:], op0=ALU.mult, op1=ALU.add),
                          reads=xp.b + prm.b + hf.b, writes=xc_.b)

                def stage_b(hb):
                    g = gw[hb % 2]
                    xc_ = xc[hb % 2]
                    sg_ = sg[hb % 2]
                    y = ym[0]
                    for d in range(2):
                        for tt in range(NT):
                            sl = slice(tt * 512, (tt + 1) * 512)
                            psa = fw.bank()
                            fw.op(pe, lambda h, psa=psa, sl=sl: h.matmul(psa[:], lhsT=g[:, d, 0, :], rhs=xc_[:, sl], start=True, stop=True),
                                  reads=g.b + xc_.b, writes=psa.b)
                            psx = fw.bank()
                            fw.op(pe, lambda h, psx=psx, sl=sl: h.matmul(psx[:], lhsT=g[:, d, 1, :], rhs=xc_[:, sl], start=True, stop=True),
                                  reads=g.b + xc_.b, writes=psx.b)
                            fw.op(act, lambda h, psa=psa, sl=sl: h.activation(out=t1[:, sl], in_=psa[:], func=AF.Sigmoid,
                                                                             bias=prm[:, hb, 5 + d:6 + d], scale=1.0),
                                  reads=psa.b + prm.b, writes=t1.b)
                            fw.op(act, lambda h, psx=psx, sl=sl: h.activation(out=uf[:, sl], in_=psx[:], func=AF.Sigmoid,
                                                                             bias=prm[:, hb, 7 + d:8 + d], scale=1.0),
                                  reads=psx.b + prm.b, writes=uf.b)
                        if d == 1 and hb + 1 < 16:
                            load_gw(hb + 1)
                        fw.op(act, lambda h: h.activation(out=af[:], in_=t1[:], func=AF.Exp, scale=prm[:, hb, 9 + d:10 + d]),
                              reads=t1.b + prm.b, writes=af.b)
                        fw.op(act, lambda h: h.activation(out=t1[:], in_=t1[:], func=AF.Exp, scale=prm[:, hb, 11 + d:12 + d]),
                              reads=t1.b + prm.b, writes=t1.b)
                        fw.op(act, lambda h: h.activation(out=t1[:], in_=t1[:], func=AF.Sqrt, scale=-1.0, bias=oneT[:]),
                              reads=t1.b + oneT.b, writes=t1.b)
                        fw.op(dve, lambda h: h.tensor_tensor(out=uf[:], in0=uf[:], in1=xc_[:], op=ALU.mult),
                              reads=uf.b + xc_.b, writes=uf.b)
                        fw.op(dve, lambda h: h.tensor_tensor(out=uf[:], in0=uf[:], in1=t1[:], op=ALU.mult),
                              reads=uf.b + t1.b, writes=uf.b)
                        if d == 0:
                            fw.op(dve, lambda h: h.tensor_tensor_scan(out=hf[:], data0=af[:], data1=uf[:], initial=0.0,
                                                                      op0=ALU.mult, op1=ALU.add),
                                  reads=af.b + uf.b, writes=hf.b)
                        else:
                            fw.op(dve, lambda h: h.tensor_tensor_scan(out=t1[:, ::-1], data0=af[:, ::-1], data1=uf[:, ::-1],
                                                                      initial=0.0, op0=ALU.mult, op1=ALU.add),
                                  reads=af.b + uf.b, writes=t1.b)
                    fw.op(dve, lambda h: h.tensor_tensor(out=hf[:], in0=hf[:], in1=t1[:], op=ALU.add),
                          reads=hf.b + t1.b, writes=hf.b)
                    fw.op(dve, lambda h: h.tensor_tensor(out=y[:], in0=hf[:], in1=sg_[:], op=ALU.mult),
                          reads=hf.b + sg_.b, writes=y.b)
                    fw.dma(sp, mixfm[hb * 128:(hb + 1) * 128, :], y[:], reads=y.b, defer=True)

                load_w(0)
                load_gw(0)
                stage_a(0)
                for hb in range(16):
                    if hb + 1 < 16:
                        stage_a(hb + 1)
                    stage_b(hb)
                    fw.flush()
                fw.barrier()


        def conv_fm(xp, K, prm_ap, out_t, tmp):
            fw.op(act, lambda h: h.activation(out=tmp[:], in_=xp[:, 0:L], func=AF.Identity, scale=prm_ap(0), bias=prm_ap(K)),
                  reads=xp.b + PRM.b, writes=tmp.b)
            for k in range(1, K):
                o = out_t if k == K - 1 else tmp
                fw.op(dve, lambda h, k=k, o=o: h.scalar_tensor_tensor(out=o[:], in0=xp[:, k:k + L], scalar=prm_ap(k), in1=tmp[:],
                                                                     op0=ALU.mult, op1=ALU.add),
                      reads=xp.b + PRM.b + tmp.b, writes=o.b)

        WLOADED = {}
        PEND_T = []

        def proj_fm(w_v, col0, wb, evac, nxt=None):
            if WLOADED.get(id(wb)) != (id(w_v), col0):
                fw.dma(pool, wb[:], w_v[:, :, col0:col0 + 128], writes=wb.b)
            WLOADED.pop(id(wb), None)
            _proj_body(wb, evac)
            while PEND_T:
                transpose_to_tm(*PEND_T.pop(0))
            if nxt is not None:
                ncol, nwb = nxt
                fw.dma(pool, nwb[:], w_v[:, :, ncol:ncol + 128], writes=nwb.b)
                WLOADED[id(nwb)] = (id(w_v), ncol)

        def _proj_body(wb, evac):
            for tt in range(NT):
                ps = fw.bank()
                for k in range(8):
                    fw.op(pe, lambda h, k=k, ps=ps, tt=tt: h.matmul(ps[:], lhsT=wb[:, k, :], rhs=hn[:, k, tt * 512:(tt + 1) * 512],
                                                                  start=(k == 0), stop=(k == 7)),
                          reads=wb.b + [hn.b[tt]], writes=ps.b)
                evac(tt, ps)

        def transpose_to_tm(src, dst_v):
            for q in range(4):
                stg = TSTG[q % 2]
                for t8 in range(2):
                    ps = fw.bank()
                    psb = ps[:].bitcast(BF16)
                    for i in range(4):
                        t = q * 8 + t8 * 4 + i
                        fw.op(pe, lambda h, i=i, t=t, psb=psb: h.transpose(out=psb[:, i * 128:(i + 1) * 128],
                                                                         in_=src[:, t * 128:(t + 1) * 128], identity=identB[:]),
                              reads=src.b + identB.b, writes=ps.b)
                    fw.op(act, lambda h, t8=t8, psb=psb, stg=stg: h.copy(
                        out=stg[:, t8 * 4:(t8 + 1) * 4, :], in_=psb[:, 0:512].rearrange("p (i c) -> p i c", c=128)),
                        reads=ps.b, writes=stg.b)
                with nc.allow_non_contiguous_dma(reason="256B runs"):
                    fw.dma(sp, dst_v[:, q * 8:(q + 1) * 8, :], stg[:], reads=stg.b, defer=True)

        def dft_forward(src_c_v, src_s_v, c0, ncol, consume):
            vc = DFT_SRC[0]
            fw.dma(sp, vc[:, :, 0:ncol], src_c_v[:, :, c0:c0 + ncol], writes=vc.b)
            if src_s_v is not None:
                vs = DFT_SRC[1]
                fw.dma(sp, vs[:, :, 0:ncol], src_s_v[:, :, c0:c0 + ncol], writes=vs.b)
            else:
                vs = vc
            for kb in range(32):
                tb = DFT_TAB[kb % 2]
                fw.dma(sp, tb[:], tabF[kb], writes=tb.b)
                fw.flush()
                tv = tb[:].rearrange("p (cs t k) -> p cs t k", cs=2, t=32)
                for g0 in range(0, ncol, 512):
                    gw_ = min(512, ncol - g0)
                    pc = fw.bank()
                    pss = fw.bank()
                    for t in range(32):
                        fw.op(pe, lambda h, t=t, pc=pc, tv=tv, g0=g0, gw_=gw_: h.matmul(pc[:, 0:gw_], lhsT=tv[:, 0, t, :], rhs=vc[:, t, g0:g0 + gw_],
                                                                                      start=(t == 0), stop=(t == 31)),
                              reads=tb.b + vc.b, writes=pc.b)
                    for t in range(32):
                        fw.op(pe, lambda h, t=t, pss=pss, tv=tv, g0=g0, gw_=gw_: h.matmul(pss[:, 0:gw_], lhsT=tv[:, 1, t, :], rhs=vs[:, t, g0:g0 + gw_],
                                                                                        start=(t == 0), stop=(t == 31)),
                              reads=tb.b + vs.b, writes=pss.b)
                    consume(kb, g0, gw_, pc, pss)

        def sin_big(out_t, ps, n, f4, f8, b4, b8, s4, s8):
            fw.op(act, lambda h: h.activation(out=s4[0:64, 0:n], in_=ps[0:64, 0:n], func=AF.Sin, scale=f4, bias=b4),
                  reads=ps.b + PRM.b, writes=s4.b)
            fw.op(act, lambda h: h.activation(out=s8[0:64, 0:n], in_=ps[0:64, 0:n], func=AF.Sin, scale=f8, bias=b8),
                  reads=ps.b + PRM.b, writes=s8.b)
            fw.op(dve, lambda h: h.tensor_tensor(out=s8[0:64, 0:n], in0=s8[0:64, 0:n], in1=s8[0:64, 0:n], op=ALU.mult), reads=s8.b, writes=s8.b)
            fw.op(dve, lambda h: h.tensor_scalar(out=s8[0:64, 0:n], in0=s8[0:64, 0:n], scalar1=-8.0, scalar2=4.0, op0=ALU.mult, op1=ALU.add),
                  reads=s8.b, writes=s8.b)
            fw.op(dve, lambda h: h.tensor_tensor(out=s8[0:64, 0:n], in0=s8[0:64, 0:n], in1=s4[0:64, 0:n], op=ALU.mult), reads=s8.b + s4.b, writes=s8.b)
            fw.op(dve, lambda h: h.tensor_tensor(out=s4[0:64, 0:n], in0=s4[0:64, 0:n], in1=s4[0:64, 0:n], op=ALU.mult), reads=s4.b, writes=s4.b)
            fw.op(dve, lambda h: h.tensor_scalar(out=s4[0:64, 0:n], in0=s4[0:64, 0:n], scalar1=-2.0, scalar2=1.0, op0=ALU.mult, op1=ALU.add),
                  reads=s4.b, writes=s4.b)
            fw.op(dve, lambda h: h.tensor_tensor(out=out_t, in0=s8[0:64, 0:n], in1=s4[0:64, 0:n], op=ALU.mult), reads=s8.b + s4.b, writes=out_bufs[0])

        out_bufs = [None]
        PRM = sb(es, "PRM", [128, 64, 8])
        TSTG = [None, None]
        DFT_SRC = [None, None]
        DFT_TAB = [None, None]

        def phase_filter(j):
            with ExitStack() as st:
                zf = sb(st, "fz", [33, L])
                w1 = sb(st, "fw1", [33, 64])
                w2 = sb(st, "fw2", [64, 64])
                w3 = sb(st, "fw3", [64, 2048])
                h1 = sb(st, "fh1", [64, L])
                h2 = sb(st, "fh2", [64, L])
                s4 = sb(st, "fs4", [64, 512])
                s8 = sb(st, "fs8", [64, 512])
                win = sb(st, "fwin", [128, L])
                hfw = sb(st, "fhf", [128, L])
                hbw = sb(st, "fhb", [128, L])
                hsum = sb(st, "fhs", [128, L], BF16)
                hdif = sb(st, "fhd", [128, L], BF16)
                nrm = sb(st, "fnrm", [128, 4])
                TSTG[0] = sb(st, "fst0", [128, 8, 128], BF16)
                TSTG[1] = sb(st, "fst1", [128, 8, 128], BF16)
                fw.dma(sp, zf[:], zfeat[:, :], writes=zf.b)
                fw.dma(sp, w1[:], hy_fw1[j], writes=w1.b)
                fw.dma(sp, w2[:], hy_fw2[j], writes=w2.b)
                fw.dma(sp, w3[:], hy_fw3[j], writes=w3.b)
                with nc.allow_non_contiguous_dma(reason="tiny"):
                    fw.dma(sp, PRM[0:64, 0, 0:1], hy_freq[j].rearrange("(p o) -> p o", o=1), writes=PRM.b)
                    fw.dma(sp, PRM[0:64, 0, 1:2], hy_fb1[j].rearrange("(p o) -> p o", o=1), writes=PRM.b)
                    fw.dma(sp, PRM[0:64, 0, 2:3], hy_fb2[j].rearrange("(p o) -> p o", o=1), writes=PRM.b)
                    fw.dma(sp, PRM[:, 2:10, 0], deltas.rearrange("(b p) -> p b", p=128), writes=PRM.b)
                P = lambda a, b: PRM[0:64, a, b:b + 1]
                fw.op(dve, lambda h: h.tensor_scalar(out=P(1, 0), in0=P(0, 0), scalar1=0.25, scalar2=None, op0=ALU.mult), reads=PRM.b, writes=PRM.b)
                fw.op(dve, lambda h: h.tensor_scalar(out=P(1, 1), in0=P(0, 0), scalar1=0.125, scalar2=None, op0=ALU.mult), reads=PRM.b, writes=PRM.b)
                for (bi, o) in ((1, 2), (2, 4)):
                    fw.op(dve, lambda h, bi=bi, o=o: h.tensor_tensor(out=P(1, o), in0=P(0, bi), in1=P(1, 0), op=ALU.mult), reads=PRM.b, writes=PRM.b)
                    fw.op(dve, lambda h, bi=bi, o=o: h.tensor_tensor(out=P(1, o + 1), in0=P(0, bi), in1=P(1, 1), op=ALU.mult), reads=PRM.b, writes=PRM.b)
                fw.op(dve, lambda h: h.tensor_scalar(out=PRM[:, 2:10, 1], in0=PRM[:, 2:10, 0], scalar1=-1.0, scalar2=None, op0=ALU.mult),
                      reads=PRM.b, writes=PRM.b)
                for tt in range(NT):
                    sl = slice(tt * 512, (tt + 1) * 512)
                    ps = fw.bank()
                    fw.op(pe, lambda h, ps=ps, sl=sl: h.matmul(ps[0:64, :], lhsT=w1[:], rhs=zf[:, sl], start=True, stop=True),
                          reads=w1.b + zf.b, writes=ps.b)
                    out_bufs[0] = h1.b
                    sin_big(h1[:, sl], ps, 512, P(1, 0), P(1, 1), P(1, 2), P(1, 3), s4, s8)
                for tt in range(NT):
                    sl = slice(tt * 512, (tt + 1) * 512)
                    ps = fw.bank()
                    fw.op(pe, lambda h, ps=ps, sl=sl: h.matmul(ps[0:64, :], lhsT=w2[:], rhs=h1[:, sl], start=True, stop=True),
                          reads=w2.b + h1.b, writes=ps.b)
                    out_bufs[0] = h2.b
                    sin_big(h2[:, sl], ps, 512, P(1, 0), P(1, 1), P(1, 4), P(1, 5), s4, s8)
                filt_v = [filt_tm[i].rearrange("(t p) c -> p t c", p=128) for i in range(2)]
                for b in range(8):
                    fw.dma(sp, win[:], trow.partition_broadcast(128), writes=win.b)
                    fw.op(act, lambda h, b=b: h.activation(out=win[:], in_=win[:], func=AF.Exp, scale=PRM[:, 2 + b, 1:2]),
                          reads=win.b + PRM.b, writes=win.b)
                    for (half, dst) in ((0, hfw), (1, hbw)):
                        for tt in range(NT):
                            sl = slice(tt * 512, (tt + 1) * 512)
                            ps = fw.bank()
                            c0 = half * 1024 + b * 128
                            fw.op(pe, lambda h, ps=ps, sl=sl, c0=c0: h.matmul(ps[:], lhsT=w3[:, c0:c0 + 128], rhs=h2[:, sl], start=True, stop=True),
                                  reads=w3.b + h2.b, writes=ps.b)
                            fw.op(dve, lambda h, ps=ps, sl=sl, dst=dst: h.tensor_tensor(out=dst[:, sl], in0=ps[:], in1=win[:, sl], op=ALU.mult),
                                  reads=ps.b + win.b, writes=dst.b)
                    fw.op(dve, lambda h: h.memset(hbw[:, 0:1], 0.0), writes=hbw.b)
                    fw.op(dve, lambda h: h.tensor_reduce(out=nrm[:, 0:1], in_=hfw[:], axis=mybir.AxisListType.X, op=ALU.add, apply_absolute_value=True),
                          reads=hfw.b, writes=nrm.b)
                    fw.op(dve, lambda h: h.tensor_reduce(out=nrm[:, 1:2], in_=hbw[:], axis=mybir.AxisListType.X, op=ALU.add, apply_absolute_value=True),
                          reads=hbw.b, writes=nrm.b)
                    fw.op(dve, lambda h: h.tensor_tensor(out=nrm[:, 2:3], in0=nrm[:, 0:1], in1=nrm[:, 1:2], op=ALU.add), reads=nrm.b, writes=nrm.b)
                    fw.op(dve, lambda h: h.reciprocal(out=nrm[:, 3:4], in_=nrm[:, 2:3]), reads=nrm.b, writes=nrm.b)
                    fw.op(dve, lambda h: h.tensor_tensor(out=win[:], in0=hfw[:], in1=hbw[:], op=ALU.add), reads=hfw.b + hbw.b, writes=win.b)
                    fw.op(act, lambda h: h.activation(out=hsum[:], in_=win[:], func=AF.Identity, scale=nrm[:, 3:4]), reads=win.b + nrm.b, writes=hsum.b)
                    fw.op(dve, lambda h: h.tensor_tensor(out=hfw[:], in0=hfw[:], in1=hbw[:], op=ALU.subtract), reads=hfw.b + hbw.b, writes=hfw.b)
                    fw.op(act, lambda h: h.activation(out=hdif[:], in_=hfw[:], func=AF.Identity, scale=nrm[:, 3:4]), reads=hfw.b + nrm.b, writes=hdif.b)
                    transpose_to_tm(hsum, filt_v[0][:, :, b * 128:(b + 1) * 128])
                    transpose_to_tm(hdif, filt_v[1][:, :, b * 128:(b + 1) * 128])
                fw.barrier()
            with ExitStack() as st:
                DFT_SRC[0] = sb(st, "gsrc0", [128, 32, 512], BF16)
                DFT_SRC[1] = sb(st, "gsrc1", [128, 32, 512], BF16)
                DFT_TAB[0] = sb(st, "gtab0", [128, 2 * 32 * 128], BF16)
                DFT_TAB[1] = sb(st, "gtab1", [128, 2 * 32 * 128], BF16)
                brow = sb(st, "gbrow", [128, 1024])
                go = [sb(st, "gout%d" % i, [128, 2, 512]) for i in range(2)]
                fw.dma(sp, brow[:], hy_bias[j].partition_broadcast(128), writes=brow.b)
                filt_v = [filt_tm[i].rearrange("(t p) c -> p t c", p=128) for i in range(2)]
                for half in range(2):
                    c0 = half * 512

                    def consume(kb, g0, gw_, pc, pss, c0=c0):
                        g = go[kb % 2]
                        fw.op(dve, lambda h: h.tensor_tensor(out=g[:, 0, :], in0=pc[:], in1=brow[:, c0:c0 + 512], op=ALU.add),
                              reads=pc.b + brow.b, writes=g.b)
                        fw.op(act, lambda h: h.copy(out=g[:, 1, :], in_=pss[:]), reads=pss.b, writes=g.b)
                        fw.dma(sp, Gsp[j, kb, :, :, c0:c0 + 512], g[:], reads=g.b, defer=True)
                    dft_forward(filt_v[0], filt_v[1], c0, 512, consume)
                fw.barrier()

        def phase_hyena(s, j):
            w_v = ev_w_in[j].rearrange("(k p) c -> p k c", p=128)
            vf_v = vf_tm.rearrange("(t p) c -> p t c", p=128)
            WLOADED.clear()
            with ExitStack() as st:
                wb = [sb(st, "hwb%d" % i, [128, 8, 128], BF16) for i in range(2)]
                xp = [sb(st, "hxp%d" % i, [128, L + 2]) for i in range(2)]
                tmp = sb(st, "htmp", [128, L])
                c0t = sb(st, "hc0", [128, L])
                c1t = sb(st, "hc1", [128, L])
                sg = sb(st, "hsg", [128, L], BF16)
                ob = [sb(st, "hob%d" % i, [128, L], BF16) for i in range(3)]
                TSTG[0] = sb(st, "hst0", [128, 8, 128], BF16)
                TSTG[1] = sb(st, "hst1", [128, 8, 128], BF16)
                for k in range(3):
                    ldvec(PRM[:, 0:24, k], hy_conv_w[j, k], PRM.b)
                ldvec(PRM[:, 0:24, 3], hy_conv_b[j], PRM.b)
                for x in xp:
                    fw.op(dve, lambda h, x=x: h.memset(x[:, 0:1], 0.0), writes=x.b)
                    fw.op(dve, lambda h, x=x: h.memset(x[:, L + 1:L + 2], 0.0), writes=x.b)
                seq = []
                for b in range(8):
                    seq += [b * 128, 3072 + b * 128, (8 + b) * 128, (16 + b) * 128]
                n = 0

                def nxt_():
                    return (seq[n + 1], wb[(n + 1) % 2]) if n + 1 < len(seq) else None

                def conv_block(blk, out_t):
                    nonlocal n
                    x = xp[n % 2]
                    w = wb[n % 2]
                    nx = nxt_()
                    n += 1
                    proj_fm(w_v, blk * 128, w, lambda tt, ps, x=x: fw.op(
                        act, lambda h: h.copy(out=x[:, 1 + tt * 512:1 + (tt + 1) * 512], in_=ps[:]), reads=ps.b, writes=x.b), nx)
                    conv_fm(x, 3, lambda k, blk=blk: PRM[:, blk, k:k + 1], out_t, tmp)

                for b in range(8):
                    conv_block(b, c0t)
                    w = wb[n % 2]
                    nx = nxt_()
                    n += 1
                    proj_fm(w_v, 3072 + b * 128, w, lambda tt, ps: fw.op(
                        act, lambda h: h.activation(out=sg[:, tt * 512:(tt + 1) * 512], in_=ps[:], func=AF.Silu), reads=ps.b, writes=sg.b), nx)
                    o = ob[0]
                    fw.op(dve, lambda h, o=o: h.tensor_tensor(out=o[:], in0=c0t[:], in1=sg[:], op=ALU.mult), reads=c0t.b + sg.b, writes=o.b)
                    fw.dma(sp, xg_fm[b * 128:(b + 1) * 128, :], o[:], reads=o.b, defer=True)
                    conv_block(8 + b, c0t)
                    conv_block(16 + b, c1t)
                    o = ob[1 + b % 2]
                    fw.op(dve, lambda h, o=o: h.tensor_tensor(out=o[:], in0=c0t[:], in1=c1t[:], op=ALU.mult), reads=c0t.b + c1t.b, writes=o.b)
                    PEND_T.append((o, vf_v[:, :, b * 128:(b + 1) * 128]))
                while PEND_T:
                    transpose_to_tm(*PEND_T.pop(0))
                fw.barrier()

        def phase_hyena_dft(s, j):
            vf_v = vf_tm.rearrange("(t p) c -> p t c", p=128)
            with ExitStack() as st:
                DFT_SRC[0] = sb(st, "dsrc0", [128, 32, 1024], BF16)
                DFT_TAB[0] = sb(st, "dtab0", [128, 2 * 32 * 128], BF16)
                DFT_TAB[1] = sb(st, "dtab1", [128, 2 * 32 * 128], BF16)
                Yt = [sb(st, "dY%d" % i, [128, 2, 1024], BF16) for i in range(2)]
                gt = [sb(st, "dg%d" % i, [128, 2, 1024]) for i in range(2)]
                t1 = sb(st, "dt1", [128, 512])
                t2 = sb(st, "dt2", [128, 512])

                def consume(kb, g0, gw_, pc, pss):
                    g = gt[kb % 2]
                    Y = Yt[kb % 2]
                    cs_ = slice(g0, g0 + gw_)
                    if g0 == 0:
                        fw.dma(sp, g[:], Gsp[j, kb], writes=g.b)
                    fw.op(dve, lambda h: h.tensor_tensor(out=t1[:, 0:gw_], in0=pc[:, 0:gw_], in1=g[:, 0, cs_], op=ALU.mult), reads=pc.b + g.b, writes=t1.b)
                    fw.op(dve, lambda h: h.tensor_tensor(out=t2[:, 0:gw_], in0=pss[:, 0:gw_], in1=g[:, 1, cs_], op=ALU.mult), reads=pss.b + g.b, writes=t2.b)
                    fw.op(pool, lambda h: h.tensor_tensor(out=Y[:, 0, cs_], in0=t1[:, 0:gw_], in1=t2[:, 0:gw_], op=ALU.subtract), reads=t1.b + t2.b, writes=Y.b)
                    fw.op(dve, lambda h: h.tensor_tensor(out=t1[:, 0:gw_], in0=pc[:, 0:gw_], in1=g[:, 1, cs_], op=ALU.mult), reads=pc.b + g.b, writes=t1.b)
                    fw.op(dve, lambda h: h.tensor_tensor(out=t2[:, 0:gw_], in0=pss[:, 0:gw_], in1=g[:, 0, cs_], op=ALU.mult), reads=pss.b + g.b, writes=t2.b)
                    fw.op(pool, lambda h: h.tensor_tensor(out=Y[:, 1, cs_], in0=t1[:, 0:gw_], in1=t2[:, 0:gw_], op=ALU.add), reads=t1.b + t2.b, writes=Y.b)
                    if g0 + gw_ == 1024:
                        fw.dma(sp, Ysp[kb], Y[:], reads=Y.b, defer=True)
                dft_forward(vf_v, None, 0, 1024, consume)
                fw.barrier()
            with ExitStack() as st:
                Yh = sb(st, "dYh", [128, 32, 2, 512], BF16)
                ti = [sb(st, "dti%d" % i, [128, 2, 512], BF16) for i in range(3)]
                xg = [sb(st, "dxg%d" % i, [128, 4, 512], BF16) for i in range(2)]
                mo = [sb(st, "dmo%d" % i, [128, 4, 512], BF16) for i in range(2)]
                for half in range(2):
                    c0 = half * 512
                    for cs in range(2):
                        fw.dma(sp, Yh[:, :, cs, :], Ysp[:, :, cs, c0:c0 + 512].rearrange("k p c -> p k c"), writes=Yh.b)
                    for nq in range(8):
                        banks = [fw.bank() for _ in range(4)]
                        xgt = xg[nq % 2]
                        fw.dma(sp, xgt[:], xg_fm[c0:c0 + 512, nq * 512:(nq + 1) * 512].rearrange("(b p) n -> p b n", p=128), writes=xgt.b)
                        for kb in range(32):
                            t = ti[kb % 3]
                            fw.dma(sp, t[:], tabI[nq, kb].rearrange("p (cs n) -> p cs n", cs=2), writes=t.b)
                            if kb == 2:
                                fw.flush()
                            for cb in range(4):
                                for cs in range(2):
                                    fw.op(pe, lambda h, cb=cb, cs=cs, kb=kb, t=t, banks=banks: h.matmul(
                                        banks[cb][:], lhsT=Yh[:, kb, cs, cb * 128:(cb + 1) * 128], rhs=t[:, cs, :],
                                        start=(kb == 0 and cs == 0), stop=(kb == 31 and cs == 1)),
                                        reads=Yh.b + t.b, writes=banks[cb].b)
                        m = mo[nq % 2]
                        for cb in range(4):
                            fw.op(dve, lambda h, cb=cb, m=m, xgt=xgt, banks=banks: h.tensor_tensor(out=m[:, cb, :], in0=banks[cb][:], in1=xgt[:, cb, :], op=ALU.mult),
                                  reads=banks[cb].b + xgt.b, writes=m.b)
                        fw.dma(sp, mixfm[c0:c0 + 512, nq * 512:(nq + 1) * 512].rearrange("(b p) n -> p b n", p=128), m[:], reads=m.b, defer=True)
                fw.barrier()

        def phase_ssd(s, j):
            w_v = ev_w_in[j].rearrange("(k p) c -> p k c", p=128)
            xs_v = xs_tm.rearrange("(t p) c -> p t c", p=128)
            Bt_v = B_tm.rearrange("(t p) c -> p t c", p=128)
            WLOADED.clear()
            with ExitStack() as st0:
                TM = sb(st0, "sTM", [128, 32, 4, 64])
                ecd = sb(st0, "secd", [128, 64, 32])
                with ExitStack() as st:
                    wb = [sb(st, "swb%d" % i, [128, 8, 128], BF16) for i in range(2)]
                    xp = [sb(st, "sxp0", [128, L + 3])] * 2
                    tmp = sb(st, "stmp", [128, L])
                    c0t = sb(st, "sc0", [128, L])
                    ob = [sb(st, "sob%d" % i, [128, L], BF16) for i in range(2)]
                    wz = sb(st, "swz", [128, 8, 1024], BF16)
                    zt = [sb(st, "szt%d" % i, [128, 1024], BF16) for i in range(2)]
                    TSTG[0] = sb(st, "sst0", [128, 8, 128], BF16)
                    TSTG[1] = sb(st, "sst1", [128, 8, 128], BF16)
                    fw.dma(pool, wz[:], w_v[:, :, 4096:5120], writes=wz.b)
                    for k in range(4):
                        ldvec(PRM[:, 0:16, k], ssd_conv_w[j, k], PRM.b)
                    ldvec(PRM[:, 0:16, 4], ssd_conv_b[j], PRM.b)
                    for x in xp[:1]:
                        fw.op(dve, lambda h, x=x: h.memset(x[:, 0:2], 0.0), writes=x.b)
                        fw.op(dve, lambda h, x=x: h.memset(x[:, L + 2:L + 3], 0.0), writes=x.b)
                    for blk in range(16):
                        x = xp[blk % 2]
                        o = ob[blk % 2]
                        proj_fm(w_v, 5120 + blk * 128, wb[blk % 2], lambda tt, ps, x=x: fw.op(
                            act, lambda h: h.copy(out=x[:, 2 + tt * 512:2 + (tt + 1) * 512], in_=ps[:]), reads=ps.b, writes=x.b),
                            (5120 + (blk + 1) * 128, wb[(blk + 1) % 2]) if blk < 15 else None)
                        conv_fm(x, 4, lambda k, blk=blk: PRM[:, blk, k:k + 1], c0t, tmp)
                        fw.op(act, lambda h, o=o: h.activation(out=o[:], in_=c0t[:], func=AF.Silu), reads=c0t.b, writes=o.b)
                        if blk < 8:
                            PEND_T.append((o, xs_v[:, :, blk * 128:(blk + 1) * 128]))
                        elif blk < 12:
                            fw.dma(sp, B_fm[(blk - 8) * 128:(blk - 7) * 128, :], o[:], reads=o.b, defer=True)
                            PEND_T.append((o, Bt_v[:, :, (blk - 8) * 128:(blk - 7) * 128]))
                        else:
                            fw.dma(sp, C_fm[(blk - 12) * 128:(blk - 11) * 128, :], o[:], reads=o.b, defer=True)
                    while PEND_T:
                        transpose_to_tm(*PEND_T.pop(0))
                    for c in range(32):
                        z = zt[c % 2]
                        for half in range(2):
                            ps = fw.bank()
                            for k in range(8):
                                fw.op(pe, lambda h, k=k, ps=ps, c=c, half=half: h.matmul(
                                    ps[:], lhsT=hn[:, k, c * 128:(c + 1) * 128], rhs=wz[:, k, half * 512:(half + 1) * 512],
                                    start=(k == 0), stop=(k == 7)), reads=wz.b + [hn.b[c // 4]], writes=ps.b)
                            fw.op(act, lambda h, ps=ps, z=z, half=half: h.activation(out=z[:, half * 512:(half + 1) * 512], in_=ps[:], func=AF.Silu),
                                  reads=ps.b, writes=z.b)
                        fw.dma(sp, z_tm[c * 128:(c + 1) * 128, :], z[:], reads=z.b, defer=True)
                    fw.barrier()
                with ExitStack() as st:
                    wdt = sb(st, "swdt", [128, 8, 64], BF16)
                    dtf = sb(st, "sdtf", [64, L])
                    laf = sb(st, "slaf", [64, L])
                    Wt = sb(st, "sW", [64, L])
                    mk = sb(st, "smk", [64, L])
                    Tt = sb(st, "sT", [64, 32])
                    pp = sb(st, "spp", [64, 4])
                    fw.op(dve, lambda h: h.memset(wdt[:], 0.0), writes=wdt.b)
                    fw.op(dve, lambda h: h.memset(pp[:], 0.0), writes=pp.b)
                    for d in range(2):
                        fw.dma(pool, wdt[:, :, d * 32:d * 32 + 16], w_v[:, :, 7168 + d * 16:7168 + (d + 1) * 16], writes=wdt.b)
                        with nc.allow_non_contiguous_dma(reason="tiny"):
                            fw.dma(sp, pp[d * 32:d * 32 + 16, 0:1], ssd_dt_bias[j, d * 16:(d + 1) * 16].rearrange("(p o) -> p o", o=1), writes=pp.b)
                            fw.dma(sp, pp[d * 32:d * 32 + 16, 1:2], ssd_A_log[j, d * 16:(d + 1) * 16].rearrange("(p o) -> p o", o=1), writes=pp.b)
                    fw.op(act, lambda h: h.activation(out=pp[:, 2:3], in_=pp[:, 1:2], func=AF.Exp), reads=pp.b, writes=pp.b)
                    fw.op(dve, lambda h: h.tensor_scalar(out=pp[:, 2:3], in0=pp[:, 2:3], scalar1=-1.0, scalar2=None, op0=ALU.mult), reads=pp.b, writes=pp.b)
                    for tt in range(NT):
                        sl = slice(tt * 512, (tt + 1) * 512)
                        ps = fw.bank()
                        for k in range(8):
                            fw.op(pe, lambda h, k=k, ps=ps, sl=sl: h.matmul(ps[0:64, :], lhsT=wdt[:, k, :], rhs=hn[:, k, sl], start=(k == 0), stop=(k == 7)),
                                  reads=wdt.b + [hn.b[tt]], writes=ps.b)
                        fw.op(act, lambda h, ps=ps, sl=sl: h.activation(out=dtf[:, sl], in_=ps[0:64, :], func=AF.Exp, bias=pp[:, 0:1], scale=1.0),
                              reads=ps.b + pp.b, writes=dtf.b)
                    fw.op(act, lambda h: h.activation(out=dtf[:], in_=dtf[:], func=AF.Ln, bias=1.0, scale=1.0), reads=dtf.b, writes=dtf.b)
                    fw.op(dve, lambda h: h.tensor_scalar(out=laf[:], in0=dtf[:], scalar1=pp[:, 2:3], scalar2=None, op0=ALU.mult),
                          reads=dtf.b + pp.b, writes=laf.b)
                    fw.op(dve, lambda h: h.memset(Wt[:], 0.0), writes=Wt.b)
                    fw.dma(sp, mk[:], mrow_d[0].partition_broadcast(64), writes=mk.b)
                    fw.op(dve, lambda h: h.tensor_tensor_scan(out=Wt[0:16, :], data0=mk[0:16, :], data1=laf[0:16, :], initial=0.0,
                                                              op0=ALU.mult, op1=ALU.add), reads=mk.b + laf.b, writes=Wt.b)
                    fw.dma(sp, mk[:], mrow_d[1].partition_broadcast(64), writes=mk.b)
                    fw.op(dve, lambda h: h.tensor_tensor_scan(out=Wt[32:48, ::-1], data0=mk[32:48, ::-1], data1=laf[32:48, ::-1], initial=0.0,
                                                              op0=ALU.mult, op1=ALU.add), reads=mk.b + laf.b, writes=Wt.b)
                    Wv = Wt[:].rearrange("p (c l) -> p c l", l=128)
                    fw.op(dve, lambda h: h.memset(Tt[:], 0.0), writes=Tt.b)
                    fw.op(dve, lambda h: h.tensor_copy(out=Tt[0:16, :], in_=Wv[0:16, :, 127]), reads=Wt.b, writes=Tt.b)
                    fw.op(dve, lambda h: h.tensor_copy(out=Tt[32:48, :], in_=Wv[32:48, :, 0]), reads=Wt.b, writes=Tt.b)
                    fw.op(dve, lambda h: h.tensor_tensor(out=mk[:].rearrange("p (c l) -> p c l", l=128), in0=Tt[:].unsqueeze(2).to_broadcast([64, 32, 128]),
                                                         in1=Wv, op=ALU.subtract), reads=Tt.b + Wt.b, writes=mk.b)
                    fw.op(act, lambda h: h.activation(out=mk[:], in_=mk[:], func=AF.Exp), reads=mk.b, writes=mk.b)
                    fw.op(act, lambda h: h.activation(out=laf[:], in_=Wt[:], func=AF.Exp), reads=Wt.b, writes=laf.b)
                    fw.op(act, lambda h: h.activation(out=Tt[:], in_=Tt[:], func=AF.Exp), reads=Tt.b, writes=Tt.b)
                    fw.dma(sp, Wd[:, :], Wt[:], reads=Wt.b, defer=True)
                    fw.dma(sp, ecd_d.rearrange("(p c) -> p c", c=32), Tt[:], reads=Tt.b, defer=True)
                    for c in range(32):
                        ps = fw.bank()
                        for qi, q in enumerate((dtf, Wt, laf, mk)):
                            fw.op(pe, lambda h, qi=qi, q=q, ps=ps, c=c: h.transpose(out=ps[:, qi * 64:(qi + 1) * 64], in_=q[:, c * 128:(c + 1) * 128],
                                                                                 identity=identF[0:64, 0:64]), reads=q.b + identF.b, writes=ps.b)
                        fw.op(act, lambda h, ps=ps, c=c: h.copy(out=TM[:, c, :, :], in_=ps[:, 0:256].rearrange("p (q d) -> p q d", d=64)),
                              reads=ps.b, writes=TM.b)
                    fw.barrier()
                    fw.dma(sp, ecd[:].rearrange("p d c -> p (d c)"), ecd_d.partition_broadcast(128), writes=ecd.b)
                    fw.barrier()
                with ExitStack() as st:
                    prev = sb(st, "sprev", [128, 1024])
                    prevb = sb(st, "sprevb", [128, 1024], BF16)
                    xs_t = [sb(st, "sxs%d" % i, [128, 1024], BF16) for i in range(2)]
                    Btt = [sb(st, "sBt%d" % i, [128, 512], BF16) for i in range(2)]
                    Bft = [sb(st, "sBf%d" % i, [128, 4, 128], BF16) for i in range(2)]
                    Cft = [sb(st, "sCf%d" % i, [128, 4, 128], BF16) for i in range(2)]
                    wbc = [sb(st, "swbc%d" % i, [128, 16, 128]) for i in range(2)]
                    xsd2 = [sb(st, "sxsd%d" % i, [128, 1024], BF16) for i in range(2)]
                    xsdo2 = [sb(st, "sxsdo%d" % i, [128, 1024], BF16) for i in range(2)]
                    cbm = sb(st, "scbm", [128, 4, 128], BF16)
                    ddA = sb(st, "sddA", [128, 16, 128])
                    EA = sb(st, "sEA", [128, 16, 128], BF16)
                    MA = sb(st, "sMA", [128, 16, 128], BF16)
                    ysum = sb(st, "sys", [128, 1024])
                    yft2 = [sb(st, "syf%d" % i, [128, 1024]) for i in range(2)]
                    tmpd = sb(st, "std", [128, 1024])
                    ztt2 = [sb(st, "sztt%d" % i, [128, 1024], BF16) for i in range(2)]
                    ngrow = sb(st, "sng", [128, 1024])
                    Drow = sb(st, "sD", [128, 16])
                    ynb = sb(st, "synb", [128, 1024], BF16)
                    mo = sb(st, "smo", [128, 8, 128], BF16)
                    ss = sb(st, "sss", [128, 8])
                    junk = sb(st, "sjunk", [128, 256])
                    eps5 = sb(st, "seps5", [128, 1])
                    mks = [sb(st, "smask%d" % i, [128, 128]) for i in range(2)]
                    fw.dma(sp, ngrow[:], ssd_norm_g[j].partition_broadcast(128), writes=ngrow.b)
                    fw.dma(sp, Drow[:], ssd_D[j].partition_broadcast(128), writes=Drow.b)
                    fw.dma(sp, mks[0][:], mask_d[0], writes=mks[0].b)
                    fw.dma(sp, mks[1][:], mask_d[1], writes=mks[1].b)
                    fw.op(dve, lambda h: h.memset(eps5[:], 1e-5), writes=eps5.b)
                    ecdv = ecd
                    Bf_v = B_fm.rearrange("(g n) l -> n g l", n=128)
                    Cf_v = C_fm.rearrange("(g n) l -> n g l", n=128)
                    mix_o = mixfm[1024:2048, :].rearrange("(b p) l -> p b l", p=128)
                    it = [0]

                    def front(d, c):
                        ro = d * 32
                        i_ = it[0]
                        it[0] += 1
                        X = dict(c=c, d=d, ro=ro, xs=xs_t[i_ % 2], Bt=Btt[i_ % 2], Bf=Bft[i_ % 2], Cf=Cft[i_ % 2], wb=wbc[i_ % 2],
                                 xsd=xsd2[i_ % 2], xsdo=xsdo2[i_ % 2], yft=yft2[i_ % 2], ztt=ztt2[i_ % 2])
                        yft, ztt = X["yft"], X["ztt"]
                        xs, Bt, Bf, Cf, wb_, xsd, xsdo = X["xs"], X["Bt"], X["Bf"], X["Cf"], X["wb"], X["xsd"], X["xsdo"]
                        rows = slice(c * 128, (c + 1) * 128)
                        fw.dma(sp, xs[:], xs_tm[rows, :], writes=xs.b)
                        fw.dma(sp, Bt[:], B_tm[rows, :], writes=Bt.b)
                        with nc.allow_non_contiguous_dma(reason="256B runs"):
                            fw.dma(sp, Bf[:], Bf_v[:, :, rows], writes=Bf.b)
                            fw.dma(sp, Cf[:], Cf_v[:, :, rows], writes=Cf.b)
                        fw.dma(sp, wb_[:], Wd[ro:ro + 16, rows].unsqueeze(0).to_broadcast([128, 16, 128]), writes=wb_.b)
                        if d == 1:
                            fw.dma(sp, yft[:], Yf[rows, :], writes=yft.b)
                            fw.dma(sp, ztt[:], z_tm[rows, :], writes=ztt.b)
                        fw.flush()
                        xs3 = xs[:].rearrange("p (h e) -> p h e", e=64)
                        fw.op(dve, lambda h: h.tensor_tensor(out=xsd[:].rearrange("p (h e) -> p h e", e=64), in0=xs3,
                                                             in1=TM[:, c, 0, ro:ro + 16].unsqueeze(2).to_broadcast([128, 16, 64]), op=ALU.mult),
                              reads=xs.b + TM.b, writes=xsd.b)
                        fw.op(dve, lambda h: h.tensor_tensor(out=xsdo[:].rearrange("p (h e) -> p h e", e=64),
                                                              in0=xsd[:].rearrange("p (h e) -> p h e", e=64),
                                                              in1=TM[:, c, 3, ro:ro + 16].unsqueeze(2).to_broadcast([128, 16, 64]), op=ALU.mult),
                              reads=xsd.b + TM.b, writes=xsdo.b)
                        pcb = fw.psum[4]
                        for g in range(4):
                            fw.op(pe, lambda h, g=g: h.matmul(pcb[:, g * 128:(g + 1) * 128], lhsT=Bf[:, g, :], rhs=Cf[:, g, :], start=True, stop=True),
                                  reads=Bf.b + Cf.b, writes=pcb.b)
                        fw.op(dve, lambda h: h.tensor_tensor(out=cbm[:], in0=pcb[:].rearrange("p (g l) -> p g l", l=128),
                                                             in1=mks[d][:].unsqueeze(1).to_broadcast([128, 4, 128]), op=ALU.mult),
                              reads=pcb.b + mks[d].b, writes=cbm.b)
                        Yd = [fw.psum[2 * (i_ % 2)], fw.psum[2 * (i_ % 2) + 1]]
                        X["Yd"] = Yd
                        wsb = TM[:, c, 1, ro:ro + 16].unsqueeze(2).to_broadcast([128, 16, 128])
                        fw.op(dve, lambda h: h.tensor_tensor(out=ddA[:], in0=wb_[:], in1=wsb, op=ALU.min),
                              reads=wb_.b + TM.b, writes=ddA.b)
                        fw.op(dve, lambda h: h.tensor_tensor(out=ddA[:], in0=ddA[:], in1=wsb, op=ALU.subtract),
                              reads=ddA.b + TM.b, writes=ddA.b)
                        fw.op(act, lambda h: h.activation(out=EA[:], in_=ddA[:], func=AF.Exp), reads=ddA.b, writes=EA.b)
                        fw.op(dve, lambda h: h.tensor_tensor(out=MA[:].rearrange("p (g r) l -> p g r l", r=4),
                                                             in0=EA[:].rearrange("p (g r) l -> p g r l", r=4),
                                                             in1=cbm[:].unsqueeze(2).to_broadcast([128, 4, 4, 128]), op=ALU.mult),
                              reads=EA.b + cbm.b, writes=MA.b)
                        for hh in range(16):
                            fw.op(pe, lambda h, hh=hh: h.matmul(Yd[hh // 8][:, (hh % 8) * 64:(hh % 8 + 1) * 64], lhsT=MA[:, hh, :],
                                                                rhs=xsd[:, hh * 64:(hh + 1) * 64], start=True, stop=True),
                                  reads=MA.b + xsd.b, writes=Yd[hh // 8].b)
                        return X

                    def back(X):
                        c, d, ro = X["c"], X["d"], X["ro"]
                        xs, Bt, Cf, xsdo, Yd = X["xs"], X["Bt"], X["Cf"], X["xsdo"], X["Yd"]
                        yft, ztt = X["yft"], X["ztt"]
                        rows = slice(c * 128, (c + 1) * 128)
                        xs3 = xs[:].rearrange("p (h e) -> p h e", e=64)
                        Yo = [fw.psum[5], fw.psum[6]]
                        for g in range(4):
                            fw.op(pe, lambda h, g=g: h.matmul(Yo[g // 2][:, (g % 2) * 256:(g % 2 + 1) * 256], lhsT=Cf[:, g, :],
                                                              rhs=prevb[:, g * 256:(g + 1) * 256], start=True, stop=True),
                                  reads=Cf.b + prevb.b, writes=Yo[g // 2].b)
                        for half in range(2):
                            fw.op(dve, lambda h, half=half: h.tensor_tensor(
                                out=ysum[:, half * 512:(half + 1) * 512].rearrange("p (h e) -> p h e", e=64),
                                in0=Yo[half][:].rearrange("p (h e) -> p h e", e=64),
                                in1=TM[:, c, 2, ro + half * 8:ro + half * 8 + 8].unsqueeze(2).to_broadcast([128, 8, 64]), op=ALU.mult),
                                reads=Yo[half].b + TM.b, writes=ysum.b)
                        St = [fw.psum[5], fw.psum[6]]
                        for g in range(4):
                            fw.op(pe, lambda h, g=g: h.matmul(St[g // 2][:, (g % 2) * 256:(g % 2 + 1) * 256], lhsT=Bt[:, g * 128:(g + 1) * 128],
                                                              rhs=xsdo[:, g * 256:(g + 1) * 256], start=True, stop=True),
                                  reads=Bt.b + xsdo.b, writes=St[g // 2].b)
                        for half in range(2):
                            fw.op(dve, lambda h, half=half: h.tensor_tensor(out=ysum[:, half * 512:(half + 1) * 512], in0=Yd[half][:],
                                                                           in1=ysum[:, half * 512:(half + 1) * 512], op=ALU.add),
                                  reads=Yd[half].b + ysum.b, writes=ysum.b)
                        fw.op(dve, lambda h: h.tensor_tensor(out=prev[:].rearrange("p (h e) -> p h e", e=64), in0=prev[:].rearrange("p (h e) -> p h e", e=64),
                                                              in1=ecd[:, ro:ro + 16, c].unsqueeze(2).to_broadcast([128, 16, 64]), op=ALU.mult),
                              reads=prev.b + ecd.b, writes=prev.b)
                        for half in range(2):
                            fw.op(dve, lambda h, half=half: h.tensor_tensor(out=prev[:, half * 512:(half + 1) * 512], in0=St[half][:],
                                                                           in1=prev[:, half * 512:(half + 1) * 512], op=ALU.add),
                                  reads=St[half].b + prev.b, writes=prev.b)
                        fw.op(act, lambda h: h.copy(out=prevb[:], in_=prev[:]), reads=prev.b, writes=prevb.b)
                        if d == 0:
                            fw.dma(sp, Yf[rows, :], ysum[:], reads=ysum.b, defer=True)
                        else:
                            fw.op(pool, lambda h: h.tensor_tensor(out=ysum[:], in0=ysum[:], in1=yft[:], op=ALU.add), reads=ysum.b + yft.b, writes=ysum.b)
                            fw.op(dve, lambda h: h.tensor_tensor(out=tmpd[:].rearrange("p (h e) -> p h e", e=64), in0=xs3,
                                                                 in1=Drow[:].unsqueeze(2).to_broadcast([128, 16, 64]), op=ALU.mult),
                                  reads=xs.b + Drow.b, writes=tmpd.b)
                            fw.op(pool, lambda h: h.tensor_tensor(out=ysum[:], in0=ysum[:], in1=tmpd[:], op=ALU.add), reads=ysum.b + tmpd.b, writes=ysum.b)
                            fw.op(pool, lambda h: h.tensor_tensor(out=ysum[:], in0=ysum[:], in1=ztt[:], op=ALU.mult), reads=ysum.b + ztt.b, writes=ysum.b)
                            for g in range(4):
                                fw.op(act, lambda h, g=g: h.activation(out=junk[:], in_=ysum[:, g * 256:(g + 1) * 256], func=AF.Square,
                                                                       accum_out=ss[:, g:g + 1]), reads=ysum.b, writes=junk.b + ss.b)
                            fw.op(act, lambda h: h.activation(out=ss[:, 4:8], in_=ss[:, 0:4], func=AF.Sqrt, scale=1.0 / 256.0, bias=eps5[:]),
                                  reads=ss.b + eps5.b, writes=ss.b)
                            fw.op(dve, lambda h: h.reciprocal(out=ss[:, 4:8], in_=ss[:, 4:8]), reads=ss.b, writes=ss.b)
                            fw.op(dve, lambda h: h.tensor_tensor(out=ysum[:].rearrange("p (g e) -> p g e", e=256), in0=ysum[:].rearrange("p (g e) -> p g e", e=256),
                                                                 in1=ss[:, 4:8].unsqueeze(2).to_broadcast([128, 4, 256]), op=ALU.mult),
                                  reads=ysum.b + ss.b, writes=ysum.b)
                            fw.op(pool, lambda h: h.tensor_tensor(out=ynb[:], in0=ysum[:], in1=ngrow[:], op=ALU.mult), reads=ysum.b + ngrow.b, writes=ynb.b)
                            for t8 in range(2):
                                ps = fw.psum[7] if t8 == 0 else fw.psum[4]
                                psb = ps[:].bitcast(BF16)
                                for i in range(4):
                                    cb = t8 * 4 + i
                                    fw.op(pe, lambda h, i=i, cb=cb, psb=psb, ps=ps: h.transpose(out=psb[:, i * 128:(i + 1) * 128],
                                                                                              in_=ynb[:, cb * 128:(cb + 1) * 128], identity=identB[:]),
                                          reads=ynb.b + identB.b, writes=ps.b)
                                fw.op(act, lambda h, t8=t8, psb=psb, ps=ps: h.copy(out=mo[:, t8 * 4:(t8 + 1) * 4, :],
                                                                                  in_=psb[:, 0:512].rearrange("p (i c) -> p i c", c=128)),
                                      reads=ps.b, writes=mo.b)
                            fw.dma(sp, mix_o[:, :, rows], mo[:], reads=mo.b, defer=True)

                    for d in range(2):
                        if d == 1:
                            fw.barrier()
                        fw.op(dve, lambda h: h.memset(prev[:], 0.0), writes=prev.b)
                        fw.op(dve, lambda h: h.memset(prevb[:], 0.0), writes=prevb.b)
                        order = list(range(32)) if d == 0 else list(range(31, -1, -1))
                        X = front(d, order[0])
                        for i_c in range(32):
                            Xn = front(d, order[i_c + 1]) if i_c + 1 < 32 else None
                            back(X)
                            X = Xn
                    fw.barrier()

        oneT = sb(es, "oneT", [128, 1])
        fw.op(dve, lambda h: h.memset(oneT[:], 1.0), writes=oneT.b)
        for j in range(2):
            if 2 * j in layers:
                phase_filter(j)

        for s in range(S):
            first = True
            for li in range(4):
                if li not in layers:
                    continue
                if first:
                    phase_norm0(s, li)
                    first = False
                rest = [l for l in layers if l > li]
                nli = rest[0] if rest else 4
                if li % 2 == 1:
                    phase_odd(s, li // 2)
                    phase_out(s, li, od_w_out[li // 2], nli)
                else:
                    phase_hyena(s, li // 2)
                    phase_hyena_dft(s, li // 2)
                    if not DEBUG.get("no_ssd"):
                        phase_ssd(s, li // 2)
                    phase_out(s, li, ev_w_out[li // 2], nli)
    return nc


_CONST_CACHE = {}


def _consts():
    if _CONST_CACHE:
        return _CONST_CACHE
    N = 2 * L
    c = {"identF_d": np.eye(128, dtype=np.float32)}
    n = np.arange(L, dtype=np.float64)[:, None]
    k = np.arange(L, dtype=np.float64)[None, :]
    th = (2.0 * np.pi / N) * ((n * (2 * k + 1)) % (2 * N)) / 2.0
    Cm = np.cos(th).astype(np.float32)
    Sm = np.sin(th).astype(np.float32)
    del th
    bf = ml_dtypes.bfloat16
    tF = np.stack([Cm, Sm], 0).reshape(2, 32, 128, 32, 128)
    c["tabF"] = np.ascontiguousarray(tF.transpose(3, 2, 0, 1, 4)).astype(bf).reshape(32, 128, 2 * 32 * 128)
    tI = (np.stack([Cm, Sm], 0) * np.float32(2.0 / N)).reshape(2, 8, 512, 32, 128)
    c["tabI"] = np.ascontiguousarray(tI.transpose(1, 3, 4, 0, 2)).astype(bf).reshape(8, 32, 128, 2 * 512)
    del Cm, Sm, tF, tI
    pos = np.arange(L, dtype=np.float32)[:, None]
    t = pos / np.float32(L - 1)
    f = np.linspace(1e-4, 15, 16, dtype=np.float32)[None, :]
    ang = f * pos * np.float32(2.0 * math.pi / L)
    z = np.concatenate([t, np.cos(ang), -np.sin(ang)], axis=-1).astype(np.float32)
    c["zfeat"] = np.ascontiguousarray(z.T)
    c["trow"] = np.ascontiguousarray(t[:, 0])
    c["deltas"] = np.abs(np.linspace(math.log(1e-2) / 1.5, math.log(1e-2) / 0.3, 1024, dtype=np.float32)).astype(np.float32)
    l = np.arange(L)
    c["mrow"] = np.stack([(l % 128 != 0), (l % 128 != 127)]).astype(np.float32)
    si = np.arange(128)[:, None]
    li_ = np.arange(128)[None, :]
    c["maskd"] = np.stack([(li_ >= si), (li_ <= si)]).astype(np.float32)
    _CONST_CACHE.update(c)
    return _CONST_CACHE


def kernel(**inputs):
    return _run(inputs, (0, 1, 2, 3))


def _run(inputs, layers):
    inp = {k: np.ascontiguousarray(np.asarray(v)) for k, v in inputs.items()}
    seqs_x = [inp["x_prompt"][i] for i in range(8)] + [inp["x_sample"][i] for i in range(4)]
    seqs_c = [inp["c_prompt"][i] for i in range(8)] + [inp["c_sample"][i] for i in range(4)]
    assign = [(2 * k, 2 * k + 1) for k in range(4)] + [(8 + k, 8 + k) for k in range(4)]
    nc = build_program(layers)
    wnames = ["mod_w", "mod_b", "norm_g", "final_g", "od_w_in", "od_w_out", "lru_conv_w", "lru_conv_b",
              "lru_w_a", "lru_b_a", "lru_w_x", "lru_b_x", "lru_lam", "ev_w_out",
              "ev_w_in", "hy_conv_w", "hy_conv_b", "hy_fw1", "hy_fb1", "hy_fw2", "hy_fb2", "hy_fw3", "hy_freq",
              "hy_bias", "ssd_conv_w", "ssd_conv_b", "ssd_norm_g"]
    consts = _consts()
    in_maps = []
    for (a, b) in assign:
        m = {"xin": np.stack([seqs_x[a], seqs_x[b]]), "cin": np.stack([seqs_c[a], seqs_c[b]])}
        for w in wnames:
            m[w] = inp[w]
        m["ssd_dt_bias"] = inp["ssd_dt_bias"].reshape(2, 32)
        m["ssd_A_log"] = inp["ssd_A_log"].reshape(2, 32)
        m["ssd_D"] = inp["ssd_D"]
        m.update(consts)
        in_maps.append(m)
    res = run_bass_kernel_spmd(nc, in_maps, core_ids=list(range(NCORES)))
    if DEBUG.get("dump"):
        DEBUG["res"] = res.results
    outs = [r["yout"] for r in res.results]
    yp = np.stack([outs[k][i] for k in range(4) for i in range(2)]).astype(np.float32)
    ys = np.stack([outs[4 + k][0] for k in range(4)]).astype(np.float32)
    return (yp, ys)
```

```python
import math
from contextlib import ExitStack
import numpy as np
import ml_dtypes
import concourse.bass as bass
import concourse.mybir as mybir
from concourse.bass_utils import run_bass_kernel_spmd

F32 = mybir.dt.float32
BF16 = mybir.dt.bfloat16
AF = mybir.ActivationFunctionType
ALU = mybir.AluOpType

D = 1024
L = 4096
S = 2
NCORES = 8
NT = L // 512
EVEN_IN = 7200


class Buf:
    __slots__ = ("w", "r")

    def __init__(self):
        self.w = None
        self.r = []


class Eng:
    def __init__(self, name, h, sem, inc):
        self.name = name
        self.h = h
        self.sem = sem
        self.inc = inc
        self.cnt = 0
        self.waited = {}

    def wait(self, dep):
        e, v = dep
        if e is self and self.name == "pe":
            return
        if self.waited.get(e.name, 0) >= v:
            return
        self.h.wait_ge(e.sem, v)
        self.waited[e.name] = v


class FW:
    NSLOT = 24

    def __init__(self, nc, es):
        self.nc = nc

        def mk(name, h, inc):
            return Eng(name, h, es.enter_context(nc.semaphore("s_" + name)), inc)

        self.pe = mk("pe", nc.tensor, 1)
        self.act = mk("act", nc.scalar, 1)
        self.dve = mk("dve", nc.vector, 1)
        self.pool = mk("pool", nc.gpsimd, 1)
        self.sp = Eng("sp", nc.sync, None, 0)
        self.slots = [mk("dq%d" % i, None, 16) for i in range(self.NSLOT)]
        self.slots_sw = [mk("dw%d" % i, None, 16) for i in range(8)]
        self.slot_i = 0
        self.slot_sw_i = 0
        self.engs = [self.pe, self.act, self.dve, self.pool]
        self.psum = []
        self.ps_i = 0
        self.pending = []
        self.pend_r = set()

    def _deps(self, eng, reads, writes):
        for b in reads:
            if b.w is not None:
                eng.wait(b.w)
        for b in writes:
            if b.w is not None:
                eng.wait(b.w)
            for d in b.r:
                eng.wait(d)

    def _mark(self, me, reads, writes):
        for b in reads:
            b.r = [x for x in b.r if x[0] is not me[0]] + [me]
        for b in writes:
            b.w = me
            b.r = []

    def _autoflush(self, writes):
        if self.pending and any(id(b) in self.pend_r for b in writes):
            self.flush()

    def flush(self):
        p, self.pending = self.pending, []
        self.pend_r = set()
        with self.nc.allow_non_contiguous_dma(reason="deferred store"):
            for (issuer, out, in_, reads, writes) in p:
                self.dma(issuer, out, in_, reads, writes)

    def op(self, eng, fn, reads=(), writes=()):
        self._autoflush(writes)
        self._deps(eng, reads, writes)
        ins = fn(eng.h)
        eng.cnt += 1
        ins.then_inc(eng.sem, 1)
        self._mark((eng, eng.cnt), reads, writes)

    def dma(self, issuer, out, in_, reads=(), writes=(), defer=False):
        if defer:
            self.pending.append((issuer, out, in_, list(reads), list(writes)))
            for b in reads:
                self.pend_r.add(id(b))
            return
        self._autoflush(writes)
        if issuer is self.pool:
            slot = self.slots_sw[self.slot_sw_i % len(self.slots_sw)]
            self.slot_sw_i += 1
        else:
            slot = self.slots[self.slot_i % self.NSLOT]
            self.slot_i += 1
        if slot.cnt:
            issuer.wait((slot, slot.cnt))
        self._deps(issuer, reads, writes)
        ins = issuer.h.dma_start(out=out, in_=in_)
        slot.cnt += 16
        ins.then_inc(slot.sem, 16)
        self._mark((slot, slot.cnt), reads, writes)

    def barrier(self):
        self.flush()
        toks = [(e, e.cnt) for e in self.engs if e.cnt] + [(s, s.cnt) for s in self.slots + self.slots_sw if s.cnt]
        for e in self.engs + [self.sp]:
            for t in toks:
                e.wait(t)

    def bank(self):
        b = self.psum[self.ps_i % len(self.psum)]
        self.ps_i += 1
        return b


class T:
    def __init__(self, t, nb=1):
        self.t = t
        self.b = [Buf() for _ in range(nb)]

    def __getitem__(self, k):
        return self.t[k]


DEBUG = {}


def build_program(layers=(0, 1, 2, 3)):
    nc = bass.Bass("TRN2", target_bir_lowering=False)

    def din(name, shape, dt=F32):
        return nc.dram_tensor(name, list(shape), dt, kind="ExternalInput").ap()

    def dscr(name, shape, dt=F32):
        return nc.dram_tensor(name, list(shape), dt, kind="Internal").ap()

    xin = din("xin", [S, L, D])
    cin = din("cin", [S, D])
    mod_w = din("mod_w", [4, D, 3 * D])
    mod_b = din("mod_b", [4, 3 * D])
    norm_g = din("norm_g", [4, D])
    final_g = din("final_g", [D])
    od_w_in = din("od_w_in", [2, D, 4096])
    od_w_out = din("od_w_out", [2, 2048, D])
    lru_conv_w = din("lru_conv_w", [2, 4, 2048])
    lru_conv_b = din("lru_conv_b", [2, 2048])
    lru_w_a = din("lru_w_a", [2, 2, 16, 128, 128])
    lru_b_a = din("lru_b_a", [2, 2, 2048])
    lru_w_x = din("lru_w_x", [2, 2, 16, 128, 128])
    lru_b_x = din("lru_b_x", [2, 2, 2048])
    lru_lam = din("lru_lam", [2, 2, 2048])
    ev_w_out = din("ev_w_out", [2, 2048, D])
    identF_d = din("identF_d", [128, 128])
    ev_w_in = din("ev_w_in", [2, D, EVEN_IN])
    hy_conv_w = din("hy_conv_w", [2, 3, 3072])
    hy_conv_b = din("hy_conv_b", [2, 3072])
    hy_fw1 = din("hy_fw1", [2, 33, 64])
    hy_fb1 = din("hy_fb1", [2, 64])
    hy_fw2 = din("hy_fw2", [2, 64, 64])
    hy_fb2 = din("hy_fb2", [2, 64])
    hy_fw3 = din("hy_fw3", [2, 64, 2048])
    hy_freq = din("hy_freq", [2, 64])
    hy_bias = din("hy_bias", [2, 1024])
    ssd_conv_w = din("ssd_conv_w", [2, 4, 2048])
    ssd_conv_b = din("ssd_conv_b", [2, 2048])
    ssd_dt_bias = din("ssd_dt_bias", [2, 32])
    ssd_A_log = din("ssd_A_log", [2, 32])
    ssd_D = din("ssd_D", [2, 16])
    ssd_norm_g = din("ssd_norm_g", [2, 1024])
    tabF = din("tabF", [32, 128, 2 * 32 * 128], BF16)
    tabI = din("tabI", [8, 32, 128, 2 * 512], BF16)
    zfeat = din("zfeat", [33, L])
    trow = din("trow", [L])
    deltas = din("deltas", [1024])
    mrow_d = din("mrow", [2, L])
    mask_d = din("maskd", [2, 128, 128])
    xs_tm = dscr("xs_tm", [L, 1024], BF16)
    B_tm = dscr("B_tm", [L, 512], BF16)
    B_fm = dscr("B_fm", [512, L], BF16)
    C_fm = dscr("C_fm", [512, L], BF16)
    z_tm = dscr("z_tm", [L, 1024], BF16)
    Wd = dscr("Wd", [64, L])
    ecd_d = dscr("ecd_d", [64 * 32])
    Yf = dscr("Yf", [L, 1024])
    vf_tm = dscr("vf_tm", [L, 1024], BF16)
    xg_fm = dscr("xg_fm", [1024, L], BF16)
    filt_tm = dscr("filt_tm", [2, L, 1024], BF16)
    Gsp = dscr("Gsp", [2, 32, 128, 2, 1024])
    Ysp = dscr("Ysp", [32, 128, 2, 1024], BF16)
    yout = nc.dram_tensor("yout", [S, L, D], F32, kind="ExternalOutput").ap()

    xres = dscr("xres", [S, D, L])
    if DEBUG.get("dump"):
        mixfm = nc.dram_tensor("mixfm", [2048, L], BF16, kind="ExternalOutput").ap()
        hn_dump = nc.dram_tensor("hn_dump", [128, 8, L], BF16, kind="ExternalOutput").ap()
    else:
        mixfm = dscr("mixfm", [2048, L], BF16)

    es = ExitStack()
    with es:
        fw = FW(nc, es)
        pe, act, dve, pool, sp = fw.pe, fw.act, fw.dve, fw.pool, fw.sp

        uid = [0]

        def sb(st, name, shape, dt=F32, nb=1):
            uid[0] += 1
            return T(st.enter_context(nc.sbuf_tensor("%s_%d" % (name, uid[0]), list(shape), dt)), nb)

        def ldvec(dst, src1d, bufs):
            with nc.allow_non_contiguous_dma(reason="tiny param loads"):
                fw.dma(sp, dst, src1d.rearrange("(h p) -> p h", p=128), writes=bufs)

        for i in range(8):
            fw.psum.append(T(es.enter_context(nc.psum_tensor("ps%d" % i, [128, 512], F32))))

        hn = sb(es, "hn", [128, 8, L], BF16, nb=NT)
        identF = sb(es, "identF", [128, 128])
        identB = sb(es, "identB", [128, 128], BF16)
        onesB = sb(es, "onesB", [128, 128], BF16)
        MOD = sb(es, "MOD", [128, 5, S, 3, 8])
        fw.dma(sp, identF[:], identF_d[:, :], writes=identF.b)
        fw.op(dve, lambda h: h.tensor_copy(out=identB[:], in_=identF[:]), reads=identF.b, writes=identB.b)
        fw.op(dve, lambda h: h.memset(onesB[:], 1.0), writes=onesB.b)

        with ExitStack() as st:
            cT = sb(st, "cT", [128, 8, S])
            csT = sb(st, "csT", [128, 8, S], BF16)
            mb = sb(st, "mb", [128, 4, 24])
            ng = sb(st, "ng", [128, 5, 8])
            mw = [sb(st, "mw%d" % i, [128, 8, 1536], BF16) for i in range(2)]
            mraw = sb(st, "mraw", [128, 24, S])
            for s_ in range(S):
                ldvec(cT[:, :, s_], cin[s_], cT.b)
            for i in range(4):
                ldvec(mb[:, i, :], mod_b[i], mb.b)
                ldvec(ng[:, i, :], norm_g[i], ng.b)
            ldvec(ng[:, 4, :], final_g, ng.b)
            fw.op(act, lambda h: h.activation(out=csT[:], in_=cT[:], func=AF.Silu), reads=cT.b, writes=csT.b)
            for i in range(4):
                ps = fw.bank()
                for half in range(2):
                    w = mw[half]
                    fw.dma(pool, w[:], mod_w[i].rearrange("(k p) c -> p k c", p=128)[:, :, half * 1536:(half + 1) * 1536],
                           writes=w.b)
                    for jj in range(12):
                        j = half * 12 + jj
                        for k in range(8):
                            fw.op(pe, lambda h, j=j, jj=jj, k=k, w=w, ps=ps: h.matmul(
                                ps[:, j * S:(j + 1) * S], lhsT=w[:, k, jj * 128:(jj + 1) * 128], rhs=csT[:, k, :],
                                start=(k == 0), stop=(k == 7)), reads=w.b + csT.b, writes=ps.b)
                fw.op(dve, lambda h, ps=ps, i=i: h.tensor_tensor(
                    out=mraw[:], in0=ps[:, 0:24 * S].rearrange("p (j s) -> p j s", s=S),
                    in1=mb[:, i, :].unsqueeze(2).to_broadcast([128, 24, S]), op=ALU.add),
                    reads=ps.b + mb.b, writes=mraw.b)
                for s in range(S):
                    fw.op(dve, lambda h, i=i, s=s: h.scalar_tensor_tensor(
                        out=MOD[:, i, s, 0, :], in0=mraw[:, 8:16, s], scalar=1.0, in1=ng[:, i, :],
                        op0=ALU.add, op1=ALU.mult), reads=mraw.b + ng.b, writes=MOD.b)
                    fw.op(dve, lambda h, i=i, s=s: h.tensor_copy(out=MOD[:, i, s, 1, :], in_=mraw[:, 0:8, s]),
                          reads=mraw.b, writes=MOD.b)
                    fw.op(dve, lambda h, i=i, s=s: h.tensor_copy(out=MOD[:, i, s, 2, :], in_=mraw[:, 16:24, s]),
                          reads=mraw.b, writes=MOD.b)
            for s in range(S):
                fw.op(dve, lambda h, s=s: h.tensor_copy(out=MOD[:, 4, s, 0, :], in_=ng[:, 4, :]), reads=ng.b, writes=MOD.b)
                fw.op(dve, lambda h, s=s: h.memset(MOD[:, 4, s, 1, :], 0.0), writes=MOD.b)
            fw.barrier()

        def norm_tile(st_bufs, xt, li, s, tt, out_fn):
            sq, rstd = st_bufs
            fw.op(act, lambda h: h.activation(out=sq[:], in_=xt[:], func=AF.Square), reads=xt.b, writes=sq.b)
            ps = fw.bank()
            for k in range(8):
                fw.op(pe, lambda h, k=k: h.matmul(ps[:], lhsT=onesB[:], rhs=sq[:, k, :], start=(k == 0), stop=(k == 7)),
                      reads=onesB.b + sq.b, writes=ps.b)
            fw.op(act, lambda h: h.activation(out=rstd[:], in_=ps[:], func=AF.Sqrt, scale=1.0 / D, bias=epsT[:]),
                  reads=ps.b + epsT.b, writes=rstd.b)
            fw.op(dve, lambda h: h.reciprocal(out=rstd[:], in_=rstd[:]), reads=rstd.b, writes=rstd.b)
            for k in range(8):
                out_fn(k, rstd)

        epsT = sb(es, "epsT", [128, 1])
        fw.op(dve, lambda h: h.memset(epsT[:], 1e-6), writes=epsT.b)

        xres_v = [xres[s].rearrange("(cb p) l -> p cb l", p=128) for s in range(S)]
        mix_v = mixfm.rearrange("(kc p) l -> p kc l", p=128)

        with ExitStack() as st:
            xa = [sb(st, "xa%d" % i, [128, 4, D]) for i in range(2)]
            xf = [sb(st, "xf%d" % i, [128, 8, 512]) for i in range(2)]
            n = 0
            for s in range(S):
                for tt in range(NT):
                    a = xa[n % 2]
                    f = xf[n % 2]
                    n += 1
                    fw.dma(sp, a[:], xin[s, tt * 512:(tt + 1) * 512, :].rearrange("(a p) c -> p a c", p=128), writes=a.b)
                    fw.flush()
                    for cb in range(8):
                        ps = fw.bank()
                        for sub in range(4):
                            fw.op(pe, lambda h, sub=sub, cb=cb, a=a, ps=ps: h.transpose(
                                out=ps[:, sub * 128:(sub + 1) * 128], in_=a[:, sub, cb * 128:(cb + 1) * 128],
                                identity=identF[:]), reads=a.b + identF.b, writes=ps.b)
                        e = act if cb % 2 == 0 else dve
                        if e is act:
                            fw.op(act, lambda h, cb=cb, f=f, ps=ps: h.copy(out=f[:, cb, :], in_=ps[:]), reads=ps.b, writes=f.b)
                        else:
                            fw.op(dve, lambda h, cb=cb, f=f, ps=ps: h.tensor_copy(out=f[:, cb, :], in_=ps[:]), reads=ps.b, writes=f.b)
                    fw.dma(sp, xres_v[s][:, :, tt * 512:(tt + 1) * 512], f[:], reads=f.b, defer=True)
            fw.barrier()

        def phase_norm0(s, li):
            with ExitStack() as st:
                xt = [sb(st, "n0x%d" % i, [128, 8, 512]) for i in range(2)]
                sq = sb(st, "n0sq", [128, 8, 512], BF16)
                rstd = sb(st, "n0rs", [128, 512])
                tmp = sb(st, "n0tmp", [128, 512])
                for tt in range(NT):
                    x = xt[tt % 2]
                    fw.dma(sp, x[:], xres_v[s][:, :, tt * 512:(tt + 1) * 512], writes=x.b)

                    def out_fn(k, rstd, x=x, tt=tt):
                        fw.op(dve, lambda h: h.scalar_tensor_tensor(
                            out=tmp[:], in0=x[:, k, :], scalar=MOD[:, li, s, 0, k:k + 1], in1=rstd[:],
                            op0=ALU.mult, op1=ALU.mult), reads=x.b + rstd.b + MOD.b, writes=tmp.b)
                        fw.op(act, lambda h: h.activation(
                            out=hn[:, k, tt * 512:(tt + 1) * 512], in_=tmp[:], func=AF.Identity,
                            bias=MOD[:, li, s, 1, k:k + 1], scale=1.0), reads=tmp.b + MOD.b, writes=[hn.b[tt]])
                    norm_tile((sq, rstd), x, li, s, tt, out_fn)
                if DEBUG.get("dump"):
                    fw.dma(sp, hn_dump[:, :, :], hn[:], reads=hn.b, defer=True)
                fw.barrier()

        def phase_out(s, li, w_out_d, nli):
            final = (nli == 4)
            with ExitStack() as st:
                wo = sb(st, "wo", [128, 16, D], BF16)
                mt = [sb(st, "mt%d" % i, [128, 16, 512], BF16) for i in range(2)]
                xo = [sb(st, "xo%d" % i, [128, 8, 512]) for i in range(2)]
                sq = sb(st, "osq", [128, 8, 512], BF16)
                rstd = sb(st, "ors", [128, 512])
                tmp = sb(st, "otmp", [128, 512])
                ytm = [sb(st, "oytm%d" % i, [128, 4, D]) for i in range(1)] if final else None
                fw.dma(pool, wo[:], w_out_d.rearrange("(kc p) c -> p kc c", p=128), writes=wo.b)
                def stage1(tt):
                        m = mt[tt % 2]
                        xo_t = xo[tt % 2]
                        xn_t = xo_t
                        yfm = xo_t
                        sl = slice(tt * 512, (tt + 1) * 512)
                        fw.dma(sp, m[:], mix_v[:, :, sl], writes=m.b)
                        fw.dma(sp, xo_t[:], xres_v[s][:, :, sl], writes=xo_t.b)
                        fw.flush()
                        for ob in range(8):
                            ps = fw.bank()
                            for kc in range(16):
                                fw.op(pe, lambda h, ob=ob, kc=kc, ps=ps, m=m: h.matmul(
                                    ps[:], lhsT=wo[:, kc, ob * 128:(ob + 1) * 128], rhs=m[:, kc, :],
                                    start=(kc == 0), stop=(kc == 15)), reads=wo.b + m.b, writes=ps.b)
                            if DEBUG.get("skip_mix"):
                                continue
                            fw.op(dve, lambda h, ob=ob, ps=ps, xo_t=xo_t, xn_t=xn_t: h.scalar_tensor_tensor(
                                out=xn_t[:, ob, :], in0=ps[:], scalar=MOD[:, li, s, 2, ob:ob + 1], in1=xo_t[:, ob, :],
                                op0=ALU.mult, op1=ALU.add), reads=ps.b + xo_t.b + MOD.b, writes=xn_t.b)
                        if not final:
                            fw.dma(sp, xres_v[s][:, :, sl], xn_t[:], reads=xn_t.b, defer=True)


                def stage2(tt):
                        xo_t = xo[tt % 2]
                        xn_t = xo_t
                        yfm = xo_t
                        sl = slice(tt * 512, (tt + 1) * 512)
                        def out_fn(k, rstd, xn_t=xn_t, tt=tt):
                            fw.op(dve, lambda h: h.scalar_tensor_tensor(
                                out=tmp[:], in0=xn_t[:, k, :], scalar=MOD[:, nli, s, 0, k:k + 1], in1=rstd[:],
                                op0=ALU.mult, op1=ALU.mult), reads=xn_t.b + rstd.b + MOD.b, writes=tmp.b)
                            if final:
                                fw.op(act, lambda h: h.copy(out=yfm[:, k, :], in_=tmp[:]), reads=tmp.b, writes=yfm.b)
                            else:
                                fw.op(act, lambda h: h.activation(
                                    out=hn[:, k, tt * 512:(tt + 1) * 512], in_=tmp[:], func=AF.Identity,
                                    bias=MOD[:, nli, s, 1, k:k + 1], scale=1.0), reads=tmp.b + MOD.b, writes=[hn.b[tt]])
                        norm_tile((sq, rstd), xn_t, nli, s, tt, out_fn)
                        if final:
                            yt = ytm[0]
                            for sub in range(4):
                                for half in range(2):
                                    ps = fw.bank()
                                    for c4 in range(4):
                                        cb = half * 4 + c4
                                        fw.op(pe, lambda h, sub=sub, cb=cb, c4=c4, ps=ps: h.transpose(
                                            out=ps[:, c4 * 128:(c4 + 1) * 128], in_=yfm[:, cb, sub * 128:(sub + 1) * 128],
                                            identity=identF[:]), reads=yfm.b + identF.b, writes=ps.b)
                                    if half == 0:
                                        fw.op(act, lambda h, sub=sub, ps=ps, yt=yt: h.copy(out=yt[:, sub, 0:512], in_=ps[:]),
                                              reads=ps.b, writes=yt.b)
                                    else:
                                        fw.op(dve, lambda h, sub=sub, ps=ps, yt=yt: h.tensor_copy(out=yt[:, sub, 512:1024], in_=ps[:]),
                                              reads=ps.b, writes=yt.b)
                            fw.dma(sp, yout[s, sl, :].rearrange("(a p) c -> p a c", p=128), yt[:], reads=yt.b, defer=True)

                stage1(0)
                for tt in range(NT):
                    if tt + 1 < NT:
                        stage1(tt + 1)
                    stage2(tt)
                fw.barrier()

        def phase_odd(s, j):
            with ExitStack() as st:
                prm = sb(st, "oprm", [128, 16, 16])
                wbuf = [sb(st, "owb%d" % i, [128, 8, 2, 128], BF16) for i in range(2)]
                gw = [sb(st, "ogw%d" % i, [128, 2, 2, 128], BF16) for i in range(2)]
                xp = sb(st, "oxp", [128, L + 4])
                xc = [sb(st, "oxc%d" % i, [128, L], BF16) for i in range(2)]
                sg = [sb(st, "osg%d" % i, [128, L], BF16) for i in range(2)]
                af = sb(st, "oaf", [128, L])
                uf = sb(st, "ouf", [128, L])
                hf = sb(st, "ohf", [128, L])
                ym = [sb(st, "oym%d" % i, [128, L], BF16) for i in range(1)]
                t1 = sb(st, "ot1", [128, L])
                for k in range(4):
                    ldvec(prm[:, :, k], lru_conv_w[j, k], prm.b)
                ldvec(prm[:, :, 4], lru_conv_b[j], prm.b)
                for d in range(2):
                    ldvec(prm[:, :, 5 + d], lru_b_a[j, d], prm.b)
                    ldvec(prm[:, :, 7 + d], lru_b_x[j, d], prm.b)
                    ldvec(prm[:, :, 9 + d], lru_lam[j, d], prm.b)
                fw.op(act, lambda h: h.activation(out=prm[:, :, 11:13], in_=prm[:, :, 9:11], func=AF.Exp, scale=-1.0),
                      reads=prm.b, writes=prm.b)
                fw.op(act, lambda h: h.activation(out=prm[:, :, 11:13], in_=prm[:, :, 11:13], func=AF.Ln, bias=1.0, scale=1.0),
                      reads=prm.b, writes=prm.b)
                fw.op(dve, lambda h: h.tensor_scalar(out=prm[:, :, 9:11], in0=prm[:, :, 11:13], scalar1=-8.0, scalar2=None, op0=ALU.mult),
                      reads=prm.b, writes=prm.b)
                fw.op(dve, lambda h: h.tensor_scalar(out=prm[:, :, 11:13], in0=prm[:, :, 9:11], scalar1=2.0, scalar2=None, op0=ALU.mult),
                      reads=prm.b, writes=prm.b)
                fw.op(dve, lambda h: h.memset(xp[:, 0:2], 0.0), writes=xp.b)
                fw.op(dve, lambda h: h.memset(xp[:, L + 2:L + 4], 0.0), writes=xp.b)
                w_in_v = od_w_in[j].rearrange("(k p) c -> p k c", p=128)
                def load_w(hb_):
                    wb_ = wbuf[hb_ % 2]
                    fw.dma(pool, wb_[:, :, 0, :], w_in_v[:, :, hb_ * 128:(hb_ + 1) * 128], writes=wb_.b)
                    fw.dma(pool, wb_[:, :, 1, :], w_in_v[:, :, 2048 + hb_ * 128:2048 + (hb_ + 1) * 128], writes=wb_.b)

                def load_gw(hb_):
                    g_ = gw[hb_ % 2]
                    fw.dma(pool, g_[:, :, 0, :], lru_w_a[j, :, hb_].rearrange("d i o -> i d o"), writes=g_.b)
                    fw.dma(pool, g_[:, :, 1, :], lru_w_x[j, :, hb_].rearrange("d i o -> i d o"), writes=g_.b)

                def stage_a(hb):
                    wb = wbuf[hb % 2]
                    xc_ = xc[hb % 2]
                    sg_ = sg[hb % 2]
                    for tt in range(NT):
                        sl = slice(tt * 512, (tt + 1) * 512)
                        ps = fw.bank()
                        for k in range(8):
                            fw.op(pe, lambda h, k=k, ps=ps, sl=sl: h.matmul(ps[:], lhsT=wb[:, k, 0, :], rhs=hn[:, k, sl],
                                                                         start=(k == 0), stop=(k == 7)),
                                  reads=wb.b + [hn.b[tt]], writes=ps.b)
                        fw.op(act, lambda h, ps=ps, tt=tt: h.copy(out=xp[:, 2 + tt * 512:2 + (tt + 1) * 512], in_=ps[:]),
                              reads=ps.b, writes=xp.b)
                        ps2 = fw.bank()
                        for k in range(8):
                            fw.op(pe, lambda h, k=k, ps2=ps2, sl=sl: h.matmul(ps2[:], lhsT=wb[:, k, 1, :], rhs=hn[:, k, sl],
                                                                           start=(k == 0), stop=(k == 7)),
                                  reads=wb.b + [hn.b[tt]], writes=ps2.b)
                        fw.op(act, lambda h, ps2=ps2, sl=sl: h.activation(out=sg_[:, sl], in_=ps2[:], func=AF.Silu),
                              reads=ps2.b, writes=sg_.b)
                    if hb + 1 < 16:
                        load_w(hb + 1)
                    fw.op(act, lambda h: h.activation(out=hf[:], in_=xp[:, 0:L], func=AF.Identity,
                                                      scale=prm[:, hb, 0:1], bias=prm[:, hb, 4:5]),
                          reads=xp.b + prm.b, writes=hf.b)
                    for k in (1, 2):
                        fw.op(dve, lambda h, k=k: h.scalar_tensor_tensor(out=hf[:], in0=xp[:, k:k + L], scalar=prm[:, hb, k:k + 1],
                                                                        in1=hf[:], op0=ALU.mult, op1=ALU.add),
                              reads=xp.b + prm.b + hf.b, writes=hf.b)
                    fw.op(dve, lambda h: h.scalar_tensor_tensor(out=xc_[:], in0=xp[:, 3:3 + L], scalar=prm[:, hb, 3:4],
                                                                in1=hf[:], op0=ALU.mult, op1=ALU.add),
                          reads=xp.b + prm.b + hf.b, writes=xc_.b)

                def stage_b(hb):
                    g = gw[hb % 2]
                    xc_ = xc[hb % 2]
                    sg_ = sg[hb % 2]
                    y = ym[0]
                    for d in range(2):
                        for tt in range(NT):
                            sl = slice(tt * 512, (tt + 1) * 512)
                            psa = fw.bank()
                            fw.op(pe, lambda h, psa=psa, sl=sl: h.matmul(psa[:], lhsT=g[:, d, 0, :], rhs=xc_[:, sl], start=True, stop=True),
                                  reads=g.b + xc_.b, writes=psa.b)
                            psx = fw.bank()
                            fw.op(pe, lambda h, psx=psx, sl=sl: h.matmul(psx[:], lhsT=g[:, d, 1, :], rhs=xc_[:, sl], start=True, stop=True),
                                  reads=g.b + xc_.b, writes=psx.b)
                            fw.op(act, lambda h, psa=psa, sl=sl: h.activation(out=t1[:, sl], in_=psa[:], func=AF.Sigmoid,
                                                                             bias=prm[:, hb, 5 + d:6 + d], scale=1.0),
                                  reads=psa.b + prm.b, writes=t1.b)
                            fw.op(act, lambda h, psx=psx, sl=sl: h.activation(out=uf[:, sl], in_=psx[:], func=AF.Sigmoid,
                                                                             bias=prm[:, hb, 7 + d:8 + d], scale=1.0),
                                  reads=psx.b + prm.b, writes=uf.b)
                        if d == 1 and hb + 1 < 16:
                            load_gw(hb + 1)
                        fw.op(act, lambda h: h.activation(out=af[:], in_=t1[:], func=AF.Exp, scale=prm[:, hb, 9 + d:10 + d]),
                              reads=t1.b + prm.b, writes=af.b)
                        fw.op(act, lambda h: h.activation(out=t1[:], in_=t1[:], func=AF.Exp, scale=prm[:, hb, 11 + d:12 + d]),
                              reads=t1.b + prm.b, writes=t1.b)
                        fw.op(act, lambda h: h.activation(out=t1[:], in_=t1[:], func=AF.Sqrt, scale=-1.0, bias=oneT[:]),
                              reads=t1.b + oneT.b, writes=t1.b)
                        fw.op(dve, lambda h: h.tensor_tensor(out=uf[:], in0=uf[:], in1=xc_[:], op=ALU.mult),
                              reads=uf.b + xc_.b, writes=uf.b)
                        fw.op(dve, lambda h: h.tensor_tensor(out=uf[:], in0=uf[:], in1=t1[:], op=ALU.mult),
                              reads=uf.b + t1.b, writes=uf.b)
                        if d == 0:
                            fw.op(dve, lambda h: h.tensor_tensor_scan(out=hf[:], data0=af[:], data1=uf[:], initial=0.0,
                                                                      op0=ALU.mult, op1=ALU.add),
                                  reads=af.b + uf.b, writes=hf.b)
                        else:
                            fw.op(dve, lambda h: h.tensor_tensor_scan(out=t1[:, ::-1], data0=af[:, ::-1], data1=uf[:, ::-1],
                                                                      initial=0.0, op0=ALU.mult, op1=ALU.add),
                                  reads=af.b + uf.b, writes=t1.b)
                    fw.op(dve, lambda h: h.tensor_tensor(out=hf[:], in0=hf[:], in1=t1[:], op=ALU.add),
                          reads=hf.b + t1.b, writes=hf.b)
                    fw.op(dve, lambda h: h.tensor_tensor(out=y[:], in0=hf[:], in1=sg_[:], op=ALU.mult),
                          reads=hf.b + sg_.b, writes=y.b)
                    fw.dma(sp, mixfm[hb * 128:(hb + 1) * 128, :], y[:], reads=y.b, defer=True)

                load_w(0)
                load_gw(0)
                stage_a(0)
                for hb in range(16):
                    if hb + 1 < 16:
                        stage_a(hb + 1)
                    stage_b(hb)
                    fw.flush()
                fw.barrier()


        def conv_fm(xp, K, prm_ap, out_t, tmp):
            fw.op(act, lambda h: h.activation(out=tmp[:], in_=xp[:, 0:L], func=AF.Identity, scale=prm_ap(0), bias=prm_ap(K)),
                  reads=xp.b + PRM.b, writes=tmp.b)
            for k in range(1, K):
                o = out_t if k == K - 1 else tmp
                fw.op(dve, lambda h, k=k, o=o: h.scalar_tensor_tensor(out=o[:], in0=xp[:, k:k + L], scalar=prm_ap(k), in1=tmp[:],
                                                                     op0=ALU.mult, op1=ALU.add),
                      reads=xp.b + PRM.b + tmp.b, writes=o.b)

        WLOADED = {}
        PEND_T = []

        def proj_fm(w_v, col0, wb, evac, nxt=None):
            if WLOADED.get(id(wb)) != (id(w_v), col0):
                fw.dma(pool, wb[:], w_v[:, :, col0:col0 + 128], writes=wb.b)
            WLOADED.pop(id(wb), None)
            _proj_body(wb, evac)
            while PEND_T:
                transpose_to_tm(*PEND_T.pop(0))
            if nxt is not None:
                ncol, nwb = nxt
                fw.dma(pool, nwb[:], w_v[:, :, ncol:ncol + 128], writes=nwb.b)
                WLOADED[id(nwb)] = (id(w_v), ncol)

        def _proj_body(wb, evac):
            for tt in range(NT):
                ps = fw.bank()
                for k in range(8):
                    fw.op(pe, lambda h, k=k, ps=ps, tt=tt: h.matmul(ps[:], lhsT=wb[:, k, :], rhs=hn[:, k, tt * 512:(tt + 1) * 512],
                                                                  start=(k == 0), stop=(k == 7)),
                          reads=wb.b + [hn.b[tt]], writes=ps.b)
                evac(tt, ps)

        def transpose_to_tm(src, dst_v):
            for q in range(4):
                stg = TSTG[q % 2]
                for t8 in range(2):
                    ps = fw.bank()
                    psb = ps[:].bitcast(BF16)
                    for i in range(4):
                        t = q * 8 + t8 * 4 + i
                        fw.op(pe, lambda h, i=i, t=t, psb=psb: h.transpose(out=psb[:, i * 128:(i + 1) * 128],
                                                                         in_=src[:, t * 128:(t + 1) * 128], identity=identB[:]),
                              reads=src.b + identB.b, writes=ps.b)
                    fw.op(act, lambda h, t8=t8, psb=psb, stg=stg: h.copy(
                        out=stg[:, t8 * 4:(t8 + 1) * 4, :], in_=psb[:, 0:512].rearrange("p (i c) -> p i c", c=128)),
                        reads=ps.b, writes=stg.b)
                with nc.allow_non_contiguous_dma(reason="256B runs"):
                    fw.dma(sp, dst_v[:, q * 8:(q + 1) * 8, :], stg[:], reads=stg.b, defer=True)

        def dft_forward(src_c_v, src_s_v, c0, ncol, consume):
            vc = DFT_SRC[0]
            fw.dma(sp, vc[:, :, 0:ncol], src_c_v[:, :, c0:c0 + ncol], writes=vc.b)
            if src_s_v is not None:
                vs = DFT_SRC[1]
                fw.dma(sp, vs[:, :, 0:ncol], src_s_v[:, :, c0:c0 + ncol], writes=vs.b)
            else:
                vs = vc
            for kb in range(32):
                tb = DFT_TAB[kb % 2]
                fw.dma(sp, tb[:], tabF[kb], writes=tb.b)
                fw.flush()
                tv = tb[:].rearrange("p (cs t k) -> p cs t k", cs=2, t=32)
                for g0 in range(0, ncol, 512):
                    gw_ = min(512, ncol - g0)
                    pc = fw.bank()
                    pss = fw.bank()
                    for t in range(32):
                        fw.op(pe, lambda h, t=t, pc=pc, tv=tv, g0=g0, gw_=gw_: h.matmul(pc[:, 0:gw_], lhsT=tv[:, 0, t, :], rhs=vc[:, t, g0:g0 + gw_],
                                                                                      start=(t == 0), stop=(t == 31)),
                              reads=tb.b + vc.b, writes=pc.b)
                    for t in range(32):
                        fw.op(pe, lambda h, t=t, pss=pss, tv=tv, g0=g0, gw_=gw_: h.matmul(pss[:, 0:gw_], lhsT=tv[:, 1, t, :], rhs=vs[:, t, g0:g0 + gw_],
                                                                                        start=(t == 0), stop=(t == 31)),
                              reads=tb.b + vs.b, writes=pss.b)
                    consume(kb, g0, gw_, pc, pss)

        def sin_big(out_t, ps, n, f4, f8, b4, b8, s4, s8):
            fw.op(act, lambda h: h.activation(out=s4[0:64, 0:n], in_=ps[0:64, 0:n], func=AF.Sin, scale=f4, bias=b4),
                  reads=ps.b + PRM.b, writes=s4.b)
            fw.op(act, lambda h: h.activation(out=s8[0:64, 0:n], in_=ps[0:64, 0:n], func=AF.Sin, scale=f8, bias=b8),
                  reads=ps.b + PRM.b, writes=s8.b)
            fw.op(dve, lambda h: h.tensor_tensor(out=s8[0:64, 0:n], in0=s8[0:64, 0:n], in1=s8[0:64, 0:n], op=ALU.mult), reads=s8.b, writes=s8.b)
            fw.op(dve, lambda h: h.tensor_scalar(out=s8[0:64, 0:n], in0=s8[0:64, 0:n], scalar1=-8.0, scalar2=4.0, op0=ALU.mult, op1=ALU.add),
                  reads=s8.b, writes=s8.b)
            fw.op(dve, lambda h: h.tensor_tensor(out=s8[0:64, 0:n], in0=s8[0:64, 0:n], in1=s4[0:64, 0:n], op=ALU.mult), reads=s8.b + s4.b, writes=s8.b)
            fw.op(dve, lambda h: h.tensor_tensor(out=s4[0:64, 0:n], in0=s4[0:64, 0:n], in1=s4[0:64, 0:n], op=ALU.mult), reads=s4.b, writes=s4.b)
            fw.op(dve, lambda h: h.tensor_scalar(out=s4[0:64, 0:n], in0=s4[0:64, 0:n], scalar1=-2.0, scalar2=1.0, op0=ALU.mult, op1=ALU.add),
                  reads=s4.b, writes=s4.b)
            fw.op(dve, lambda h: h.tensor_tensor(out=out_t, in0=s8[0:64, 0:n], in1=s4[0:64, 0:n], op=ALU.mult), reads=s8.b + s4.b, writes=out_bufs[0])

        out_bufs = [None]
        PRM = sb(es, "PRM", [128, 64, 8])
        TSTG = [None, None]
        DFT_SRC = [None, None]
        DFT_TAB = [None, None]

        def phase_filter(j):
            with ExitStack() as st:
                zf = sb(st, "fz", [33, L])
                w1 = sb(st, "fw1", [33, 64])
                w2 = sb(st, "fw2", [64, 64])
                w3 = sb(st, "fw3", [64, 2048])
                h1 = sb(st, "fh1", [64, L])
                h2 = sb(st, "fh2", [64, L])
                s4 = sb(st, "fs4", [64, 512])
                s8 = sb(st, "fs8", [64, 512])
                win = sb(st, "fwin", [128, L])
                hfw = sb(st, "fhf", [128, L])
                hbw = sb(st, "fhb", [128, L])
                hsum = sb(st, "fhs", [128, L], BF16)
                hdif = sb(st, "fhd", [128, L], BF16)
                nrm = sb(st, "fnrm", [128, 4])
                TSTG[0] = sb(st, "fst0", [128, 8, 128], BF16)
                TSTG[1] = sb(st, "fst1", [128, 8, 128], BF16)
                fw.dma(sp, zf[:], zfeat[:, :], writes=zf.b)
                fw.dma(sp, w1[:], hy_fw1[j], writes=w1.b)
                fw.dma(sp, w2[:], hy_fw2[j], writes=w2.b)
                fw.dma(sp, w3[:], hy_fw3[j], writes=w3.b)
                with nc.allow_non_contiguous_dma(reason="tiny"):
                    fw.dma(sp, PRM[0:64, 0, 0:1], hy_freq[j].rearrange("(p o) -> p o", o=1), writes=PRM.b)
                    fw.dma(sp, PRM[0:64, 0, 1:2], hy_fb1[j].rearrange("(p o) -> p o", o=1), writes=PRM.b)
                    fw.dma(sp, PRM[0:64, 0, 2:3], hy_fb2[j].rearrange("(p o) -> p o", o=1), writes=PRM.b)
                    fw.dma(sp, PRM[:, 2:10, 0], deltas.rearrange("(b p) -> p b", p=128), writes=PRM.b)
                P = lambda a, b: PRM[0:64, a, b:b + 1]
                fw.op(dve, lambda h: h.tensor_scalar(out=P(1, 0), in0=P(0, 0), scalar1=0.25, scalar2=None, op0=ALU.mult), reads=PRM.b, writes=PRM.b)
                fw.op(dve, lambda h: h.tensor_scalar(out=P(1, 1), in0=P(0, 0), scalar1=0.125, scalar2=None, op0=ALU.mult), reads=PRM.b, writes=PRM.b)
                for (bi, o) in ((1, 2), (2, 4)):
                    fw.op(dve, lambda h, bi=bi, o=o: h.tensor_tensor(out=P(1, o), in0=P(0, bi), in1=P(1, 0), op=ALU.mult), reads=PRM.b, writes=PRM.b)
                    fw.op(dve, lambda h, bi=bi, o=o: h.tensor_tensor(out=P(1, o + 1), in0=P(0, bi), in1=P(1, 1), op=ALU.mult), reads=PRM.b, writes=PRM.b)
                fw.op(dve, lambda h: h.tensor_scalar(out=PRM[:, 2:10, 1], in0=PRM[:, 2:10, 0], scalar1=-1.0, scalar2=None, op0=ALU.mult),
                      reads=PRM.b, writes=PRM.b)
                for tt in range(NT):
                    sl = slice(tt * 512, (tt + 1) * 512)
                    ps = fw.bank()
                    fw.op(pe, lambda h, ps=ps, sl=sl: h.matmul(ps[0:64, :], lhsT=w1[:], rhs=zf[:, sl], start=True, stop=True),
                          reads=w1.b + zf.b, writes=ps.b)
                    out_bufs[0] = h1.b
                    sin_big(h1[:, sl], ps, 512, P(1, 0), P(1, 1), P(1, 2), P(1, 3), s4, s8)
                for tt in range(NT):
                    sl = slice(tt * 512, (tt + 1) * 512)
                    ps = fw.bank()
                    fw.op(pe, lambda h, ps=ps, sl=sl: h.matmul(ps[0:64, :], lhsT=w2[:], rhs=h1[:, sl], start=True, stop=True),
                          reads=w2.b + h1.b, writes=ps.b)
                    out_bufs[0] = h2.b
                    sin_big(h2[:, sl], ps, 512, P(1, 0), P(1, 1), P(1, 4), P(1, 5), s4, s8)
                filt_v = [filt_tm[i].rearrange("(t p) c -> p t c", p=128) for i in range(2)]
                for b in range(8):
                    fw.dma(sp, win[:], trow.partition_broadcast(128), writes=win.b)
                    fw.op(act, lambda h, b=b: h.activation(out=win[:], in_=win[:], func=AF.Exp, scale=PRM[:, 2 + b, 1:2]),
                          reads=win.b + PRM.b, writes=win.b)
                    for (half, dst) in ((0, hfw), (1, hbw)):
                        for tt in range(NT):
                            sl = slice(tt * 512, (tt + 1) * 512)
                            ps = fw.bank()
                            c0 = half * 1024 + b * 128
                            fw.op(pe, lambda h, ps=ps, sl=sl, c0=c0: h.matmul(ps[:], lhsT=w3[:, c0:c0 + 128], rhs=h2[:, sl], start=True, stop=True),
                                  reads=w3.b + h2.b, writes=ps.b)
                            fw.op(dve, lambda h, ps=ps, sl=sl, dst=dst: h.tensor_tensor(out=dst[:, sl], in0=ps[:], in1=win[:, sl], op=ALU.mult),
                                  reads=ps.b + win.b, writes=dst.b)
                    fw.op(dve, lambda h: h.memset(hbw[:, 0:1], 0.0), writes=hbw.b)
                    fw.op(dve, lambda h: h.tensor_reduce(out=nrm[:, 0:1], in_=hfw[:], axis=mybir.AxisListType.X, op=ALU.add, apply_absolute_value=True),
                          reads=hfw.b, writes=nrm.b)
                    fw.op(dve, lambda h: h.tensor_reduce(out=nrm[:, 1:2], in_=hbw[:], axis=mybir.AxisListType.X, op=ALU.add, apply_absolute_value=True),
                          reads=hbw.b, writes=nrm.b)
                    fw.op(dve, lambda h: h.tensor_tensor(out=nrm[:, 2:3], in0=nrm[:, 0:1], in1=nrm[:, 1:2], op=ALU.add), reads=nrm.b, writes=nrm.b)
                    fw.op(dve, lambda h: h.reciprocal(out=nrm[:, 3:4], in_=nrm[:, 2:3]), reads=nrm.b, writes=nrm.b)
                    fw.op(dve, lambda h: h.tensor_tensor(out=win[:], in0=hfw[:], in1=hbw[:], op=ALU.add), reads=hfw.b + hbw.b, writes=win.b)
                    fw.op(act, lambda h: h.activation(out=hsum[:], in_=win[:], func=AF.Identity, scale=nrm[:, 3:4]), reads=win.b + nrm.b, writes=hsum.b)
                    fw.op(dve, lambda h: h.tensor_tensor(out=hfw[:], in0=hfw[:], in1=hbw[:], op=ALU.subtract), reads=hfw.b + hbw.b, writes=hfw.b)
                    fw.op(act, lambda h: h.activation(out=hdif[:], in_=hfw[:], func=AF.Identity, scale=nrm[:, 3:4]), reads=hfw.b + nrm.b, writes=hdif.b)
                    transpose_to_tm(hsum, filt_v[0][:, :, b * 128:(b + 1) * 128])
                    transpose_to_tm(hdif, filt_v[1][:, :, b * 128:(b + 1) * 128])
                fw.barrier()
            with ExitStack() as st:
                DFT_SRC[0] = sb(st, "gsrc0", [128, 32, 512], BF16)
                DFT_SRC[1] = sb(st, "gsrc1", [128, 32, 512], BF16)
                DFT_TAB[0] = sb(st, "gtab0", [128, 2 * 32 * 128], BF16)
                DFT_TAB[1] = sb(st, "gtab1", [128, 2 * 32 * 128], BF16)
                brow = sb(st, "gbrow", [128, 1024])
                go = [sb(st, "gout%d" % i, [128, 2, 512]) for i in range(2)]
                fw.dma(sp, brow[:], hy_bias[j].partition_broadcast(128), writes=brow.b)
                filt_v = [filt_tm[i].rearrange("(t p) c -> p t c", p=128) for i in range(2)]
                for half in range(2):
                    c0 = half * 512

                    def consume(kb, g0, gw_, pc, pss, c0=c0):
                        g = go[kb % 2]
                        fw.op(dve, lambda h: h.tensor_tensor(out=g[:, 0, :], in0=pc[:], in1=brow[:, c0:c0 + 512], op=ALU.add),
                              reads=pc.b + brow.b, writes=g.b)
                        fw.op(act, lambda h: h.copy(out=g[:, 1, :], in_=pss[:]), reads=pss.b, writes=g.b)
                        fw.dma(sp, Gsp[j, kb, :, :, c0:c0 + 512], g[:], reads=g.b, defer=True)
                    dft_forward(filt_v[0], filt_v[1], c0, 512, consume)
                fw.barrier()

        def phase_hyena(s, j):
            w_v = ev_w_in[j].rearrange("(k p) c -> p k c", p=128)
            vf_v = vf_tm.rearrange("(t p) c -> p t c", p=128)
            WLOADED.clear()
            with ExitStack() as st:
                wb = [sb(st, "hwb%d" % i, [128, 8, 128], BF16) for i in range(2)]
                xp = [sb(st, "hxp%d" % i, [128, L + 2]) for i in range(2)]
                tmp = sb(st, "htmp", [128, L])
                c0t = sb(st, "hc0", [128, L])
                c1t = sb(st, "hc1", [128, L])
                sg = sb(st, "hsg", [128, L], BF16)
                ob = [sb(st, "hob%d" % i, [128, L], BF16) for i in range(3)]
                TSTG[0] = sb(st, "hst0", [128, 8, 128], BF16)
                TSTG[1] = sb(st, "hst1", [128, 8, 128], BF16)
                for k in range(3):
                    ldvec(PRM[:, 0:24, k], hy_conv_w[j, k], PRM.b)
                ldvec(PRM[:, 0:24, 3], hy_conv_b[j], PRM.b)
                for x in xp:
                    fw.op(dve, lambda h, x=x: h.memset(x[:, 0:1], 0.0), writes=x.b)
                    fw.op(dve, lambda h, x=x: h.memset(x[:, L + 1:L + 2], 0.0), writes=x.b)
                seq = []
                for b in range(8):
                    seq += [b * 128, 3072 + b * 128, (8 + b) * 128, (16 + b) * 128]
                n = 0

                def nxt_():
                    return (seq[n + 1], wb[(n + 1) % 2]) if n + 1 < len(seq) else None

                def conv_block(blk, out_t):
                    nonlocal n
                    x = xp[n % 2]
                    w = wb[n % 2]
                    nx = nxt_()
                    n += 1
                    proj_fm(w_v, blk * 128, w, lambda tt, ps, x=x: fw.op(
                        act, lambda h: h.copy(out=x[:, 1 + tt * 512:1 + (tt + 1) * 512], in_=ps[:]), reads=ps.b, writes=x.b), nx)
                    conv_fm(x, 3, lambda k, blk=blk: PRM[:, blk, k:k + 1], out_t, tmp)

                for b in range(8):
                    conv_block(b, c0t)
                    w = wb[n % 2]
                    nx = nxt_()
                    n += 1
                    proj_fm(w_v, 3072 + b * 128, w, lambda tt, ps: fw.op(
                        act, lambda h: h.activation(out=sg[:, tt * 512:(tt + 1) * 512], in_=ps[:], func=AF.Silu), reads=ps.b, writes=sg.b), nx)
                    o = ob[0]
                    fw.op(dve, lambda h, o=o: h.tensor_tensor(out=o[:], in0=c0t[:], in1=sg[:], op=ALU.mult), reads=c0t.b + sg.b, writes=o.b)
                    fw.dma(sp, xg_fm[b * 128:(b + 1) * 128, :], o[:], reads=o.b, defer=True)
                    conv_block(8 + b, c0t)
                    conv_block(16 + b, c1t)
                    o = ob[1 + b % 2]
                    fw.op(dve, lambda h, o=o: h.tensor_tensor(out=o[:], in0=c0t[:], in1=c1t[:], op=ALU.mult), reads=c0t.b + c1t.b, writes=o.b)
                    PEND_T.append((o, vf_v[:, :, b * 128:(b + 1) * 128]))
                while PEND_T:
                    transpose_to_tm(*PEND_T.pop(0))
                fw.barrier()

        def phase_hyena_dft(s, j):
            vf_v = vf_tm.rearrange("(t p) c -> p t c", p=128)
            with ExitStack() as st:
                DFT_SRC[0] = sb(st, "dsrc0", [128, 32, 1024], BF16)
                DFT_TAB[0] = sb(st, "dtab0", [128, 2 * 32 * 128], BF16)
                DFT_TAB[1] = sb(st, "dtab1", [128, 2 * 32 * 128], BF16)
                Yt = [sb(st, "dY%d" % i, [128, 2, 1024], BF16) for i in range(2)]
                gt = [sb(st, "dg%d" % i, [128, 2, 1024]) for i in range(2)]
                t1 = sb(st, "dt1", [128, 512])
                t2 = sb(st, "dt2", [128, 512])

                def consume(kb, g0, gw_, pc, pss):
                    g = gt[kb % 2]
                    Y = Yt[kb % 2]
                    cs_ = slice(g0, g0 + gw_)
                    if g0 == 0:
                        fw.dma(sp, g[:], Gsp[j, kb], writes=g.b)
                    fw.op(dve, lambda h: h.tensor_tensor(out=t1[:, 0:gw_], in0=pc[:, 0:gw_], in1=g[:, 0, cs_], op=ALU.mult), reads=pc.b + g.b, writes=t1.b)
                    fw.op(dve, lambda h: h.tensor_tensor(out=t2[:, 0:gw_], in0=pss[:, 0:gw_], in1=g[:, 1, cs_], op=ALU.mult), reads=pss.b + g.b, writes=t2.b)
                    fw.op(pool, lambda h: h.tensor_tensor(out=Y[:, 0, cs_], in0=t1[:, 0:gw_], in1=t2[:, 0:gw_], op=ALU.subtract), reads=t1.b + t2.b, writes=Y.b)
                    fw.op(dve, lambda h: h.tensor_tensor(out=t1[:, 0:gw_], in0=pc[:, 0:gw_], in1=g[:, 1, cs_], op=ALU.mult), reads=pc.b + g.b, writes=t1.b)
                    fw.op(dve, lambda h: h.tensor_tensor(out=t2[:, 0:gw_], in0=pss[:, 0:gw_], in1=g[:, 0, cs_], op=ALU.mult), reads=pss.b + g.b, writes=t2.b)
                    fw.op(pool, lambda h: h.tensor_tensor(out=Y[:, 1, cs_], in0=t1[:, 0:gw_], in1=t2[:, 0:gw_], op=ALU.add), reads=t1.b + t2.b, writes=Y.b)
                    if g0 + gw_ == 1024:
                        fw.dma(sp, Ysp[kb], Y[:], reads=Y.b, defer=True)
                dft_forward(vf_v, None, 0, 1024, consume)
                fw.barrier()
            with ExitStack() as st:
                Yh = sb(st, "dYh", [128, 32, 2, 512], BF16)
                ti = [sb(st, "dti%d" % i, [128, 2, 512], BF16) for i in range(3)]
                xg = [sb(st, "dxg%d" % i, [128, 4, 512], BF16) for i in range(2)]
                mo = [sb(st, "dmo%d" % i, [128, 4, 512], BF16) for i in range(2)]
                for half in range(2):
                    c0 = half * 512
                    for cs in range(2):
                        fw.dma(sp, Yh[:, :, cs, :], Ysp[:, :, cs, c0:c0 + 512].rearrange("k p c -> p k c"), writes=Yh.b)
                    for nq in range(8):
                        banks = [fw.bank() for _ in range(4)]
                        xgt = xg[nq % 2]
                        fw.dma(sp, xgt[:], xg_fm[c0:c0 + 512, nq * 512:(nq + 1) * 512].rearrange("(b p) n -> p b n", p=128), writes=xgt.b)
                        for kb in range(32):
                            t = ti[kb % 3]
                            fw.dma(sp, t[:], tabI[nq, kb].rearrange("p (cs n) -> p cs n", cs=2), writes=t.b)
                            if kb == 2:
                                fw.flush()
                            for cb in range(4):
                                for cs in range(2):
                                    fw.op(pe, lambda h, cb=cb, cs=cs, kb=kb, t=t, banks=banks: h.matmul(
                                        banks[cb][:], lhsT=Yh[:, kb, cs, cb * 128:(cb + 1) * 128], rhs=t[:, cs, :],
                                        start=(kb == 0 and cs == 0), stop=(kb == 31 and cs == 1)),
                                        reads=Yh.b + t.b, writes=banks[cb].b)
                        m = mo[nq % 2]
                        for cb in range(4):
                            fw.op(dve, lambda h, cb=cb, m=m, xgt=xgt, banks=banks: h.tensor_tensor(out=m[:, cb, :], in0=banks[cb][:], in1=xgt[:, cb, :], op=ALU.mult),
                                  reads=banks[cb].b + xgt.b, writes=m.b)
                        fw.dma(sp, mixfm[c0:c0 + 512, nq * 512:(nq + 1) * 512].rearrange("(b p) n -> p b n", p=128), m[:], reads=m.b, defer=True)
                fw.barrier()

        def phase_ssd(s, j):
            w_v = ev_w_in[j].rearrange("(k p) c -> p k c", p=128)
            xs_v = xs_tm.rearrange("(t p) c -> p t c", p=128)
            Bt_v = B_tm.rearrange("(t p) c -> p t c", p=128)
            WLOADED.clear()
            with ExitStack() as st0:
                TM = sb(st0, "sTM", [128, 32, 4, 64])
                ecd = sb(st0, "secd", [128, 64, 32])
                with ExitStack() as st:
                    wb = [sb(st, "swb%d" % i, [128, 8, 128], BF16) for i in range(2)]
                    xp = [sb(st, "sxp0", [128, L + 3])] * 2
                    tmp = sb(st, "stmp", [128, L])
                    c0t = sb(st, "sc0", [128, L])
                    ob = [sb(st, "sob%d" % i, [128, L], BF16) for i in range(2)]
                    wz = sb(st, "swz", [128, 8, 1024], BF16)
                    zt = [sb(st, "szt%d" % i, [128, 1024], BF16) for i in range(2)]
                    TSTG[0] = sb(st, "sst0", [128, 8, 128], BF16)
                    TSTG[1] = sb(st, "sst1", [128, 8, 128], BF16)
                    fw.dma(pool, wz[:], w_v[:, :, 4096:5120], writes=wz.b)
                    for k in range(4):
                        ldvec(PRM[:, 0:16, k], ssd_conv_w[j, k], PRM.b)
                    ldvec(PRM[:, 0:16, 4], ssd_conv_b[j], PRM.b)
                    for x in xp[:1]:
                        fw.op(dve, lambda h, x=x: h.memset(x[:, 0:2], 0.0), writes=x.b)
                        fw.op(dve, lambda h, x=x: h.memset(x[:, L + 2:L + 3], 0.0), writes=x.b)
                    for blk in range(16):
                        x = xp[blk % 2]
                        o = ob[blk % 2]
                        proj_fm(w_v, 5120 + blk * 128, wb[blk % 2], lambda tt, ps, x=x: fw.op(
                            act, lambda h: h.copy(out=x[:, 2 + tt * 512:2 + (tt + 1) * 512], in_=ps[:]), reads=ps.b, writes=x.b),
                            (5120 + (blk + 1) * 128, wb[(blk + 1) % 2]) if blk < 15 else None)
                        conv_fm(x, 4, lambda k, blk=blk: PRM[:, blk, k:k + 1], c0t, tmp)
                        fw.op(act, lambda h, o=o: h.activation(out=o[:], in_=c0t[:], func=AF.Silu), reads=c0t.b, writes=o.b)
                        if blk < 8:
                            PEND_T.append((o, xs_v[:, :, blk * 128:(blk + 1) * 128]))
                        elif blk < 12:
                            fw.dma(sp, B_fm[(blk - 8) * 128:(blk - 7) * 128, :], o[:], reads=o.b, defer=True)
                            PEND_T.append((o, Bt_v[:, :, (blk - 8) * 128:(blk - 7) * 128]))
                        else:
                            fw.dma(sp, C_fm[(blk - 12) * 128:(blk - 11) * 128, :], o[:], reads=o.b, defer=True)
                    while PEND_T:
                        transpose_to_tm(*PEND_T.pop(0))
                    for c in range(32):
                        z = zt[c % 2]
                        for half in range(2):
                            ps = fw.bank()
                            for k in range(8):
                                fw.op(pe, lambda h, k=k, ps=ps, c=c, half=half: h.matmul(
                                    ps[:], lhsT=hn[:, k, c * 128:(c + 1) * 128], rhs=wz[:, k, half * 512:(half + 1) * 512],
                                    start=(k == 0), stop=(k == 7)), reads=wz.b + [hn.b[c // 4]], writes=ps.b)
                            fw.op(act, lambda h, ps=ps, z=z, half=half: h.activation(out=z[:, half * 512:(half + 1) * 512], in_=ps[:], func=AF.Silu),
                                  reads=ps.b, writes=z.b)
                        fw.dma(sp, z_tm[c * 128:(c + 1) * 128, :], z[:], reads=z.b, defer=True)
                    fw.barrier()
                with ExitStack() as st:
                    wdt = sb(st, "swdt", [128, 8, 64], BF16)
                    dtf = sb(st, "sdtf", [64, L])
                    laf = sb(st, "slaf", [64, L])
                    Wt = sb(st, "sW", [64, L])
                    mk = sb(st, "smk", [64, L])
                    Tt = sb(st, "sT", [64, 32])
                    pp = sb(st, "spp", [64, 4])
                    fw.op(dve, lambda h: h.memset(wdt[:], 0.0), writes=wdt.b)
                    fw.op(dve, lambda h: h.memset(pp[:], 0.0), writes=pp.b)
                    for d in range(2):
                        fw.dma(pool, wdt[:, :, d * 32:d * 32 + 16], w_v[:, :, 7168 + d * 16:7168 + (d + 1) * 16], writes=wdt.b)
                        with nc.allow_non_contiguous_dma(reason="tiny"):
                            fw.dma(sp, pp[d * 32:d * 32 + 16, 0:1], ssd_dt_bias[j, d * 16:(d + 1) * 16].rearrange("(p o) -> p o", o=1), writes=pp.b)
                            fw.dma(sp, pp[d * 32:d * 32 + 16, 1:2], ssd_A_log[j, d * 16:(d + 1) * 16].rearrange("(p o) -> p o", o=1), writes=pp.b)
                    fw.op(act, lambda h: h.activation(out=pp[:, 2:3], in_=pp[:, 1:2], func=AF.Exp), reads=pp.b, writes=pp.b)
                    fw.op(dve, lambda h: h.tensor_scalar(out=pp[:, 2:3], in0=pp[:, 2:3], scalar1=-1.0, scalar2=None, op0=ALU.mult), reads=pp.b, writes=pp.b)
                    for tt in range(NT):
                        sl = slice(tt * 512, (tt + 1) * 512)
                        ps = fw.bank()
                        for k in range(8):
                            fw.op(pe, lambda h, k=k, ps=ps, sl=sl: h.matmul(ps[0:64, :], lhsT=wdt[:, k, :], rhs=hn[:, k, sl], start=(k == 0), stop=(k == 7)),
                                  reads=wdt.b + [hn.b[tt]], writes=ps.b)
                        fw.op(act, lambda h, ps=ps, sl=sl: h.activation(out=dtf[:, sl], in_=ps[0:64, :], func=AF.Exp, bias=pp[:, 0:1], scale=1.0),
                              reads=ps.b + pp.b, writes=dtf.b)
                    fw.op(act, lambda h: h.activation(out=dtf[:], in_=dtf[:], func=AF.Ln, bias=1.0, scale=1.0), reads=dtf.b, writes=dtf.b)
                    fw.op(dve, lambda h: h.tensor_scalar(out=laf[:], in0=dtf[:], scalar1=pp[:, 2:3], scalar2=None, op0=ALU.mult),
                          reads=dtf.b + pp.b, writes=laf.b)
                    fw.op(dve, lambda h: h.memset(Wt[:], 0.0), writes=Wt.b)
                    fw.dma(sp, mk[:], mrow_d[0].partition_broadcast(64), writes=mk.b)
                    fw.op(dve, lambda h: h.tensor_tensor_scan(out=Wt[0:16, :], data0=mk[0:16, :], data1=laf[0:16, :], initial=0.0,
                                                              op0=ALU.mult, op1=ALU.add), reads=mk.b + laf.b, writes=Wt.b)
                    fw.dma(sp, mk[:], mrow_d[1].partition_broadcast(64), writes=mk.b)
                    fw.op(dve, lambda h: h.tensor_tensor_scan(out=Wt[32:48, ::-1], data0=mk[32:48, ::-1], data1=laf[32:48, ::-1], initial=0.0,
                                                              op0=ALU.mult, op1=ALU.add), reads=mk.b + laf.b, writes=Wt.b)
                    Wv = Wt[:].rearrange("p (c l) -> p c l", l=128)
                    fw.op(dve, lambda h: h.memset(Tt[:], 0.0), writes=Tt.b)
                    fw.op(dve, lambda h: h.tensor_copy(out=Tt[0:16, :], in_=Wv[0:16, :, 127]), reads=Wt.b, writes=Tt.b)
                    fw.op(dve, lambda h: h.tensor_copy(out=Tt[32:48, :], in_=Wv[32:48, :, 0]), reads=Wt.b, writes=Tt.b)
                    fw.op(dve, lambda h: h.tensor_tensor(out=mk[:].rearrange("p (c l) -> p c l", l=128), in0=Tt[:].unsqueeze(2).to_broadcast([64, 32, 128]),
                                                         in1=Wv, op=ALU.subtract), reads=Tt.b + Wt.b, writes=mk.b)
                    fw.op(act, lambda h: h.activation(out=mk[:], in_=mk[:], func=AF.Exp), reads=mk.b, writes=mk.b)
                    fw.op(act, lambda h: h.activation(out=laf[:], in_=Wt[:], func=AF.Exp), reads=Wt.b, writes=laf.b)
                    fw.op(act, lambda h: h.activation(out=Tt[:], in_=Tt[:], func=AF.Exp), reads=Tt.b, writes=Tt.b)
                    fw.dma(sp, Wd[:, :], Wt[:], reads=Wt.b, defer=True)
                    fw.dma(sp, ecd_d.rearrange("(p c) -> p c", c=32), Tt[:], reads=Tt.b, defer=True)
                    for c in range(32):
                        ps = fw.bank()
                        for qi, q in enumerate((dtf, Wt, laf, mk)):
                            fw.op(pe, lambda h, qi=qi, q=q, ps=ps, c=c: h.transpose(out=ps[:, qi * 64:(qi + 1) * 64], in_=q[:, c * 128:(c + 1) * 128],
                                                                                 identity=identF[0:64, 0:64]), reads=q.b + identF.b, writes=ps.b)
                        fw.op(act, lambda h, ps=ps, c=c: h.copy(out=TM[:, c, :, :], in_=ps[:, 0:256].rearrange("p (q d) -> p q d", d=64)),
                              reads=ps.b, writes=TM.b)
                    fw.barrier()
                    fw.dma(sp, ecd[:].rearrange("p d c -> p (d c)"), ecd_d.partition_broadcast(128), writes=ecd.b)
                    fw.barrier()
                with ExitStack() as st:
                    prev = sb(st, "sprev", [128, 1024])
                    prevb = sb(st, "sprevb", [128, 1024], BF16)
                    xs_t = [sb(st, "sxs%d" % i, [128, 1024], BF16) for i in range(2)]
                    Btt = [sb(st, "sBt%d" % i, [128, 512], BF16) for i in range(2)]
                    Bft = [sb(st, "sBf%d" % i, [128, 4, 128], BF16) for i in range(2)]
                    Cft = [sb(st, "sCf%d" % i, [128, 4, 128], BF16) for i in range(2)]
                    wbc = [sb(st, "swbc%d" % i, [128, 16, 128]) for i in range(2)]
                    xsd = sb(st, "sxsd", [128, 1024], BF16)
                    xsdo = sb(st, "sxsdo", [128, 1024], BF16)
                    cbm = sb(st, "scbm", [128, 4, 128], BF16)
                    ddA = sb(st, "sddA", [128, 16, 128])
                    EA = sb(st, "sEA", [128, 16, 128], BF16)
                    MA = sb(st, "sMA", [128, 16, 128], BF16)
                    ysum = sb(st, "sys", [128, 1024])
                    yft = sb(st, "syf", [128, 1024])
                    tmpd = sb(st, "std", [128, 1024])
                    ztt = sb(st, "sztt", [128, 1024], BF16)
                    ngrow = sb(st, "sng", [128, 1024])
                    Drow = sb(st, "sD", [128, 16])
                    ynb = sb(st, "synb", [128, 1024], BF16)
                    mo = sb(st, "smo", [128, 8, 128], BF16)
                    ss = sb(st, "sss", [128, 8])
                    junk = sb(st, "sjunk", [128, 256])
                    eps5 = sb(st, "seps5", [128, 1])
                    mks = [sb(st, "smask%d" % i, [128, 128]) for i in range(2)]
                    fw.dma(sp, ngrow[:], ssd_norm_g[j].partition_broadcast(128), writes=ngrow.b)
                    fw.dma(sp, Drow[:], ssd_D[j].partition_broadcast(128), writes=Drow.b)
                    fw.dma(sp, mks[0][:], mask_d[0], writes=mks[0].b)
                    fw.dma(sp, mks[1][:], mask_d[1], writes=mks[1].b)
                    fw.op(dve, lambda h: h.memset(eps5[:], 1e-5), writes=eps5.b)
                    ecdv = ecd
                    Bf_v = B_fm.rearrange("(g n) l -> n g l", n=128)
                    Cf_v = C_fm.rearrange("(g n) l -> n g l", n=128)
                    mix_o = mixfm[1024:2048, :].rearrange("(b p) l -> p b l", p=128)
                    it = 0
                    for d in range(2):
                        ro = d * 32
                        if d == 1:
                            fw.barrier()
                        fw.op(dve, lambda h: h.memset(prev[:], 0.0), writes=prev.b)
                        fw.op(dve, lambda h: h.memset(prevb[:], 0.0), writes=prevb.b)
                        order = range(32) if d == 0 else range(31, -1, -1)
                        for c in order:
                            xs = xs_t[it % 2]
                            Bt = Btt[it % 2]
                            Bf = Bft[it % 2]
                            Cf = Cft[it % 2]
                            wb_ = wbc[it % 2]
                            it += 1
                            rows = slice(c * 128, (c + 1) * 128)
                            fw.dma(sp, xs[:], xs_tm[rows, :], writes=xs.b)
                            fw.dma(sp, Bt[:], B_tm[rows, :], writes=Bt.b)
                            with nc.allow_non_contiguous_dma(reason="256B runs"):
                                fw.dma(sp, Bf[:], Bf_v[:, :, rows], writes=Bf.b)
                                fw.dma(sp, Cf[:], Cf_v[:, :, rows], writes=Cf.b)
                            fw.dma(sp, wb_[:], Wd[ro:ro + 16, rows].unsqueeze(0).to_broadcast([128, 16, 128]), writes=wb_.b)
                            fw.flush()
                            xs3 = xs[:].rearrange("p (h e) -> p h e", e=64)
                            fw.op(dve, lambda h, xs3=xs3, c=c, ro=ro: h.tensor_tensor(out=xsd[:].rearrange("p (h e) -> p h e", e=64), in0=xs3,
                                                                 in1=TM[:, c, 0, ro:ro + 16].unsqueeze(2).to_broadcast([128, 16, 64]), op=ALU.mult),
                                  reads=xs.b + TM.b, writes=xsd.b)
                            fw.op(dve, lambda h, c=c, ro=ro: h.tensor_tensor(out=xsdo[:].rearrange("p (h e) -> p h e", e=64),
                                                                 in0=xsd[:].rearrange("p (h e) -> p h e", e=64),
                                                                 in1=TM[:, c, 3, ro:ro + 16].unsqueeze(2).to_broadcast([128, 16, 64]), op=ALU.mult),
                                  reads=xsd.b + TM.b, writes=xsdo.b)
                            pcb = fw.bank()
                            for g in range(4):
                                fw.op(pe, lambda h, g=g, pcb=pcb, Bf=Bf, Cf=Cf: h.matmul(pcb[:, g * 128:(g + 1) * 128], lhsT=Bf[:, g, :], rhs=Cf[:, g, :], start=True, stop=True),
                                      reads=Bf.b + Cf.b, writes=pcb.b)
                            fw.op(dve, lambda h, pcb=pcb, d=d: h.tensor_tensor(out=cbm[:], in0=pcb[:].rearrange("p (g l) -> p g l", l=128),
                                                                          in1=mks[d][:].unsqueeze(1).to_broadcast([128, 4, 128]), op=ALU.mult),
                                  reads=pcb.b + mks[d].b, writes=cbm.b)
                            Yd = [fw.bank(), fw.bank()]
                            wsb = TM[:, c, 1, ro:ro + 16].unsqueeze(2).to_broadcast([128, 16, 128])
                            fw.op(dve, lambda h, wb_=wb_, wsb=wsb: h.tensor_tensor(out=ddA[:], in0=wb_[:], in1=wsb, op=ALU.min),
                                  reads=wb_.b + TM.b, writes=ddA.b)
                            fw.op(dve, lambda h, wsb=wsb: h.tensor_tensor(out=ddA[:], in0=ddA[:], in1=wsb, op=ALU.subtract),
                                  reads=ddA.b + TM.b, writes=ddA.b)
                            fw.op(act, lambda h: h.activation(out=EA[:], in_=ddA[:], func=AF.Exp), reads=ddA.b, writes=EA.b)
                            fw.op(dve, lambda h: h.tensor_tensor(out=MA[:].rearrange("p (g r) l -> p g r l", r=4),
                                                                 in0=EA[:].rearrange("p (g r) l -> p g r l", r=4),
                                                                 in1=cbm[:].unsqueeze(2).to_broadcast([128, 4, 4, 128]), op=ALU.mult),
                                  reads=EA.b + cbm.b, writes=MA.b)
                            for hh in range(16):
                                fw.op(pe, lambda h, hh=hh, Yd=Yd: h.matmul(Yd[hh // 8][:, (hh % 8) * 64:(hh % 8 + 1) * 64], lhsT=MA[:, hh, :],
                                                                         rhs=xsd[:, hh * 64:(hh + 1) * 64], start=True, stop=True),
                                      reads=MA.b + xsd.b, writes=Yd[hh // 8].b)
                            Yo = [fw.bank(), fw.bank()]
                            for g in range(4):
                                fw.op(pe, lambda h, g=g, Yo=Yo, Cf=Cf: h.matmul(Yo[g // 2][:, (g % 2) * 256:(g % 2 + 1) * 256], lhsT=Cf[:, g, :],
                                                                  rhs=prevb[:, g * 256:(g + 1) * 256], start=True, stop=True),
                                      reads=Cf.b + prevb.b, writes=Yo[g // 2].b)
                            for half in range(2):
                                fw.op(dve, lambda h, half=half, Yo=Yo, c=c, ro=ro: h.tensor_tensor(
                                    out=ysum[:, half * 512:(half + 1) * 512].rearrange("p (h e) -> p h e", e=64),
                                    in0=Yo[half][:].rearrange("p (h e) -> p h e", e=64),
                                    in1=TM[:, c, 2, ro + half * 8:ro + half * 8 + 8].unsqueeze(2).to_broadcast([128, 8, 64]), op=ALU.mult),
                                    reads=Yo[half].b + TM.b, writes=ysum.b)
                                fw.op(dve, lambda h, half=half, Yd=Yd: h.tensor_tensor(out=ysum[:, half * 512:(half + 1) * 512], in0=Yd[half][:],
                                                                               in1=ysum[:, half * 512:(half + 1) * 512], op=ALU.add),
                                      reads=Yd[half].b + ysum.b, writes=ysum.b)
                            St = [fw.bank(), fw.bank()]
                            for g in range(4):
                                fw.op(pe, lambda h, g=g, St=St, Bt=Bt: h.matmul(St[g // 2][:, (g % 2) * 256:(g % 2 + 1) * 256], lhsT=Bt[:, g * 128:(g + 1) * 128],
                                                                  rhs=xsdo[:, g * 256:(g + 1) * 256], start=True, stop=True),
                                      reads=Bt.b + xsdo.b, writes=St[g // 2].b)
                            fw.op(dve, lambda h, c=c, ro=ro: h.tensor_tensor(out=prev[:].rearrange("p (h e) -> p h e", e=64), in0=prev[:].rearrange("p (h e) -> p h e", e=64),
                                                                 in1=ecdv[:, ro:ro + 16, c].unsqueeze(2).to_broadcast([128, 16, 64]), op=ALU.mult),
                                  reads=prev.b + ecd.b, writes=prev.b)
                            for half in range(2):
                                fw.op(dve, lambda h, half=half, St=St: h.tensor_tensor(out=prev[:, half * 512:(half + 1) * 512], in0=St[half][:],
                                                                               in1=prev[:, half * 512:(half + 1) * 512], op=ALU.add),
                                      reads=St[half].b + prev.b, writes=prev.b)
                            fw.op(act, lambda h: h.copy(out=prevb[:], in_=prev[:]), reads=prev.b, writes=prevb.b)
                            if d == 0:
                                fw.dma(sp, Yf[rows, :], ysum[:], reads=ysum.b, defer=True)
                            else:
                                fw.dma(sp, yft[:], Yf[rows, :], writes=yft.b)
                                fw.dma(sp, ztt[:], z_tm[rows, :], writes=ztt.b)
                                fw.op(pool, lambda h: h.tensor_tensor(out=ysum[:], in0=ysum[:], in1=yft[:], op=ALU.add), reads=ysum.b + yft.b, writes=ysum.b)
                                fw.op(dve, lambda h, xs3=xs3: h.tensor_tensor(out=tmpd[:].rearrange("p (h e) -> p h e", e=64), in0=xs3,
                                                                     in1=Drow[:].unsqueeze(2).to_broadcast([128, 16, 64]), op=ALU.mult),
                                      reads=xs.b + Drow.b, writes=tmpd.b)
                                fw.op(pool, lambda h: h.tensor_tensor(out=ysum[:], in0=ysum[:], in1=tmpd[:], op=ALU.add), reads=ysum.b + tmpd.b, writes=ysum.b)
                                fw.op(pool, lambda h: h.tensor_tensor(out=ysum[:], in0=ysum[:], in1=ztt[:], op=ALU.mult), reads=ysum.b + ztt.b, writes=ysum.b)
                                for g in range(4):
                                    fw.op(act, lambda h, g=g: h.activation(out=junk[:], in_=ysum[:, g * 256:(g + 1) * 256], func=AF.Square,
                                                                           accum_out=ss[:, g:g + 1]), reads=ysum.b, writes=junk.b + ss.b)
                                fw.op(act, lambda h: h.activation(out=ss[:, 4:8], in_=ss[:, 0:4], func=AF.Sqrt, scale=1.0 / 256.0, bias=eps5[:]),
                                      reads=ss.b + eps5.b, writes=ss.b)
                                fw.op(dve, lambda h: h.reciprocal(out=ss[:, 4:8], in_=ss[:, 4:8]), reads=ss.b, writes=ss.b)
                                fw.op(dve, lambda h: h.tensor_tensor(out=ysum[:].rearrange("p (g e) -> p g e", e=256), in0=ysum[:].rearrange("p (g e) -> p g e", e=256),
                                                                     in1=ss[:, 4:8].unsqueeze(2).to_broadcast([128, 4, 256]), op=ALU.mult),
                                      reads=ysum.b + ss.b, writes=ysum.b)
                                fw.op(pool, lambda h: h.tensor_tensor(out=ynb[:], in0=ysum[:], in1=ngrow[:], op=ALU.mult), reads=ysum.b + ngrow.b, writes=ynb.b)
                                for t8 in range(2):
                                    ps = fw.bank()
                                    psb = ps[:].bitcast(BF16)
                                    for i in range(4):
                                        cb = t8 * 4 + i
                                        fw.op(pe, lambda h, i=i, cb=cb, psb=psb: h.transpose(out=psb[:, i * 128:(i + 1) * 128],
                                                                                           in_=ynb[:, cb * 128:(cb + 1) * 128], identity=identB[:]),
                                              reads=ynb.b + identB.b, writes=ps.b)
                                    fw.op(act, lambda h, t8=t8, psb=psb: h.copy(out=mo[:, t8 * 4:(t8 + 1) * 4, :],
                                                                               in_=psb[:, 0:512].rearrange("p (i c) -> p i c", c=128)),
                                          reads=ps.b, writes=mo.b)
                                with nc.allow_non_contiguous_dma(reason="256B runs"):
                                    fw.dma(sp, mix_o[:, :, rows], mo[:], reads=mo.b, defer=True)
                    fw.barrier()

        oneT = sb(es, "oneT", [128, 1])
        fw.op(dve, lambda h: h.memset(oneT[:], 1.0), writes=oneT.b)
        for j in range(2):
            if 2 * j in layers:
                phase_filter(j)

        for s in range(S):
            first = True
            for li in range(4):
                if li not in layers:
                    continue
                if first:
                    phase_norm0(s, li)
                    first = False
                rest = [l for l in layers if l > li]
                nli = rest[0] if rest else 4
                if li % 2 == 1:
                    phase_odd(s, li // 2)
                    phase_out(s, li, od_w_out[li // 2], nli)
                else:
                    phase_hyena(s, li // 2)
                    phase_hyena_dft(s, li // 2)
                    if not DEBUG.get("no_ssd"):
                        phase_ssd(s, li // 2)
                    phase_out(s, li, ev_w_out[li // 2], nli)
    return nc


_CONST_CACHE = {}


def _consts():
    if _CONST_CACHE:
        return _CONST_CACHE
    N = 2 * L
    c = {"identF_d": np.eye(128, dtype=np.float32)}
    n = np.arange(L, dtype=np.float64)[:, None]
    k = np.arange(L, dtype=np.float64)[None, :]
    th = (2.0 * np.pi / N) * ((n * (2 * k + 1)) % (2 * N)) / 2.0
    Cm = np.cos(th).astype(np.float32)
    Sm = np.sin(th).astype(np.float32)
    del th
    bf = ml_dtypes.bfloat16
    tF = np.stack([Cm, Sm], 0).reshape(2, 32, 128, 32, 128)
    c["tabF"] = np.ascontiguousarray(tF.transpose(3, 2, 0, 1, 4)).astype(bf).reshape(32, 128, 2 * 32 * 128)
    tI = (np.stack([Cm, Sm], 0) * np.float32(2.0 / N)).reshape(2, 8, 512, 32, 128)
    c["tabI"] = np.ascontiguousarray(tI.transpose(1, 3, 4, 0, 2)).astype(bf).reshape(8, 32, 128, 2 * 512)
    del Cm, Sm, tF, tI
    pos = np.arange(L, dtype=np.float32)[:, None]
    t = pos / np.float32(L - 1)
    f = np.linspace(1e-4, 15, 16, dtype=np.float32)[None, :]
    ang = f * pos * np.float32(2.0 * math.pi / L)
    z = np.concatenate([t, np.cos(ang), -np.sin(ang)], axis=-1).astype(np.float32)
    c["zfeat"] = np.ascontiguousarray(z.T)
    c["trow"] = np.ascontiguousarray(t[:, 0])
    c["deltas"] = np.abs(np.linspace(math.log(1e-2) / 1.5, math.log(1e-2) / 0.3, 1024, dtype=np.float32)).astype(np.float32)
    l = np.arange(L)
    c["mrow"] = np.stack([(l % 128 != 0), (l % 128 != 127)]).astype(np.float32)
    si = np.arange(128)[:, None]
    li_ = np.arange(128)[None, :]
    c["maskd"] = np.stack([(li_ >= si), (li_ <= si)]).astype(np.float32)
    _CONST_CACHE.update(c)
    return _CONST_CACHE


def kernel(**inputs):
    return _run(inputs, (0, 1, 2, 3))


def _run(inputs, layers):
    inp = {k: np.ascontiguousarray(np.asarray(v)) for k, v in inputs.items()}
    seqs_x = [inp["x_prompt"][i] for i in range(8)] + [inp["x_sample"][i] for i in range(4)]
    seqs_c = [inp["c_prompt"][i] for i in range(8)] + [inp["c_sample"][i] for i in range(4)]
    assign = [(2 * k, 2 * k + 1) for k in range(4)] + [(8 + k, 8 + k) for k in range(4)]
    nc = build_program(layers)
    wnames = ["mod_w", "mod_b", "norm_g", "final_g", "od_w_in", "od_w_out", "lru_conv_w", "lru_conv_b",
              "lru_w_a", "lru_b_a", "lru_w_x", "lru_b_x", "lru_lam", "ev_w_out",
              "ev_w_in", "hy_conv_w", "hy_conv_b", "hy_fw1", "hy_fb1", "hy_fw2", "hy_fb2", "hy_fw3", "hy_freq",
              "hy_bias", "ssd_conv_w", "ssd_conv_b", "ssd_norm_g"]
    consts = _consts()
    in_maps = []
    for (a, b) in assign:
        m = {"xin": np.stack([seqs_x[a], seqs_x[b]]), "cin": np.stack([seqs_c[a], seqs_c[b]])}
        for w in wnames:
            m[w] = inp[w]
        m["ssd_dt_bias"] = inp["ssd_dt_bias"].reshape(2, 32)
        m["ssd_A_log"] = inp["ssd_A_log"].reshape(2, 32)
        m["ssd_D"] = inp["ssd_D"]
        m.update(consts)
        in_maps.append(m)
    res = run_bass_kernel_spmd(nc, in_maps, core_ids=list(range(NCORES)))
    if DEBUG.get("dump"):
        DEBUG["res"] = res.results
    outs = [r["yout"] for r in res.results]
    yp = np.stack([outs[k][i] for k in range(4) for i in range(2)]).astype(np.float32)
    ys = np.stack([outs[4 + k][0] for k in range(4)]).astype(np.float32)
    return (yp, ys)
```
